# Optimizing a Trainium2 kernel written in Bass

```python
import math
import jax, jax.numpy as jnp
from jax import lax
import numpy as np

D_MODEL = 1024
BATCH = 4
SEQ = 8192
DEPTH = 4

MIX_WIDTH = 2 * D_MODEL
RET_WIDTH = MIX_WIDTH // 4
SSD_WIDTH = MIX_WIDTH // 2
FOX_WIDTH = MIX_WIDTH // 4
RET_HEAD_DIM = 128
RET_HEADS = RET_WIDTH // RET_HEAD_DIM
SSD_HEAD_DIM = 64
SSD_HEADS = SSD_WIDTH // SSD_HEAD_DIM
SSD_GROUPS = 2
SSD_STATE = 128
CONV_K = 4
CONV_DIM = SSD_WIDTH + 2 * SSD_GROUPS * SSD_STATE
FOX_HEAD_DIM = 64
FOX_HEADS = FOX_WIDTH // FOX_HEAD_DIM
CHUNK = 128
Q_BLOCK = 128
ROPE_BASE = 10000.0
EPS = 1e-6

IN_SIZES = (RET_WIDTH, RET_WIDTH, RET_WIDTH, RET_WIDTH,
            CONV_DIM, SSD_HEADS, SSD_WIDTH,
            FOX_WIDTH, FOX_WIDTH, FOX_WIDTH, FOX_WIDTH,
            FOX_HEADS)
N_IN = sum(IN_SIZES)
IN_OFFSETS = tuple(int(v) for v in np.cumsum(IN_SIZES)[:-1])

kernel_name = "hymba_style_retention_ssd_fox_trunk"


def _rmsnorm(t, gain=None):
    t32 = t.astype(jnp.float32)
    t32 = t32 * lax.rsqrt(jnp.mean(t32 * t32, axis=-1, keepdims=True) + EPS)
    if gain is not None:
        t32 = t32 * gain.astype(jnp.float32)
    return t32.astype(t.dtype)


def _rotary(t, positions):
    half = t.shape[-1] // 2
    freq = ROPE_BASE ** (-jnp.arange(half, dtype=jnp.float32) / half)
    ang = positions.astype(jnp.float32)[..., None] * freq
    cos = jnp.cos(ang)[:, :, None, :]
    sin = jnp.sin(ang)[:, :, None, :]
    t1 = t[..., :half].astype(jnp.float32)
    t2 = t[..., half:].astype(jnp.float32)
    return jnp.concatenate([t1 * cos - t2 * sin, t1 * sin + t2 * cos], axis=-1).astype(t.dtype)


def _retention(q, k, v, positions):
    b, L, H, dh = q.shape
    n = L // CHUNK
    q = _rotary(q, positions)
    k = _rotary(k, positions) * (dh ** -0.5)
    log_g = jnp.log(1.0 - 2.0 ** (-5.0 - jnp.arange(H, dtype=jnp.float32)))
    idx = jnp.arange(CHUNK, dtype=jnp.float32)
    diff = idx[:, None] - idx[None, :]
    decay_intra = jnp.where(diff >= 0, jnp.exp(log_g[:, None, None] * jnp.maximum(diff, 0.0)), 0.0)
    decay_q = jnp.exp(log_g[:, None] * (idx + 1.0))
    decay_k = jnp.exp(log_g[:, None] * (CHUNK - 1.0 - idx))
    decay_chunk = jnp.exp(log_g * CHUNK)

    def to_chunks(t):
        return t.reshape(b, n, CHUNK, H, dh).transpose(1, 0, 3, 2, 4)

    def step(S, inp):
        qc, kc, vc = inp
        s = jnp.einsum('bhid,bhjd->bhij', qc, kc) * decay_intra
        o = (jnp.einsum('bhij,bhjd->bhid', s, vc)
             + jnp.einsum('bhid,bhde->bhie', qc, S) * decay_q[:, :, None])
        S = S * decay_chunk[:, None, None] + jnp.einsum('bhjd,hj,bhje->bhde', kc, decay_k, vc)
        return S, o

    S0 = jnp.zeros((b, H, dh, dh), jnp.float32)
    _, o = lax.scan(step, S0, (to_chunks(q), to_chunks(k), to_chunks(v)))
    o = o.transpose(1, 0, 3, 2, 4).reshape(b, L, H, dh)
    return _rmsnorm(o).astype(v.dtype)


def _ssd(xs, dt, A, Bm, Cm):
    b, L, H, P = xs.shape
    G, N = Bm.shape[2], Bm.shape[3]
    hpg = H // G
    n = L // CHUNK
    a = (dt * A).astype(jnp.float32)
    xdt = xs * dt[..., None]
    xc_all = xdt.reshape(b, n, CHUNK, G, hpg, P).transpose(1, 0, 2, 3, 4, 5)
    a_all = a.reshape(b, n, CHUNK, G, hpg).transpose(1, 0, 3, 4, 2)
    b_all = Bm.reshape(b, n, CHUNK, G, N).transpose(1, 0, 2, 3, 4)
    c_all = Cm.reshape(b, n, CHUNK, G, N).transpose(1, 0, 2, 3, 4)
    causal = jnp.tril(jnp.ones((CHUNK, CHUNK), dtype=bool))

    def step(S, inp):
        xc, ac, bc, cc = inp
        acum = jnp.cumsum(ac, axis=-1)
        seg = acum[..., :, None] - acum[..., None, :]
        Lm = jnp.exp(jnp.where(causal, seg, -jnp.inf))
        cb = jnp.einsum('bign,bjgn->bgij', cc, bc)
        y = jnp.einsum('bgij,bghij,bjghp->bighp', cb, Lm, xc)
        y = y + jnp.einsum('bign,bghpn,bghi->bighp', cc, S, jnp.exp(acum))
        last = acum[..., -1:]
        S = (S * jnp.exp(last)[..., None]
             + jnp.einsum('bjgn,bghj,bjghp->bghpn', bc, jnp.exp(last - acum), xc))
        return S, y

    S0 = jnp.zeros((b, G, hpg, P, N), jnp.float32)
    _, y = lax.scan(step, S0, (xc_all, a_all, b_all, c_all))
    return y.transpose(1, 0, 2, 3, 4, 5).reshape(b, L, H, P)


def _causal_dwconv(u, w, bias):
    out = lax.conv_general_dilated(
        u, w[:, None, :].astype(u.dtype), window_strides=(1,), padding=[(CONV_K - 1, 0)],
        dimension_numbers=('NWC', 'WIO', 'NWC'), feature_group_count=u.shape[-1])
    return out + bias


def _forgetting_attention(q, k, v, log_f):
    b, L, H, dh = q.shape
    nb = L // Q_BLOCK
    Ft = jnp.cumsum(log_f.astype(jnp.float32), axis=1).transpose(0, 2, 1)
    q_blocks = q.reshape(b, nb, Q_BLOCK, H, dh).transpose(1, 0, 2, 3, 4)
    f_blocks = Ft.reshape(b, H, nb, Q_BLOCK).transpose(2, 0, 1, 3)
    kpos = jnp.arange(L)
    scale = dh ** -0.5

    def block(inp):
        qi, fi, i = inp
        s = jnp.einsum('bqhd,bkhd->bhqk', qi, k).astype(jnp.float32) * scale
        s = s + fi[..., None] - Ft[:, :, None, :]
        qpos = i * Q_BLOCK + jnp.arange(Q_BLOCK)
        s = jnp.where(qpos[:, None] >= kpos[None, :], s, -jnp.inf)
        p = jax.nn.softmax(s, axis=-1)
        return jnp.einsum('bhqk,bkhd->bqhd', p.astype(v.dtype), v)

    o = lax.map(block, (q_blocks, f_blocks, jnp.arange(nb)))
    return o.transpose(1, 0, 2, 3, 4).reshape(b, L, H, dh)


def _layer(x, c, positions, norm_g, w_ada, b_ada, w_in, conv_w, conv_b,
           dt_bias, a_log, d_skip, ssd_norm_g, b_forget, w_out):
    b, L, _ = x.shape
    mod = jax.nn.silu(c) @ w_ada + b_ada
    shift, scale, gate = jnp.split(mod, 3, axis=-1)
    h = _rmsnorm(x, norm_g) * (1.0 + scale[:, None, :]) + shift[:, None, :]

    proj = h @ w_in
    (rq, rk, rv, rg, xbc, dt_raw, z, fq, fk, fv, fg, f_raw) = jnp.split(proj, IN_OFFSETS, axis=-1)

    ret = _retention(rq.reshape(b, L, RET_HEADS, RET_HEAD_DIM),
                     rk.reshape(b, L, RET_HEADS, RET_HEAD_DIM),
                     rv.reshape(b, L, RET_HEADS, RET_HEAD_DIM), positions)
    ret = ret.reshape(b, L, RET_WIDTH) * jax.nn.silu(rg)

    xbc = jax.nn.silu(_causal_dwconv(xbc, conv_w, conv_b))
    xs, Bm, Cm = jnp.split(xbc, (SSD_WIDTH, SSD_WIDTH + SSD_GROUPS * SSD_STATE), axis=-1)
    xs = xs.reshape(b, L, SSD_HEADS, SSD_HEAD_DIM)
    Bm = Bm.reshape(b, L, SSD_GROUPS, SSD_STATE)
    Cm = Cm.reshape(b, L, SSD_GROUPS, SSD_STATE)
    dt = jax.nn.softplus(dt_raw + dt_bias)
    A = -jnp.exp(a_log.astype(jnp.float32))
    y = _ssd(xs, dt, A, Bm, Cm) + d_skip[:, None] * xs
    ssd = _rmsnorm(y.reshape(b, L, SSD_WIDTH).astype(x.dtype) * jax.nn.silu(z), ssd_norm_g)

    log_f = jax.nn.log_sigmoid((f_raw + b_forget).astype(jnp.float32))
    fox = _forgetting_attention(fq.reshape(b, L, FOX_HEADS, FOX_HEAD_DIM),
                                fk.reshape(b, L, FOX_HEADS, FOX_HEAD_DIM),
                                fv.reshape(b, L, FOX_HEADS, FOX_HEAD_DIM), log_f)
    fox = fox.reshape(b, L, FOX_WIDTH) * jax.nn.silu(fg)

    mixed = jnp.concatenate([ret.astype(x.dtype), ssd.astype(x.dtype), fox.astype(x.dtype)], axis=-1)
    out = mixed @ w_out
    return x + gate[:, None, :] * out


def setup_inputs(seed: int = 0) -> dict:
    key = jax.random.key(seed)
    ks = jax.random.split(key, 16)
    f32 = jnp.float32
    x = jax.random.normal(ks[0], (BATCH, SEQ, D_MODEL), f32)
    c = jax.random.normal(ks[1], (BATCH, D_MODEL), f32)
    offsets = jax.random.randint(ks[2], (BATCH, 1), 0, 4096, dtype=jnp.int32)
    positions = (jnp.arange(SEQ, dtype=jnp.int32)[None, :] + offsets).astype(jnp.int32)
    norm_g = 1.0 + 0.02 * jax.random.normal(ks[3], (DEPTH, D_MODEL), f32)
    w_ada = 0.5 * D_MODEL ** -0.5 * jax.random.normal(ks[4], (DEPTH, D_MODEL, 3 * D_MODEL), f32)
    b_ada = 0.02 * jax.random.normal(ks[5], (DEPTH, 3 * D_MODEL), f32)
    w_in = D_MODEL ** -0.5 * jax.random.normal(ks[6], (DEPTH, D_MODEL, N_IN), f32)
    conv_w = CONV_K ** -0.5 * jax.random.normal(ks[7], (DEPTH, CONV_K, CONV_DIM), f32)
    conv_b = 0.02 * jax.random.normal(ks[8], (DEPTH, CONV_DIM), f32)
    dt0 = jnp.exp(jax.random.uniform(ks[9], (DEPTH, SSD_HEADS), f32, math.log(1e-3), math.log(1e-1)))
    dt_bias = dt0 + jnp.log(-jnp.expm1(-dt0))
    a_log = jnp.log(jax.random.uniform(ks[10], (DEPTH, SSD_HEADS), f32, 1.0, 16.0))
    d_skip = 1.0 + 0.1 * jax.random.normal(ks[11], (DEPTH, SSD_HEADS), f32)
    ssd_norm_g = 1.0 + 0.02 * jax.random.normal(ks[12], (DEPTH, SSD_WIDTH), f32)
    b_forget = jax.random.uniform(ks[13], (DEPTH, FOX_HEADS), f32, 2.0, 5.0)
    w_out = MIX_WIDTH ** -0.5 * jax.random.normal(ks[14], (DEPTH, MIX_WIDTH, D_MODEL), f32)
    final_g = 1.0 + 0.02 * jax.random.normal(ks[15], (D_MODEL,), f32)
    return {"x": x, "c": c, "positions": positions, "norm_g": norm_g, "w_ada": w_ada,
            "b_ada": b_ada, "w_in": w_in, "conv_w": conv_w, "conv_b": conv_b,
            "dt_bias": dt_bias, "a_log": a_log, "d_skip": d_skip, "ssd_norm_g": ssd_norm_g,
            "b_forget": b_forget, "w_out": w_out, "final_g": final_g}


def reference(x, c, positions, norm_g, w_ada, b_ada, w_in, conv_w, conv_b,
              dt_bias, a_log, d_skip, ssd_norm_g, b_forget, w_out, final_g):
    for layer in range(DEPTH):
        x = _layer(x, c, positions, norm_g[layer], w_ada[layer], b_ada[layer], w_in[layer],
                   conv_w[layer], conv_b[layer], dt_bias[layer], a_log[layer], d_skip[layer],
                   ssd_norm_g[layer], b_forget[layer], w_out[layer])
    return _rmsnorm(x, final_g)
```

```python
import contextlib
import math
import numpy as np
import ml_dtypes
import concourse.bass as bass
import concourse.mybir as mybir
from concourse.bass_utils import run_bass_kernel_spmd

F32 = mybir.dt.float32
BF16 = mybir.dt.bfloat16
I32 = mybir.dt.int32
U8 = mybir.dt.uint8
AF = mybir.ActivationFunctionType
ALU = mybir.AluOpType

D = 1024
NIN = 6680
EPS = 1e-6
EPOCH = 30000
SEQ = 8192
DEPTH = 4


class T:
    __slots__ = ("name", "lw", "rd", "rd_dma")

    def __init__(self, name=""):
        self.name = name
        self.lw = None
        self.rd = {}
        self.rd_dma = []


def TL(n, name=""):
    return [T("%s%d" % (name, i)) for i in range(n)]


class Prog:
    QUEUES = ("sp", "pool", "act")
    NDMASEM = 8

    def __init__(self, nc):
        self.nc = nc
        self.streams = {e: [] for e in ("pe", "dve", "act", "pool", "sp")}
        self.cnt = {}
        self.known = {e: {} for e in self.streams}
        self.semkeys = []
        self.semset = set()
        self.dma_rr = {q: 0 for q in self.QUEUES}
        self.dma_last = {}
        self.last_tok = {}
        self.nops = 0

    def _newtok(self, cname, step):
        n = self.cnt.get(cname, 0)
        self.cnt[cname] = n + 1
        key = (cname, n // EPOCH)
        if key not in self.semset:
            self.semset.add(key)
            self.semkeys.append(key)
        tok = (key, (n % EPOCH + 1) * step)
        self.last_tok[cname] = tok
        return tok

    def _collect(self, eng, reads, writes, extra=()):
        waits = {}
        kn = self.known[eng]

        def need(tok):
            if tok is None:
                return
            key, val = tok
            if eng == "pe" and key[0] == "pe":
                return
            if kn.get(key, 0) >= val:
                return
            if waits.get(key, 0) < val:
                waits[key] = val

        for t in reads:
            need(t.lw)
        for t in writes:
            need(t.lw)
            for r in t.rd.values():
                need(r)
            for r in t.rd_dma:
                need(r)
        for tok in extra:
            need(tok)
        for k, v in waits.items():
            kn[k] = v
        return list(waits.items())

    def op(self, eng, fn, reads=(), writes=()):
        waits = self._collect(eng, reads, writes)
        tok = self._newtok(eng, 1)
        self.streams[eng].append((waits, fn, tok))
        for t in reads:
            t.rd[eng] = tok
        for t in writes:
            t.lw = tok
            t.rd = {}
            t.rd_dma = []
        self.nops += 1
        return tok

    def dma(self, q, out, in_, reads=(), writes=()):
        s = self.dma_rr[q]
        self.dma_rr[q] = (s + 1) % self.NDMASEM
        prev = self.dma_last.get((q, s))
        waits = self._collect(q, reads, writes, extra=(prev,) if prev else ())
        tok = self._newtok(("dma", q, s), 16)
        self.dma_last[(q, s)] = tok

        def fn(e, out=out, in_=in_):
            return e.dma_start(out=out, in_=in_)

        self.streams[q].append((waits, fn, tok))
        for t in reads:
            t.rd_dma.append(tok)
        for t in writes:
            t.lw = tok
            t.rd = {}
            t.rd_dma = []
        self.nops += 1
        return tok

    def barrier(self):
        toks = list(self.last_tok.values())
        for eng in self.streams:
            waits = self._collect(eng, (), (), extra=toks)
            if waits:
                self.streams[eng].append((waits, None, None))

    def emit(self):
        nc = self.nc
        with contextlib.ExitStack() as es:
            sems = {}
            for i, key in enumerate(self.semkeys):
                sems[key] = es.enter_context(nc.semaphore("s%d" % i))
            block = es.enter_context(nc.Block())

            def run(eng_name):
                def body(e):
                    for waits, fn, tok in self.streams[eng_name]:
                        for key, val in waits:
                            e.wait_ge(sems[key], val)
                        if fn is not None:
                            ins = fn(e)
                            step = 16 if isinstance(tok[0][0], tuple) else 1
                            ins.then_inc(sems[tok[0]], step)
                return body

            block.tensor(run("pe"))
            block.vector(run("dve"))
            block.scalar(run("act"))
            block.gpsimd(run("pool"))
            block.sync(run("sp"))


class Arena:
    def __init__(self, ap_u8, nbytes):
        self.a = ap_u8
        self.n = nbytes
        self.off = 0
        self.peak = 0

    def alloc(self, dims, dt, parts=128, p0=0):
        esz = {F32: 4, BF16: 2, I32: 4}[dt]
        tot = esz
        for d_ in dims:
            tot *= d_
        tot_al = (tot + 63) // 64 * 64
        if self.off + tot_al > self.n:
            raise RuntimeError("SBUF arena overflow: need %d at %d of %d" % (tot_al, self.off, self.n))
        v = self.a[p0:p0 + parts, self.off:self.off + tot].bitcast(dt)
        self.off += tot_al
        self.peak = max(self.peak, self.off)
        if len(dims) == 2:
            v = v.rearrange("p (a b) -> p a b", a=dims[0])
        elif len(dims) == 3:
            v = v.rearrange("p (a b c) -> p a b c", a=dims[0], b=dims[1])
        return v

    def mark(self):
        return self.off

    def reset(self, m):
        self.off = m


def bc1(ap, n):
    return ap.unsqueeze(2).to_broadcast([ap.shape[0], ap.shape[1], n])


def bc0(ap, n):
    return ap.unsqueeze(1).to_broadcast([ap.shape[0], n, ap.shape[1]])


C_RQ, C_RK, C_RV, C_RG = 0, 512, 1024, 1536
C_XBC, C_DT, C_Z = 2048, 3584, 3600
C_FQ, C_FK, C_FV, C_FG, C_FR = 4624, 5136, 5648, 6160, 6672
WG_COLS = [0, 512, 1024, 1536, 2048, 2560, 3072, 3600, 4112, 4624, 5136, 5648, 6160]


def host_consts():
    c = {}
    idx = np.arange(128)
    c["ident_bf"] = np.eye(128, dtype=np.float32).astype(ml_dtypes.bfloat16)
    c["tri_incl"] = (idx[:, None] <= idx[None, :]).astype(np.float32)
    c["gstrict"] = (idx[:, None] > idx[None, :]).astype(np.float32)
    c["ones_f"] = np.ones((128, 128), np.float32)
    m01 = (idx[None, :] >= idx[:, None]).astype(np.float32)
    c["mask01_f"] = m01
    c["mask01_bf"] = m01.astype(ml_dtypes.bfloat16)
    H = 4
    log_g = np.log(1.0 - 2.0 ** (-5.0 - np.arange(H, dtype=np.float64)))
    diff = idx[None, :] - idx[:, None]
    dintraT = np.where(diff[None] >= 0, np.exp(log_g[:, None, None] * np.maximum(diff[None], 0)), 0.0)
    c["dintraT"] = np.ascontiguousarray(dintraT.transpose(1, 0, 2)).astype(np.float32)
    dq = np.exp(log_g[:, None] * (idx[None, :] + 1.0))
    c["dq"] = np.ascontiguousarray(np.broadcast_to(dq[None], (128, H, 128))).astype(np.float32)
    dk = np.exp(log_g[:, None] * (127.0 - idx[None, :]))
    c["dk"] = np.ascontiguousarray(dk.T).astype(np.float32)
    c["_dchunk"] = [float(np.exp(log_g[h] * 128.0)) for h in range(H)]
    freq = (10000.0 ** (-np.arange(64, dtype=np.float32) / np.float32(64))).astype(np.float32)
    c["freq_bc"] = np.ascontiguousarray(np.broadcast_to(freq[None], (128, 64))).astype(np.float32)
    return c


CONST_SPECS = [("ident_bf", [128, 128], BF16), ("tri_incl", [128, 128], F32), ("gstrict", [128, 128], F32),
               ("ones_f", [128, 128], F32), ("mask01_f", [128, 128], F32), ("mask01_bf", [128, 128], BF16),
               ("dintraT", [128, 4, 128], F32), ("dq", [128, 4, 128], F32), ("dk", [128, 4], F32),
               ("freq_bc", [128, 64], F32)]


class _Stop(Exception):
    pass


def build_nc(L=SEQ, depth=DEPTH, debug=False, stop=None):
    NT = L // 128
    NST = L // 512
    nc = bass.Bass("TRN2", target_bir_lowering=False)
    dchunk = host_consts()["_dchunk"]

    def din(name, shape, dt=F32):
        return nc.dram_tensor(name, shape, dt, kind="ExternalInput").ap()

    dbgkind = "ExternalOutput" if debug else "Internal"

    def dscr(name, shape, dt, dbg=False):
        return nc.dram_tensor(name, shape, dt, kind=(dbgkind if dbg else "Internal")).ap()

    x_in = din("x", [L, D])
    c_t = din("c_t", [128, 8])
    pos_t = din("pos_t", [128, NT], I32)
    norm_g = din("norm_g", [depth, D])
    w_ada = din("w_ada", [depth, D, 3 * D])
    b_ada = din("b_ada", [depth, 3 * D])
    w_in = din("w_in", [depth, D, NIN])
    cw_t = din("cw_t", [depth, 128, 12, 4])
    cb_t = din("cb_t", [depth, 128, 12])
    dt_bias = din("dt_bias", [depth, 16])
    a_log = din("a_log", [depth, 16])
    d_skip = din("d_skip", [depth, 16])
    ssd_g = din("ssd_norm_g", [depth, D])
    b_forget = din("b_forget", [depth, 8])
    w_out = din("w_out", [depth, 2 * D, D])
    final_g = din("final_g", [1, D])
    cin = {n: din(n, s, dt) for (n, s, dt) in CONST_SPECS}
    y_out = nc.dram_tensor("y", [L, D], F32, kind="ExternalOutput").ap()

    wb_in = dscr("wb_in", [depth, 13, 128, 8, 512], BF16)
    wb_sm = dscr("wb_sm", [depth, 128, 8, 24], BF16)
    wb_out = dscr("wb_out", [depth, 128, 12, D], BF16)
    wb_outF = dscr("wb_outF", [depth, 64, 8, D], BF16)
    cs_tab = dscr("cs_tab", [L, 128], F32, dbg=True)
    modtab = dscr("modtab", [depth, 128, 3 * D], F32, dbg=True)
    mixA = dscr("mixA", [L, 1536], BF16, dbg=True)
    QTs = dscr("QTs", [8, 64, L], BF16, dbg=True)
    KTs = dscr("KTs", [8, 64, L], BF16, dbg=True)
    Vs = dscr("Vs", [8, 128, NT, 72], BF16, dbg=True)
    GTs = dscr("GTs", [8, 64, L], BF16, dbg=True)
    FQs = dscr("FQs", [2, 8, L], BF16, dbg=True)
    xs_buf = dscr("xs_buf", [L, D], F32, dbg=True)
    dbgF = dscr("dbgF", [128, NT, 8], F32, dbg=True) if debug else None

    P = Prog(nc)
    out_toks = []

    with contextlib.ExitStack() as es:
        ARENA_BYTES = 207 * 1024
        arena_t = es.enter_context(nc.sbuf_tensor("arena", [128, ARENA_BYTES], U8))
        A = Arena(arena_t, ARENA_BYTES)
        banks = [es.enter_context(nc.psum_tensor("bank%d" % i, [128, 512], F32)) for i in range(8)]
        bT = TL(8, "bank")

        def bankf(i):
            return banks[i][:]

        def bankb(i):
            return banks[i][:].bitcast(BF16)

        def mm(out, lhsT, rhs, start, stop, R, W):
            P.op("pe", lambda e: e.matmul(out, lhsT, rhs, start=start, stop=stop, skip_group_check=True),
                 reads=R, writes=W)

        def tr(out, in_, R, W):
            P.op("pe", lambda e: e.transpose(out, in_, ident), reads=list(R) + [tIdent], writes=W)

        def act(out, in_, func, R, W, bias=None, scale=None, accum=None):
            kw = {}
            if bias is not None:
                kw["bias"] = bias
            if scale is not None:
                kw["scale"] = scale
            if accum is not None:
                kw["accum_out"] = accum
            P.op("act", lambda e: e.activation(out, in_, func, **kw), reads=R, writes=W)

        def tt(eng, out, in0, in1, op, R, W):
            P.op(eng, lambda e: e.tensor_tensor(out, in0, in1, op=op), reads=R, writes=W)

        def ts(eng, out, in0, s1, s2, op0, op1, R, W):
            if s2 is None:
                P.op(eng, lambda e: e.tensor_scalar(out, in0, s1, None, op0=op0), reads=R, writes=W)
            else:
                P.op(eng, lambda e: e.tensor_scalar(out, in0, s1, s2, op0=op0, op1=op1), reads=R, writes=W)

        def stt(eng, out, in0, scalar, in1, op0, op1, R, W):
            P.op(eng, lambda e: e.scalar_tensor_tensor(out, in0, scalar, in1, op0=op0, op1=op1), reads=R, writes=W)

        def cp(eng, out, in_, R, W):
            if eng == "act":
                P.op("act", lambda e: e.copy(out, in_), reads=R, writes=W)
            else:
                P.op(eng, lambda e: e.tensor_copy(out, in_), reads=R, writes=W)

        def memset(eng, out, val, W):
            P.op(eng, lambda e: e.memset(out, val), writes=W)

        def rsqrt_mean(out, ss, n, R, W):
            act(out, ss, AF.Ln, R, W, bias=epsb[0:out.shape[0], :], scale=1.0 / n)
            act(out, out, AF.Exp, W, W, scale=-0.5)

        csb = {}
        tConst = T("consts")
        for (n, s, dt) in CONST_SPECS:
            parts = s[0]
            v = A.alloc(s[1:], dt, parts=parts)
            csb[n] = v
            P.dma("sp", v, cin[n], writes=[tConst])
        ident = csb["ident_bf"]
        tIdent = tConst
        tri = csb["tri_incl"]
        gst = csb["gstrict"]
        ones_f = csb["ones_f"]
        epsb = A.alloc([1], F32)
        memset("pool", epsb, EPS, [tConst])

        F_all = A.alloc([NT, 8], F32)
        tF = T("F_all")
        carry_hist = A.alloc([NST, 8], F32)
        tCH = T("carry_hist")
        modv = A.alloc([3 * D], F32)
        tMod = T("modv")
        ssdg_bc = A.alloc([D], F32)
        dskip_bc = A.alloc([16], F32)
        dtb_bc = A.alloc([16], F32)
        A_bc = A.alloc([16], F32)
        bf_bc = A.alloc([8], F32)
        cw_sb = A.alloc([12, 4], F32)
        cb_sb = A.alloc([12], F32)
        tLay = T("layer_consts")
        tAl = T("a_log")
        base_mark = A.mark()

        m0 = A.mark()
        stg_f = [A.alloc([NIN], F32) for _ in range(2)]
        stg_b = [A.alloc([NIN], BF16) for _ in range(2)]
        tSf = TL(2, "stgf")
        tSb = TL(2, "stgb")
        tSb3 = [TL(3, "stgb3_%d" % i) for i in range(2)]
        it = 0
        conv_engs = ["dve", "act", "pool"]
        for l in range(depth):
            for r in range(8):
                s = it % 2
                P.dma("sp", stg_f[s], w_in[l, r * 128:(r + 1) * 128, :], writes=[tSf[s]])
                thirds = [(0, 2048), (2048, 4624), (4624, NIN)]
                for ei, (c0, c1) in enumerate(thirds):
                    cp(conv_engs[ei], stg_b[s][:, c0:c1], stg_f[s][:, c0:c1], [tSf[s]], [tSb3[s][ei]])
                rds = tSb3[s]
                for g, c0 in enumerate(WG_COLS):
                    P.dma("pool", wb_in[l, g, :, r, :], stg_b[s][:, c0:c0 + 512], reads=rds)
                P.dma("pool", wb_sm[l, :, r, 0:16], stg_b[s][:, C_DT:C_DT + 16], reads=rds)
                P.dma("pool", wb_sm[l, :, r, 16:24], stg_b[s][:, C_FR:C_FR + 8], reads=rds)
                it += 1
            for r in range(16):
                s = it % 2
                P.dma("sp", stg_f[s][:, 0:D], w_out[l, r * 128:(r + 1) * 128, :], writes=[tSf[s]])
                cp(conv_engs[it % 3], stg_b[s][:, 0:D], stg_f[s][:, 0:D], [tSf[s]], [tSb[s]] + tSb3[s])
                if r < 12:
                    P.dma("pool", wb_out[l, :, r, :], stg_b[s][:, 0:D], reads=[tSb[s]])
                else:
                    i2 = r - 12
                    P.dma("pool", wb_outF[l, :, 2 * i2, :], stg_b[s][0:64, 0:D], reads=[tSb[s]])
                    P.dma("pool", wb_outF[l, :, 2 * i2 + 1, :], stg_b[s][64:128, 0:D], reads=[tSb[s]])
                it += 1
        A.reset(m0)
        P.barrier()
        def _phases():
            if stop == "wconv":
                raise _Stop()
            m0 = A.mark()
            pos_i = A.alloc([NT], I32)
            pos_f = A.alloc([NT], F32)
            tPos = T("pos")
            P.dma("sp", pos_i, pos_t, writes=[tPos])
            cp("dve", pos_f, pos_i, [tPos], [tPos])
            GT = 8
            angs = [A.alloc([GT, 64], F32) for _ in range(2)]
            kf = A.alloc([GT, 64], F32)
            ki = A.alloc([GT, 64], I32)
            cstile = [A.alloc([GT, 128], F32) for _ in range(2)]
            tAng = T("ang")
            tCs = TL(2, "cstile")
            C1 = 6.28125
            C2 = 2.0 * math.pi - C1
            for gi, t0 in enumerate(range(0, NT, GT)):
                n = min(GT, NT - t0)
                s = gi % 2
                for which in range(2):
                    ang = angs[which]
                    tt("dve", ang[:, 0:n, :], bc0(csb["freq_bc"], n), bc1(pos_f[:, t0:t0 + n], 64), ALU.mult, [tPos, tConst], [tAng])
                    if which == 0:
                        ts("dve", ang[:, 0:n, :], ang[:, 0:n, :], math.pi / 2, None, ALU.add, None, [tAng], [tAng])
                    ts("dve", kf[:, 0:n, :], ang[:, 0:n, :], 1.0 / (2 * math.pi), None, ALU.mult, None, [tAng], [tAng])
                    cp("dve", ki[:, 0:n, :], kf[:, 0:n, :], [tAng], [tAng])
                    cp("dve", kf[:, 0:n, :], ki[:, 0:n, :], [tAng], [tAng])
                    stt("dve", ang[:, 0:n, :], kf[:, 0:n, :], -C1, ang[:, 0:n, :], ALU.mult, ALU.add, [tAng], [tAng])
                    stt("dve", ang[:, 0:n, :], kf[:, 0:n, :], -C2, ang[:, 0:n, :], ALU.mult, ALU.add, [tAng], [tAng])
                    ts("dve", ang[:, 0:n, :], ang[:, 0:n, :], 3.1415925, -3.1415925, ALU.min, ALU.max, [tAng], [tAng])
                    act(cstile[s][:, 0:n, which * 64:(which + 1) * 64], ang[:, 0:n, :], AF.Sin, [tAng], [tCs[s]])
                P.dma("pool", cs_tab.rearrange("(t p) c -> p t c", p=128)[:, t0:t0 + n, :], cstile[s][:, 0:n, :], reads=[tCs[s]])
            A.reset(m0)

            P.barrier()
            if stop == "cs":
                raise _Stop()
            m0 = A.mark()
            c_sb = A.alloc([8], F32)
            csil = A.alloc([8], F32)
            crep = A.alloc([8, 128], F32)
            tC = T("c")
            P.dma("sp", c_sb, c_t, writes=[tC])
            act(csil, c_sb, AF.Silu, [tC], [tC])
            cp("dve", crep, bc1(csil, 128), [tC], [tC])
            wada_sb = [A.alloc([8, 512], F32) for _ in range(2)]
            tWa = TL(2, "wada")
            bada_bc = A.alloc([3 * D], F32)
            ng_bc = A.alloc([D], F32)
            modst = A.alloc([3 * D], F32)
            tBa = T("bada")
            tNg = T("ng")
            tMs = T("modst")
            it = 0
            for l in range(depth):
                P.dma("sp", bada_bc, b_ada[l:l + 1, :].partition_broadcast(128), writes=[tBa])
                P.dma("sp", ng_bc, norm_g[l:l + 1, :].partition_broadcast(128), writes=[tNg])
                for cg in range(6):
                    s = it % 2
                    it += 1
                    P.dma("sp", wada_sb[s], w_ada[l].rearrange("(k p) c -> p k c", p=128)[:, :, cg * 512:(cg + 1) * 512], writes=[tWa[s]])
                    bk = cg % 2
                    for k in range(8):
                        mm(bankf(bk), crep[:, k, :], wada_sb[s][:, k, :], k == 0, k == 7, [tC, tWa[s]], [bT[bk]])
                    tt("dve", modst[:, cg * 512:(cg + 1) * 512], bankf(bk), bada_bc[:, cg * 512:(cg + 1) * 512], ALU.add, [tBa], [bT[bk], tMs])
                stt("dve", modst[:, D:2 * D], modst[:, D:2 * D], 1.0, ng_bc, ALU.add, ALU.mult, [tNg], [tMs])
                P.dma("pool", modtab[l], modst, reads=[tMs])
            A.reset(m0)
            P.barrier()

            if stop == "ada":
                raise _Stop()
            for l in range(depth):
                last = (l == depth - 1)
                x_src = x_in if l == 0 else xs_buf
                P.dma("sp", modv, modtab[l], writes=[tMod])
                P.dma("sp", ssdg_bc, ssd_g[l:l + 1, :].partition_broadcast(128), writes=[tLay])
                P.dma("sp", dskip_bc, d_skip[l:l + 1, :].partition_broadcast(128), writes=[tLay])
                P.dma("sp", dtb_bc, dt_bias[l:l + 1, :].partition_broadcast(128), writes=[tLay])
                P.dma("sp", A_bc, a_log[l:l + 1, :].partition_broadcast(128), writes=[tAl])
                P.dma("sp", bf_bc, b_forget[l:l + 1, :].partition_broadcast(128), writes=[tLay])
                P.dma("sp", cw_sb, cw_t[l], writes=[tLay])
                P.dma("sp", cb_sb, cb_t[l], writes=[tLay])
                act(A_bc, A_bc, AF.Exp, [tAl], [tAl])
                ts("dve", A_bc, A_bc, -1.0, None, ALU.mult, None, [tAl], [tAl])
                P.barrier()
                shift_bc = modv[:, 0:D]
                gs_bc = modv[:, D:2 * D]
                gate_bc = modv[:, 2 * D:3 * D]

                m1 = A.mark()
                wbuf = [A.alloc([8, 512], BF16) for _ in range(2)]
                tW = TL(2, "wbuf")
                wsm = A.alloc([8, 24], BF16)
                tWs = T("wsm")
                xt = [A.alloc([D], F32) for _ in range(2)]
                tX = TL(2, "xt")
                hn = [A.alloc([D], BF16)] * 2
                tHn = [T("hn")] * 2
                htmp = A.alloc([D], F32)
                tHt = T("htmp")
                junk = A.alloc([D], BF16)
                tJunk = T("junk")
                hT = A.alloc([8, 512], BF16)
                tHT = TL(4, "hT")
                dbufs = []
                for _i in range(2):
                    dbufs.append(dict(
                        rq=A.alloc([4, 512], BF16), rk=A.alloc([4, 512], BF16), rv=A.alloc([4, 512], BF16),
                        srg=A.alloc([4, 512], BF16), sz=A.alloc([4, D], BF16), dtf=A.alloc([4, 24], F32),
                        tRq=TL(4, "rq"), tRk=TL(4, "rk"), tRv=TL(4, "rv"), tRg=TL(4, "rg"), tSz=TL(4, "sz"), tDtf=TL(4, "dtf")))
                fvp = A.alloc([8, 4, 72], BF16)
                tFv = TL(4, "fvp")
                xbcT = A.alloc([12, 512], BF16)
                tXbc = TL(12, "xbcT")
                ubuf = [A.alloc([515], F32) for _ in range(2)]
                tU = TL(2, "ubuf")
                tUh = TL(2, "ubufh")
                cacc = [A.alloc([512], F32)] * 2
                tCa = [T("cacc")] * 2
                hist = A.alloc([12, 3], F32)
                tHist = TL(12, "hist")
                qkst = [A.alloc([512], BF16) for _ in range(2)]
                qkst.append(qkst[0])
                tQk = TL(2, "qkst")
                tQk.append(tQk[0])
                cst = [A.alloc([128], F32) for _ in range(2)]
                tCst = TL(2, "cst")
                ssx = A.alloc([8], F32)
                tSsx = T("ssx")
                rt = [A.alloc([4, 64], F32) for _ in range(4)]
                tRt = TL(4, "rt")
                qrot = A.alloc([4, 128], BF16)
                krot = A.alloc([4, 128], BF16)
                tQrot, tKrot = T("qrot"), T("krot")
                qT = A.alloc([4, 128], BF16)
                kT = A.alloc([4, 128], BF16)
                qdT = A.alloc([4, 128], BF16)
                kd = A.alloc([4, 128], BF16)
                tQT, tKT, tQdT, tKd = T("qT"), T("kT"), T("qdT"), T("kd")
                smT = A.alloc([4, 128], BF16)
                tSmT = T("smT")
                Sr = A.alloc([4, 128], F32)
                Sr_bf = A.alloc([4, 128], BF16)
                tSr, tSrb = T("Sr"), T("Srb")
                rss = A.alloc([8], F32)
                tRss = T("rss")
                otmp = A.alloc([4, 128], F32)
                tOt = T("otmp")
                mixst = [A.alloc([1536], BF16) for _ in range(2)]
                tMixR = TL(2, "mixstR")
                tMixS = TL(2, "mixstS")
                dtv = A.alloc([16], F32)
                av = A.alloc([16], F32)
                tDt = T("dtv")
                acum = A.alloc([16], F32)
                eacum = A.alloc([16], F32)
                elast = A.alloc([16], F32)
                decj = A.alloc([16], F32)
                tAc = T("acum")
                tDecj = T("decj")
                rhs1 = A.alloc([16, 128], F32)
                tRhs1 = TL(2, "rhs1")
                LT = A.alloc([16, 128], BF16)
                tLT = TL(2, "LT")
                MT = LT
                tMT = tLT
                cbm = A.alloc([2, 128], BF16)
                tCbm = T("cbm")
                xs_tm = A.alloc([D], BF16)
                xdt = A.alloc([D], BF16)
                xdtd = A.alloc([D], BF16)
                tXs, tXdt, tXdtd = T("xs_tm"), T("xdt"), T("xdtd")
                B_tm = A.alloc([2, 128], BF16)
                tBtm = T("B_tm")
                Ss = A.alloc([2, 512], F32)
                Ss_bf = A.alloc([2, 512], BF16)
                tSs, tSsb = T("Ss"), T("Ssb")
                y1 = A.alloc([D], F32)
                y2 = A.alloc([D], F32)
                tY1, tY2 = T("y1"), T("y2")
                sss = A.alloc([2], F32)
                tSss = T("sss")
                fr = A.alloc([8], F32)
                tFr = T("fr")
                carry_bc = A.alloc([8], F32)
                tCarry = T("carry")
                relc = A.alloc([1], F32, parts=8)
                tRelc = T("relc")
                fpt = A.alloc([128], F32, parts=8)
                fpl = A.alloc([128], F32, parts=8)
                tFpt = T("fpt")
                fqhi = [A.alloc([512], BF16, parts=8) for _ in range(2)]
                fqlo = [A.alloc([512], BF16, parts=8) for _ in range(2)]
                tFq = TL(2, "fq")
                if l == 0:
                    print("pass1 arena used", A.off, "base", base_mark)

                memset("pool", Sr, 0.0, [tSr])
                memset("pool", Sr_bf, 0.0, [tSrb])
                memset("pool", Ss, 0.0, [tSs])
                memset("pool", Ss_bf, 0.0, [tSsb])
                memset("pool", hist, 0.0, tHist)
                memset("pool", carry_bc, 0.0, [tCarry])
                memset("pool", fvp, 1.0, tFv)

                B_ACC = [0, 1]
                B_T = 7
                B_O = 2
                B_SM = 3
                B_W = [4, 5, 6, 7]

                wgi = [0]

                def load_wgroup(c0, ncols):
                    s = wgi[0] % 2
                    wgi[0] += 1
                    P.dma("sp", wbuf[s], wb_in[l, WG_COLS.index(c0)], writes=[tW[s]])
                    return s

                acci = [0]

                def next_acc():
                    b = B_ACC[acci[0] % 2]
                    acci[0] += 1
                    return b

                def gen_AB(st):
                    _d = dbufs[st % 2]
                    rq, rk, rv, srg, sz, dtf = _d["rq"], _d["rk"], _d["rv"], _d["srg"], _d["sz"], _d["dtf"]
                    tRq, tRk, tRv, tRg, tSz, tDtf = _d["tRq"], _d["tRk"], _d["tRv"], _d["tRg"], _d["tSz"], _d["tDtf"]
                    for j in range(4):
                        t = st * 4 + j
                        s = t % 2
                        P.dma("sp", xt[s], x_src[t * 128:(t + 1) * 128, :], writes=[tX[s]])
                        act(junk, xt[s], AF.Square, [tX[s]], [tJunk, tSsx], accum=ssx[:, 0:1])
                        rsqrt_mean(ssx[:, 1:2], ssx[:, 0:1], D, [tSsx], [tSsx])
                        stt("dve", htmp, xt[s], ssx[:, 1:2], gs_bc, ALU.mult, ALU.mult, [tX[s], tSsx, tMod], [tHt])
                        tt("dve", hn[s], htmp, shift_bc, ALU.add, [tHt, tMod], [tHn[s]])
                        bA = next_acc()
                        for k in range(8):
                            tr(bankb(bA)[:, k * 128:(k + 1) * 128], hn[s][:, k * 128:(k + 1) * 128], [tHn[s]], [bT[bA]])
                        cp("act" if j % 2 == 0 else "dve", hT[:, :, j * 128:(j + 1) * 128],
                           bankb(bA).rearrange("p (k c) -> p k c", k=8), [], [bT[bA], tHT[j]])
                        yield

                    pend = []

                    def flush():
                        while pend:
                            pend.pop(0)()

                    def tm_group(c0, evac):
                        s = load_wgroup(c0, 512)
                        for j in range(4):
                            b = next_acc()
                            for k in range(8):
                                mm(bankf(b), hT[:, k, j * 128:(j + 1) * 128], wbuf[s][:, k, :], k == 0, k == 7, [tHT[j], tW[s]], [bT[b]])
                            if pend:
                                pend.pop(0)()
                            pend.append(lambda j=j, b=b: evac(j, b))
                            yield

                    def fm_group(c0, evac):
                        s = load_wgroup(c0, 512)
                        for cc in range(4):
                            b = next_acc()
                            for k in range(8):
                                mm(bankf(b), wbuf[s][:, k, cc * 128:(cc + 1) * 128], hT[:, k, :], k == 0, k == 7, tHT + [tW[s]], [bT[b]])
                            if pend:
                                pend.pop(0)()
                            pend.append(lambda cc=cc, b=b: evac(cc, b))
                            yield

                    def qk_evac(dst, scale):
                        def f(cc, b):
                            s = acci[0] % 3
                            if scale is None:
                                cp("dve", qkst[s], bankf(b), [], [bT[b], tQk[s]])
                            else:
                                act(qkst[s], bankf(b), AF.Copy, [], [bT[b], tQk[s]], scale=scale)
                            P.dma("pool", dst.rearrange("h d l -> (h d) l")[cc * 128:(cc + 1) * 128, st * 512:(st + 1) * 512], qkst[s], reads=[tQk[s]])
                        return f
                    yield from fm_group(C_FQ, qk_evac(QTs, 0.125))
                    yield from fm_group(C_FK, qk_evac(KTs, None))
                    yield from tm_group(C_FV, lambda j, b: cp("dve", fvp[:, :, j, 0:64], bankf(b).rearrange("p (h d) -> p h d", h=8), [], [bT[b], tFv[j]]))
                    flush()
                    for h8 in range(8):
                        P.dma("pool", Vs[h8, :, st * 4:(st + 1) * 4, :], fvp[:, h8, :, :], reads=tFv)

                    def g_evac(cc, b):
                        s3 = acci[0] % 3
                        act(qkst[s3], bankf(b), AF.Silu, [], [bT[b], tQk[s3]])
                        P.dma("pool", GTs.rearrange("h d l -> (h d) l")[cc * 128:(cc + 1) * 128, st * 512:(st + 1) * 512], qkst[s3], reads=[tQk[s3]])
                    yield from fm_group(C_FG, g_evac)

                    yield from tm_group(C_RQ, lambda j, b: cp("dve", rq[:, j, :], bankf(b), [], [bT[b], tRq[j]]))
                    yield from tm_group(C_RK, lambda j, b: act(rk[:, j, :], bankf(b), AF.Copy, [], [bT[b], tRk[j]], scale=128.0 ** -0.5))
                    yield from tm_group(C_RV, lambda j, b: cp("dve", rv[:, j, :], bankf(b), [], [bT[b], tRv[j]]))
                    yield from tm_group(C_RG, lambda j, b: act(srg[:, j, :], bankf(b), AF.Silu, [], [bT[b], tRg[j]]))
                    flush()
                    if st == 0:
                        P.dma("sp", wsm, wb_sm[l], writes=[tWs])
                    for j in range(4):
                        b = next_acc()
                        for k in range(8):
                            mm(bankf(b)[:, 0:24], hT[:, k, j * 128:(j + 1) * 128], wsm[:, k, :], k == 0, k == 7, [tHT[j], tWs], [bT[b]])
                        cp("dve", dtf[:, j, :], bankf(b)[:, 0:24], [], [bT[b], tDtf[j]])
                    for half in range(2):
                        yield from tm_group(C_Z + half * 512,
                                 lambda j, b, half=half: act(sz[:, j, half * 512:(half + 1) * 512], bankf(b), AF.Silu, [], [bT[b], tSz[j]]))

                    def conv_evac(g):
                        def f(cc, b):
                            ch = g * 4 + cc
                            s = ch % 2
                            cp("act", ubuf[s][:, 3:515], bankf(b), [], [bT[b], tU[s]])
                            cp("pool", ubuf[s][:, 0:3], hist[:, ch, :], [tHist[ch]], [tUh[s]])
                            ts("dve", cacc[s], ubuf[s][:, 3:515], cw_sb[:, ch, 3:4], cb_sb[:, ch:ch + 1], ALU.mult, ALU.add, [tU[s], tLay], [tCa[s]])
                            for kk in range(3):
                                stt("dve", cacc[s], ubuf[s][:, kk:kk + 512], cw_sb[:, ch, kk:kk + 1], cacc[s], ALU.mult, ALU.add, [tU[s], tUh[s], tLay], [tCa[s]])
                            cp("pool", hist[:, ch, :], ubuf[s][:, 512:515], [tU[s]], [tHist[ch]])
                            act(xbcT[:, ch, :], cacc[s], AF.Silu, [tCa[s]], [tXbc[ch]])
                        return f
                    for g in range(3):
                        yield from fm_group(C_XBC + g * 512, conv_evac(g))

                    flush()

                def gen_C(st):
                    _d = dbufs[st % 2]
                    rq, rk, rv, srg, sz, dtf = _d["rq"], _d["rk"], _d["rv"], _d["srg"], _d["sz"], _d["dtf"]
                    tRq, tRk, tRv, tRg, tSz, tDtf = _d["tRq"], _d["tRk"], _d["tRv"], _d["tRg"], _d["tSz"], _d["tDtf"]
                    for j in range(4):
                        t = st * 4 + j
                        ms = t % 2
                        cs_ = cst[t % 2]
                        P.dma("sp", cs_, cs_tab[t * 128:(t + 1) * 128, :], writes=[tCst[t % 2]])
                        cosb = bc0(cs_[:, 0:64], 4)
                        sinb = bc0(cs_[:, 64:128], 4)
                        def g_ret():
                            for (src, tsrc, dst, tdst, eng) in ((rq, tRq, qrot, tQrot, "pool"), (rk, tRk, krot, tKrot, "dve")):
                                v = src[:, j, :].rearrange("p (h two d) -> p h two d", h=4, two=2)
                                o = dst.rearrange("p h (two d) -> p h two d", two=2)
                                R = [tsrc[j], tCst[t % 2]]
                                tt(eng, rt[0], v[:, :, 0, :], cosb, ALU.mult, R, [tRt[0]])
                                tt(eng, rt[1], v[:, :, 1, :], sinb, ALU.mult, R, [tRt[1]])
                                tt(eng, o[:, :, 0, :], rt[0], rt[1], ALU.subtract, [tRt[0], tRt[1]], [tdst])
                                tt(eng, rt[2], v[:, :, 0, :], sinb, ALU.mult, R, [tRt[2]])
                                tt(eng, rt[3], v[:, :, 1, :], cosb, ALU.mult, R, [tRt[3]])
                                tt(eng, o[:, :, 1, :], rt[2], rt[3], ALU.add, [tRt[2], tRt[3]], [tdst])
                            tt("dve", kd, krot, bc1(csb["dk"], 128), ALU.mult, [tKrot, tConst], [tKd])
                            yield
                            bq = B_W[0]
                            for h in range(4):
                                tr(bankb(bq)[:, h * 128:(h + 1) * 128], qrot[:, h, :], [tQrot], [bT[bq]])
                            for h in range(4):
                                tr(bankb(bq)[:, 512 + h * 128:512 + (h + 1) * 128], krot[:, h, :], [tKrot], [bT[bq]])
                            cp("act", qT, bankb(bq)[:, 0:512].rearrange("p (h c) -> p h c", h=4), [], [bT[bq], tQT])
                            tt("dve", qdT, bankb(bq)[:, 0:512].rearrange("p (h c) -> p h c", h=4), csb["dq"], ALU.mult, [tConst], [bT[bq], tQdT])
                            cp("act", kT, bankb(bq)[:, 512:1024].rearrange("p (h c) -> p h c", h=4), [], [bT[bq], tKT])
                            yield
                            bs_ = B_W[1]
                            for h in range(4):
                                mm(bankf(bs_)[:, h * 128:(h + 1) * 128], kT[:, h, :], qT[:, h, :], h == 0, h == 3, [tKT, tQT], [bT[bs_]])
                            tt("dve", smT, bankf(bs_).rearrange("p (h c) -> p h c", h=4), csb["dintraT"], ALU.mult, [tConst], [bT[bs_], tSmT])
                            yield
                            bo = B_O
                            for h in range(4):
                                mm(bankf(bo)[:, h * 128:(h + 1) * 128], smT[:, h, :], rv[:, j, h * 128:(h + 1) * 128], h == 0, False, [tSmT, tRv[j]], [bT[bo]])
                                mm(bankf(bo)[:, h * 128:(h + 1) * 128], qdT[:, h, :], Sr_bf[:, h, :], False, h == 3, [tQdT, tSrb], [bT[bo]])
                            bn = B_W[3]
                            for h in range(4):
                                mm(bankf(bn)[:, h * 128:(h + 1) * 128], kd[:, h, :], rv[:, j, h * 128:(h + 1) * 128], h == 0, h == 3, [tKd, tRv[j]], [bT[bn]])
                            for h in range(4):
                                stt("dve", Sr[:, h, :], Sr[:, h, :], dchunk[h], bankf(bn)[:, h * 128:(h + 1) * 128], ALU.mult, ALU.add, [], [bT[bn], tSr])
                            cp("act", Sr_bf, Sr, [tSr], [tSrb])
                            yield
                            for h in range(4):
                                act(junk[:, 0:128], bankf(bo)[:, h * 128:(h + 1) * 128], AF.Square, [], [bT[bo], tJunk, tRss], accum=rss[:, h:h + 1])
                            rsqrt_mean(rss[:, 4:8], rss[:, 0:4], 128, [tRss], [tRss])
                            tt("dve", otmp, bankf(bo).rearrange("p (h c) -> p h c", h=4), bc1(rss[:, 4:8], 128), ALU.mult, [tRss], [bT[bo], tOt])
                            tt("pool", mixst[ms][:, 0:512], otmp.rearrange("p h c -> p (h c)"), srg[:, j, :], ALU.mult, [tOt, tRg[j]], [tMixR[ms]])
                            yield

                        def g_ssd():
                            tsl = slice(j * 128, (j + 1) * 128)
                            tt("dve", dtv, dtf[:, j, 0:16], dtb_bc, ALU.add, [tDtf[j], tLay], [tDt])
                            act(dtv, dtv, AF.Exp, [tDt], [tDt])
                            act(dtv, dtv, AF.Ln, [tDt], [tDt], bias=1.0)
                            tt("dve", av, dtv, A_bc, ALU.mult, [tDt, tLay], [tDt])
                            bsm = B_SM
                            mm(bankf(bsm)[:, 0:16], tri, av, True, False, [tConst, tDt], [bT[bsm]])
                            mm(bankf(bsm)[:, 16:32], ones_f, av, False, True, [tConst, tDt], [bT[bsm]])
                            act(eacum, bankf(bsm)[:, 0:16], AF.Exp, [], [bT[bsm], tAc])
                            act(elast, bankf(bsm)[:, 16:32], AF.Exp, [], [bT[bsm], tAc])
                            tt("dve", rhs1[:, 0:8, :], bc0(tri, 8), bc1(av[:, 0:8], 128), ALU.mult, [tConst, tDt], [tRhs1[0]])
                            tt("pool", rhs1[:, 8:16, :], bc0(tri, 8), bc1(av[:, 8:16], 128), ALU.mult, [tConst, tDt], [tRhs1[1]])
                            yield
                            for q4 in range(4):
                                b = B_W[q4]
                                mm(bankf(b), gst, rhs1[:, q4 * 4:(q4 + 1) * 4, :].rearrange("p h c -> p (h c)"), True, True, [tConst, tRhs1[q4 // 2]], [bT[b]])
                                act(LT[:, q4 * 4:(q4 + 1) * 4, :].rearrange("p h c -> p (h c)"), bankf(b), AF.Exp, [], [bT[b], tLT[q4 // 2]])
                                cp("dve", decj[:, q4 * 4:(q4 + 1) * 4], bankf(b).rearrange("p (h c) -> p h c", h=4)[:, :, 127], [], [bT[b], tDecj])
                            act(decj, decj, AF.Exp, [tDecj], [tDecj])
                            yield
                            for g in range(2):
                                mm(bankf(bsm)[:, 128 + g * 128:128 + (g + 1) * 128], xbcT[:, 8 + g, tsl], xbcT[:, 10 + g, tsl], False, g == 1,
                                   [tXbc[8 + g], tXbc[10 + g]], [bT[bsm]])
                            tt("dve", cbm, bankf(bsm)[:, 128:384].rearrange("p (g c) -> p g c", g=2), bc0(csb["mask01_f"], 2), ALU.mult, [tConst], [bT[bsm], tCbm])
                            for g in range(2):
                                tt("dve" if g == 0 else "pool", MT[:, g * 8:(g + 1) * 8, :], LT[:, g * 8:(g + 1) * 8, :], bc0(cbm[:, g, :], 8), ALU.mult, [tLT[g], tCbm], [tMT[g]])
                            for c8 in range(8):
                                tr(bankb(B_T)[:, c8 * 128:(c8 + 1) * 128], xbcT[:, c8, tsl], [tXbc[c8]], [bT[B_T]])
                            cp("act", xs_tm, bankb(B_T), [], [bT[B_T], tXs])
                            tt("dve", xdt.rearrange("p (h d) -> p h d", h=16), bankb(B_T).rearrange("p (h d) -> p h d", h=16), bc1(dtv, 64), ALU.mult, [tDt], [bT[B_T], tXdt])
                            for g in range(2):
                                tr(bankb(B_T)[:, g * 128:(g + 1) * 128], xbcT[:, 8 + g, tsl], [tXbc[8 + g]], [bT[B_T]])
                            cp("act", B_tm, bankb(B_T)[:, 0:256].rearrange("p (g c) -> p g c", g=2), [], [bT[B_T], tBtm])
                            yield
                            for h in range(16):
                                b = B_W[h // 8]
                                hh = h % 8
                                mm(bankf(b)[:, hh * 64:(hh + 1) * 64], MT[:, h, :], xdt[:, h * 64:(h + 1) * 64], hh == 0, hh == 7, [tMT[h // 8], tXdt], [bT[b]])
                            for g in range(2):
                                b = B_W[2 + g]
                                mm(bankf(b), xbcT[:, 10 + g, tsl], Ss_bf[:, g, :], True, True, [tXbc[10 + g], tSsb], [bT[b]])
                            for g in range(2):
                                b = B_W[2 + g]
                                tt("dve", y1[:, g * 512:(g + 1) * 512].rearrange("p (h d) -> p h d", h=8), bankf(b).rearrange("p (h d) -> p h d", h=8),
                                   bc1(eacum[:, g * 8:(g + 1) * 8], 64), ALU.mult, [tAc], [bT[b], tY1])
                            for g in range(2):
                                b = B_W[g]
                                tt("dve", y1[:, g * 512:(g + 1) * 512], y1[:, g * 512:(g + 1) * 512], bankf(b), ALU.add, [], [bT[b], tY1])
                            tt("pool", y2.rearrange("p (h d) -> p h d", h=16), xs_tm.rearrange("p (h d) -> p h d", h=16), bc1(dskip_bc, 64), ALU.mult, [tXs, tLay], [tY2])
                            tt("pool", y2, y2, y1, ALU.add, [tY1], [tY2])
                            tt("pool", y2, y2, sz[:, j, :], ALU.mult, [tSz[j]], [tY2])
                            yield
                            act(junk, y2, AF.Square, [tY2], [tJunk, tSss], accum=sss[:, 0:1])
                            rsqrt_mean(sss[:, 1:2], sss[:, 0:1], D, [tSss], [tSss])
                            stt("dve", mixst[ms][:, 512:1536], y2, sss[:, 1:2], ssdg_bc, ALU.mult, ALU.mult, [tY2, tSss, tLay], [tMixS[ms]])
                            tt("pool", xdtd.rearrange("p (h d) -> p h d", h=16), xdt.rearrange("p (h d) -> p h d", h=16), bc1(decj, 64), ALU.mult, [tXdt, tDecj], [tXdtd])
                            for g in range(2):
                                b = B_W[2 + g]
                                mm(bankf(b), B_tm[:, g, :], xdtd[:, g * 512:(g + 1) * 512], True, True, [tBtm, tXdtd], [bT[b]])
                            for g in range(2):
                                b = B_W[2 + g]
                                tt("pool", Ss[:, g, :].rearrange("p (h d) -> p h d", h=8), Ss[:, g, :].rearrange("p (h d) -> p h d", h=8),
                                   bc1(elast[:, g * 8:(g + 1) * 8], 64), ALU.mult, [tAc], [tSs])
                                tt("dve", Ss[:, g, :], Ss[:, g, :], bankf(b), ALU.add, [], [bT[b], tSs])
                            cp("act", Ss_bf, Ss, [tSs], [tSsb])
                            yield
                            fs = st % 2
                            tt("dve", fr, dtf[:, j, 16:24], bf_bc, ALU.add, [tDtf[j], tLay], [tFr])
                            act(fr, fr, AF.Exp, [tFr], [tFr], scale=-1.0)
                            act(fr, fr, AF.Ln, [tFr], [tFr], bias=1.0)
                            ts("dve", fr, fr, -1.0, None, ALU.mult, None, [tFr], [tFr])
                            mm(bankf(bsm)[:, 0:8], tri, fr, True, False, [tConst, tFr], [bT[bsm]])
                            mm(bankf(bsm)[:, 8:16], ones_f, fr, False, False, [tConst, tFr], [bT[bsm]])
                            mm(bankf(bsm)[0:8, 16:144], fr, tri, False, True, [tConst, tFr], [bT[bsm]])
                            if j == 0:
                                cp("dve", carry_hist[:, st, :], carry_bc, [tCarry], [tCH])
                                memset("pool", relc, 0.0, [tRelc])
                            tt("dve", F_all[:, t, :], bankf(bsm)[:, 0:8], carry_bc, ALU.add, [tCarry], [bT[bsm], tF])
                            tt("dve", carry_bc, bankf(bsm)[:, 8:16], carry_bc, ALU.add, [], [bT[bsm], tCarry])
                            ts("dve", fpt, bankf(bsm)[0:8, 16:144], relc[:, 0:1], None, ALU.add, None, [tRelc], [bT[bsm], tFpt])
                            cp("dve", relc, fpt[:, 127:128], [tFpt], [tRelc])
                            cp("dve", fqhi[fs][:, tsl], fpt, [tFpt], [tFq[fs]])
                            tt("dve", fpl, fpt, fqhi[fs][:, tsl], ALU.subtract, [tFpt, tFq[fs]], [tFpt])
                            cp("dve", fqlo[fs][:, tsl], fpl, [tFpt], [tFq[fs]])
                            yield
                        alive_ = [g_ret(), g_ssd()]
                        while alive_:
                            for g__ in list(alive_):
                                try:
                                    next(g__)
                                except StopIteration:
                                    alive_.remove(g__)
                            yield
                        P.dma("pool", mixA[t * 128:(t + 1) * 128, :], mixst[ms], reads=[tMixR[ms], tMixS[ms]])
                    fs = st % 2
                    P.dma("pool", FQs[0, :, st * 512:(st + 1) * 512], fqhi[fs], reads=[tFq[fs]])
                    P.dma("pool", FQs[1, :, st * 512:(st + 1) * 512], fqlo[fs], reads=[tFq[fs]])
                def drive(gens):
                    alive = list(gens)
                    while alive:
                        for g_ in list(alive):
                            try:
                                next(g_)
                            except StopIteration:
                                alive.remove(g_)

                for st in range(NST + 1):
                    gens = []
                    if st < NST:
                        gens.append(gen_AB(st))
                    if st >= 1:
                        gens.append(gen_C(st - 1))
                    drive(gens)
                if debug and l == depth - 1:
                    P.dma("pool", dbgF, F_all, reads=[tF])
                A.reset(m1)
                P.barrier()
                if stop == "pass1":
                    raise _Stop()

                m2 = A.mark()
                wout = A.alloc([12, D], BF16)
                woutF = A.alloc([8, D], BF16, parts=64)
                tWo = T("wout")
                tWoF = T("woutF")
                P.dma("sp", wout, wb_out[l], writes=[tWo])
                P.dma("sp", woutF, wb_outF[l], writes=[tWoF])
                ktb = [A.alloc([L], BF16) for _ in range(2)]
                tKtb = TL(2, "ktb")
                vb = [A.alloc([NT, 72], BF16) for _ in range(2)]
                tVb = TL(2, "vb")
                qtb = [A.alloc([8, 512], BF16) for _ in range(2)]
                tQtb = TL(2, "qtb")
                tQtbH = TL(2, "qtbH")
                tQtbL = TL(2, "qtbL")
                gtb = A.alloc([8, 512], BF16, parts=64)
                tGtb = T("gtb")
                mxa = [A.alloc([1536], BF16) for _ in range(2)]
                tMxa = TL(2, "mxa")
                x2 = [A.alloc([D], F32) for _ in range(2)]
                tX2 = TL(2, "x2")
                pT = [A.alloc([512], BF16) for _ in range(3)]
                tPT = TL(3, "pT")
                mixFT = A.alloc([8, 512], BF16, parts=64)
                tMixFT = T("mixFT")
                mT = A.alloc([12, 128], BF16)
                tMT2 = TL(2, "mT")
                biasq = A.alloc([NT, 8], F32)
                tBq = T("biasq")
                accS = [A.alloc([512], F32) for _ in range(2)]
                tAccS = TL(2, "accS")
                rden = A.alloc([512], F32, parts=64)
                tRden = T("rden")
                o1 = A.alloc([512], F32, parts=64)
                tO1 = T("o1")
                selden = A.alloc([64], F32)
                tSd = T("selden")
                xn = [A.alloc([D], F32) for _ in range(2)]
                tXn = TL(2, "xn")
                fss = A.alloc([2], F32)
                tFss = T("fss")
                fjunk = A.alloc([D], BF16)
                tFj = T("fjunk")
                fgb = A.alloc([D], F32)
                tFgb = T("fgb")
                if l == 0:
                    print("pass2 arena used", A.off)
                if last:
                    P.dma("sp", fgb, final_g.partition_broadcast(128), writes=[tFgb])
                for s in range(2):
                    memset("pool", ktb[s][64:128, :], 1.0, [tKtb[s]])
                memset("pool", selden, 0.0, [tSd])
                memset("pool", selden[64:65, :], 1.0, [tSd])

                BS = [0, 1, 2]
                BA = [3, 4]
                BX = 5
                BO = [6, 7]
                si = 0
                ai = 0
                pi_ = 0
                hi_ = 0
                for qt in range(NST):
                    nkt = 4 * qt + 4
                    qs = qt % 2
                    qcols = slice(qt * 512, (qt + 1) * 512)
                    P.dma("sp", qtb[qs][0:64, :, :], QTs.rearrange("h d l -> d h l")[:, :, qcols], writes=[tQtb[qs]])
                    P.dma("sp", qtb[qs][64:65, :, :], FQs[0:1, :, qcols], writes=[tQtbH[qs]])
                    P.dma("sp", qtb[qs][65:66, :, :], FQs[1:2, :, qcols], writes=[tQtbL[qs]])
                    P.dma("sp", gtb, GTs.rearrange("h d l -> d h l")[:, :, qcols], writes=[tGtb])
                    tt("dve", biasq[:, 0:nkt, :], bc0(carry_hist[:, qt, :], nkt), F_all[:, 0:nkt, :], ALU.subtract, [tCH, tF], [tBq])
                    LOOK = 2
                    its = []
                    for h in range(8):
                        ks = hi_ % 2
                        hi_ += 1
                        ba = BA[ai % 2]
                        as_ = ai % 2
                        ai += 1
                        for kt in range(nkt):
                            its.append((h, kt, ks, ba, as_))
                    state = {}

                    def emit_qk(idx):
                        h, kt, ks, ba, as_ = its[idx]
                        if kt == 0:
                            P.dma("sp", ktb[ks][0:64, 0:nkt * 128], KTs[h, :, 0:nkt * 128], writes=[tKtb[ks]])
                            P.dma("sp", vb[ks][:, 0:nkt, :], Vs[h, :, 0:nkt, :], writes=[tVb[ks]])
                        di = kt - 4 * qt
                        q0 = 128 * max(di, 0)
                        bs_ = BS[idx % 3]
                        ps = (pi_ + idx) % 3
                        mm(bankf(bs_)[:, q0:512], ktb[ks][0:66, kt * 128:(kt + 1) * 128], qtb[qs][0:66, h, q0:512], True, True,
                           [tKtb[ks], tQtb[qs], tQtbH[qs], tQtbL[qs]], [bT[bs_]])
                        act(pT[ps][:, q0:512], bankf(bs_)[:, q0:512], AF.Exp, [tBq], [bT[bs_], tPT[ps]], bias=biasq[:, kt, h:h + 1])
                        if di >= 0:
                            tt("pool", pT[ps][:, q0:q0 + 128], pT[ps][:, q0:q0 + 128], csb["mask01_bf"], ALU.mult, [tConst], [tPT[ps]])

                    def emit_pv(idx):
                        h, kt, ks, ba, as_ = its[idx]
                        di = kt - 4 * qt
                        q0 = 128 * max(di, 0)
                        ps = (pi_ + idx) % 3
                        mm(bankf(ba)[0:65, q0:512], vb[ks][:, kt, 0:65], pT[ps][:, q0:512], kt == 0, kt == nkt - 1, [tPT[ps], tVb[ks]], [bT[ba]])
                        if kt == nkt - 1:
                            cp("act", accS[as_][0:65, :], bankf(ba)[0:65, :], [], [bT[ba], tAccS[as_]])
                            mm(bankf(BX)[0:64, :], selden[0:65, :], accS[as_][0:65, :], True, True, [tSd, tAccS[as_]], [bT[BX]])
                            P.op("dve", lambda e, o=rden, i=bankf(BX)[0:64, :]: e.reciprocal(o, i), reads=[], writes=[bT[BX], tRden])
                            tt("dve", o1, accS[as_][0:64, :], rden, ALU.mult, [tAccS[as_], tRden], [tO1])
                            tt("pool", mixFT[:, h, :], o1, gtb[:, h, :], ALU.mult, [tO1, tGtb], [tMixFT])

                    nit = len(its)
                    for idx in range(min(LOOK, nit)):
                        emit_qk(idx)
                    for idx in range(nit):
                        if idx + LOOK < nit:
                            emit_qk(idx + LOOK)
                        emit_pv(idx)
                    pi_ += nit
                    for j in range(4):
                        t = qt * 4 + j
                        s = t % 2
                        P.dma("sp", mxa[s], mixA[t * 128:(t + 1) * 128, :], writes=[tMxa[s]])
                        P.dma("sp", x2[s], x_src[t * 128:(t + 1) * 128, :], writes=[tX2[s]])
                        for half in range(2):
                            nk = 8 if half == 0 else 4
                            for kk in range(nk):
                                k = half * 8 + kk
                                tr(bankb(BX)[:, kk * 128:(kk + 1) * 128], mxa[s][:, k * 128:(k + 1) * 128], [tMxa[s]], [bT[BX]])
                            cp("act" if half == 0 else "dve", mT[:, half * 8:half * 8 + nk, :],
                               bankb(BX)[:, 0:nk * 128].rearrange("p (k c) -> p k c", k=nk), [], [bT[BX], tMT2[half]])
                        for cg in range(2):
                            bo = BO[cg]
                            csl = slice(cg * 512, (cg + 1) * 512)
                            for k in range(12):
                                mm(bankf(bo), mT[:, k, :], wout[:, k, csl], k == 0, False, [tMT2[k // 8], tWo], [bT[bo]])
                            for h in range(8):
                                mm(bankf(bo), mixFT[:, h, j * 128:(j + 1) * 128], woutF[:, h, csl], False, h == 7, [tMixFT, tWoF], [bT[bo]])
                            tt("dve", xn[s][:, csl], bankf(bo), gate_bc[:, csl], ALU.mult, [tMod], [bT[bo], tXn[s]])
                            tt("pool", xn[s][:, csl], xn[s][:, csl], x2[s][:, csl], ALU.add, [tX2[s]], [tXn[s]])
                        if not last:
                            P.dma("pool", xs_buf[t * 128:(t + 1) * 128, :], xn[s], reads=[tXn[s]], writes=[])
                        else:
                            act(fjunk, xn[s], AF.Square, [tXn[s]], [tFj, tFss], accum=fss[:, 0:1])
                            rsqrt_mean(fss[:, 1:2], fss[:, 0:1], D, [tFss], [tFss])
                            stt("dve", xn[s], xn[s], fss[:, 1:2], fgb, ALU.mult, ALU.mult, [tFss, tFgb], [tXn[s]])
                            out_toks.append(P.dma("pool", y_out[t * 128:(t + 1) * 128, :], xn[s], reads=[tXn[s]]))
                A.reset(m2)
                P.barrier()

        try:
            _phases()
        except _Stop:
            P.barrier()
        waits = P._collect("pool", (), (), extra=out_toks)
        P.streams["pool"].append((waits, None, None))
        print("ops recorded:", P.nops, "sems:", len(P.semkeys))
        P.emit()
    return nc


def make_in_maps(inputs, L, depth, ncores):
    consts = host_consts()
    maps = []
    f32 = np.float32
    x = np.asarray(inputs["x"], f32)
    c = np.asarray(inputs["c"], f32)
    pos = np.asarray(inputs["positions"], np.int32)
    conv_w = np.asarray(inputs["conv_w"], f32)
    conv_b = np.asarray(inputs["conv_b"], f32)
    cw_t = np.ascontiguousarray(conv_w[:depth].reshape(depth, 4, 12, 128).transpose(0, 3, 2, 1))
    cb_t = np.ascontiguousarray(conv_b[:depth].reshape(depth, 12, 128).transpose(0, 2, 1))
    B = x.shape[0]
    shared = {
        "norm_g": np.ascontiguousarray(np.asarray(inputs["norm_g"], f32)[:depth]),
        "w_ada": np.ascontiguousarray(np.asarray(inputs["w_ada"], f32)[:depth]),
        "b_ada": np.ascontiguousarray(np.asarray(inputs["b_ada"], f32)[:depth]),
        "w_in": np.ascontiguousarray(np.asarray(inputs["w_in"], f32)[:depth]),
        "cw_t": cw_t, "cb_t": cb_t,
        "dt_bias": np.ascontiguousarray(np.asarray(inputs["dt_bias"], f32)[:depth]),
        "a_log": np.ascontiguousarray(np.asarray(inputs["a_log"], f32)[:depth]),
        "d_skip": np.ascontiguousarray(np.asarray(inputs["d_skip"], f32)[:depth]),
        "ssd_norm_g": np.ascontiguousarray(np.asarray(inputs["ssd_norm_g"], f32)[:depth]),
        "b_forget": np.ascontiguousarray(np.asarray(inputs["b_forget"], f32)[:depth]),
        "w_out": np.ascontiguousarray(np.asarray(inputs["w_out"], f32)[:depth]),
        "final_g": np.ascontiguousarray(np.asarray(inputs["final_g"], f32).reshape(1, D)),
    }
    for n, s, dt in CONST_SPECS:
        shared[n] = consts[n]
    for core in range(ncores):
        b = core % B
        m = dict(shared)
        m["x"] = np.ascontiguousarray(x[b, :L])
        m["c_t"] = np.ascontiguousarray(c[b].reshape(8, 128).T)
        m["pos_t"] = np.ascontiguousarray(pos[b, :L].reshape(L // 128, 128).T)
        maps.append(m)
    return maps


def kernel(x, c, positions, norm_g, w_ada, b_ada, w_in, conv_w, conv_b, dt_bias, a_log, d_skip,
           ssd_norm_g, b_forget, w_out, final_g):
    inputs = dict(x=x, c=c, positions=positions, norm_g=norm_g, w_ada=w_ada, b_ada=b_ada, w_in=w_in,
                  conv_w=conv_w, conv_b=conv_b, dt_bias=dt_bias, a_log=a_log, d_skip=d_skip,
                  ssd_norm_g=ssd_norm_g, b_forget=b_forget, w_out=w_out, final_g=final_g)
    B, L, _ = np.asarray(x).shape
    nc = build_nc(L, DEPTH)
    maps = make_in_maps(inputs, L, DEPTH, 8)
    res = run_bass_kernel_spmd(nc, maps, core_ids=list(range(8)))
    out = np.stack([np.asarray(res.results[b]["y"], np.float32) for b in range(B)], axis=0)
    return out
```

```python
import contextlib
import math
import numpy as np
import ml_dtypes
import concourse.bass as bass
import concourse.mybir as mybir
from concourse.bass_utils import run_bass_kernel_spmd

F32 = mybir.dt.float32
BF16 = mybir.dt.bfloat16
I32 = mybir.dt.int32
U8 = mybir.dt.uint8
AF = mybir.ActivationFunctionType
ALU = mybir.AluOpType

D = 1024
NIN = 6680
EPS = 1e-6
EPOCH = 30000
SEQ = 8192
DEPTH = 4


class T:
    __slots__ = ("name", "lw", "rd", "rd_dma")

    def __init__(self, name=""):
        self.name = name
        self.lw = None
        self.rd = {}
        self.rd_dma = []


def TL(n, name=""):
    return [T("%s%d" % (name, i)) for i in range(n)]


class Prog:
    QUEUES = ("sp", "pool", "act")
    NDMASEM = 8

    def __init__(self, nc):
        self.nc = nc
        self.streams = {e: [] for e in ("pe", "dve", "act", "pool", "sp")}
        self.cnt = {}
        self.known = {e: {} for e in self.streams}
        self.semkeys = []
        self.semset = set()
        self.dma_rr = {q: 0 for q in self.QUEUES}
        self.dma_last = {}
        self.last_tok = {}
        self.nops = 0

    def _newtok(self, cname, step):
        n = self.cnt.get(cname, 0)
        self.cnt[cname] = n + 1
        key = (cname, n // EPOCH)
        if key not in self.semset:
            self.semset.add(key)
            self.semkeys.append(key)
        tok = (key, (n % EPOCH + 1) * step)
        self.last_tok[cname] = tok
        return tok

    def _collect(self, eng, reads, writes, extra=()):
        waits = {}
        kn = self.known[eng]

        def need(tok):
            if tok is None:
                return
            key, val = tok
            if eng == "pe" and key[0] == "pe":
                return
            if kn.get(key, 0) >= val:
                return
            if waits.get(key, 0) < val:
                waits[key] = val

        for t in reads:
            need(t.lw)
        for t in writes:
            need(t.lw)
            for r in t.rd.values():
                need(r)
            for r in t.rd_dma:
                need(r)
        for tok in extra:
            need(tok)
        for k, v in waits.items():
            kn[k] = v
        return list(waits.items())

    def op(self, eng, fn, reads=(), writes=()):
        waits = self._collect(eng, reads, writes)
        tok = self._newtok(eng, 1)
        self.streams[eng].append((waits, fn, tok))
        for t in reads:
            t.rd[eng] = tok
        for t in writes:
            t.lw = tok
            t.rd = {}
            t.rd_dma = []
        self.nops += 1
        return tok

    def dma(self, q, out, in_, reads=(), writes=()):
        s = self.dma_rr[q]
        self.dma_rr[q] = (s + 1) % self.NDMASEM
        prev = self.dma_last.get((q, s))
        waits = self._collect(q, reads, writes, extra=(prev,) if prev else ())
        tok = self._newtok(("dma", q, s), 16)
        self.dma_last[(q, s)] = tok

        def fn(e, out=out, in_=in_):
            return e.dma_start(out=out, in_=in_)

        self.streams[q].append((waits, fn, tok))
        for t in reads:
            t.rd_dma.append(tok)
        for t in writes:
            t.lw = tok
            t.rd = {}
            t.rd_dma = []
        self.nops += 1
        return tok

    def barrier(self):
        toks = list(self.last_tok.values())
        for eng in self.streams:
            waits = self._collect(eng, (), (), extra=toks)
            if waits:
                self.streams[eng].append((waits, None, None))

    def emit(self):
        nc = self.nc
        with contextlib.ExitStack() as es:
            sems = {}
            for i, key in enumerate(self.semkeys):
                sems[key] = es.enter_context(nc.semaphore("s%d" % i))
            block = es.enter_context(nc.Block())

            def run(eng_name):
                def body(e):
                    for waits, fn, tok in self.streams[eng_name]:
                        for key, val in waits:
                            e.wait_ge(sems[key], val)
                        if fn is not None:
                            ins = fn(e)
                            step = 16 if isinstance(tok[0][0], tuple) else 1
                            ins.then_inc(sems[tok[0]], step)
                return body

            block.tensor(run("pe"))
            block.vector(run("dve"))
            block.scalar(run("act"))
            block.gpsimd(run("pool"))
            block.sync(run("sp"))


class Arena:
    def __init__(self, ap_u8, nbytes):
        self.a = ap_u8
        self.n = nbytes
        self.off = 0
        self.peak = 0

    def alloc(self, dims, dt, parts=128, p0=0):
        esz = {F32: 4, BF16: 2, I32: 4}[dt]
        tot = esz
        for d_ in dims:
            tot *= d_
        tot_al = (tot + 63) // 64 * 64
        if self.off + tot_al > self.n:
            raise RuntimeError("SBUF arena overflow: need %d at %d of %d" % (tot_al, self.off, self.n))
        v = self.a[p0:p0 + parts, self.off:self.off + tot].bitcast(dt)
        self.off += tot_al
        self.peak = max(self.peak, self.off)
        if len(dims) == 2:
            v = v.rearrange("p (a b) -> p a b", a=dims[0])
        elif len(dims) == 3:
            v = v.rearrange("p (a b c) -> p a b c", a=dims[0], b=dims[1])
        return v

    def mark(self):
        return self.off

    def reset(self, m):
        self.off = m


def bc1(ap, n):
    return ap.unsqueeze(2).to_broadcast([ap.shape[0], ap.shape[1], n])


def bc0(ap, n):
    return ap.unsqueeze(1).to_broadcast([ap.shape[0], n, ap.shape[1]])


C_RQ, C_RK, C_RV, C_RG = 0, 512, 1024, 1536
C_XBC, C_DT, C_Z = 2048, 3584, 3600
C_FQ, C_FK, C_FV, C_FG, C_FR = 4624, 5136, 5648, 6160, 6672
WG_COLS = [0, 512, 1024, 1536, 2048, 2560, 3072, 3600, 4112, 4624, 5136, 5648, 6160]


def host_consts():
    c = {}
    idx = np.arange(128)
    c["ident_bf"] = np.eye(128, dtype=np.float32).astype(ml_dtypes.bfloat16)
    c["tri_incl"] = (idx[:, None] <= idx[None, :]).astype(np.float32)
    c["gstrict"] = (idx[:, None] > idx[None, :]).astype(np.float32)
    c["ones_f"] = np.ones((128, 128), np.float32)
    m01 = (idx[None, :] >= idx[:, None]).astype(np.float32)
    c["mask01_f"] = m01
    c["mask01_bf"] = m01.astype(ml_dtypes.bfloat16)
    H = 4
    log_g = np.log(1.0 - 2.0 ** (-5.0 - np.arange(H, dtype=np.float64)))
    diff = idx[None, :] - idx[:, None]
    dintraT = np.where(diff[None] >= 0, np.exp(log_g[:, None, None] * np.maximum(diff[None], 0)), 0.0)
    c["dintraT"] = np.ascontiguousarray(dintraT.transpose(1, 0, 2)).astype(np.float32)
    dq = np.exp(log_g[:, None] * (idx[None, :] + 1.0))
    c["dq"] = np.ascontiguousarray(np.broadcast_to(dq[None], (128, H, 128))).astype(np.float32)
    dk = np.exp(log_g[:, None] * (127.0 - idx[None, :]))
    c["dk"] = np.ascontiguousarray(dk.T).astype(np.float32)
    c["_dchunk"] = [float(np.exp(log_g[h] * 128.0)) for h in range(H)]
    freq = (10000.0 ** (-np.arange(64, dtype=np.float32) / np.float32(64))).astype(np.float32)
    c["freq_bc"] = np.ascontiguousarray(np.broadcast_to(freq[None], (128, 64))).astype(np.float32)
    return c


CONST_SPECS = [("ident_bf", [128, 128], BF16), ("tri_incl", [128, 128], F32), ("gstrict", [128, 128], F32),
               ("ones_f", [128, 128], F32), ("mask01_f", [128, 128], F32), ("mask01_bf", [128, 128], BF16),
               ("dintraT", [128, 4, 128], F32), ("dq", [128, 4, 128], F32), ("dk", [128, 4], F32),
               ("freq_bc", [128, 64], F32)]


class _Stop(Exception):
    pass


def build_nc(L=SEQ, depth=DEPTH, debug=False, stop=None):
    NT = L // 128
    NST = L // 512
    nc = bass.Bass("TRN2", target_bir_lowering=False)
    dchunk = host_consts()["_dchunk"]

    def din(name, shape, dt=F32):
        return nc.dram_tensor(name, shape, dt, kind="ExternalInput").ap()

    dbgkind = "ExternalOutput" if debug else "Internal"

    def dscr(name, shape, dt, dbg=False):
        return nc.dram_tensor(name, shape, dt, kind=(dbgkind if dbg else "Internal")).ap()

    x_in = din("x", [L, D])
    c_t = din("c_t", [128, 8])
    pos_t = din("pos_t", [128, NT], I32)
    norm_g = din("norm_g", [depth, D])
    w_ada = din("w_ada", [depth, D, 3 * D])
    b_ada = din("b_ada", [depth, 3 * D])
    w_in = din("w_in", [depth, D, NIN])
    cw_t = din("cw_t", [depth, 128, 12, 4])
    cb_t = din("cb_t", [depth, 128, 12])
    dt_bias = din("dt_bias", [depth, 16])
    a_log = din("a_log", [depth, 16])
    d_skip = din("d_skip", [depth, 16])
    ssd_g = din("ssd_norm_g", [depth, D])
    b_forget = din("b_forget", [depth, 8])
    w_out = din("w_out", [depth, 2 * D, D])
    final_g = din("final_g", [1, D])
    cin = {n: din(n, s, dt) for (n, s, dt) in CONST_SPECS}
    y_out = nc.dram_tensor("y", [L, D], F32, kind="ExternalOutput").ap()

    wb_in = dscr("wb_in", [depth, 13, 128, 8, 512], BF16)
    wb_sm = dscr("wb_sm", [depth, 128, 8, 24], BF16)
    wb_out = dscr("wb_out", [depth, 128, 12, D], BF16)
    wb_outF = dscr("wb_outF", [depth, 64, 8, D], BF16)
    cs_tab = dscr("cs_tab", [L, 128], F32, dbg=True)
    modtab = dscr("modtab", [depth, 128, 3 * D], F32, dbg=True)
    mixA = dscr("mixA", [L, 1536], BF16, dbg=True)
    QTs = dscr("QTs", [8, 64, L], BF16, dbg=True)
    KTs = dscr("KTs", [8, 64, L], BF16, dbg=True)
    Vs = dscr("Vs", [8, 128, NT, 72], BF16, dbg=True)
    GTs = dscr("GTs", [8, 64, L], BF16, dbg=True)
    FQs = dscr("FQs", [2, 8, L], BF16, dbg=True)
    xs_buf = dscr("xs_buf", [L, D], F32, dbg=True)
    dbgF = dscr("dbgF", [128, NT, 8], F32, dbg=True) if debug else None

    P = Prog(nc)
    out_toks = []

    with contextlib.ExitStack() as es:
        ARENA_BYTES = 207 * 1024
        arena_t = es.enter_context(nc.sbuf_tensor("arena", [128, ARENA_BYTES], U8))
        A = Arena(arena_t, ARENA_BYTES)
        banks = [es.enter_context(nc.psum_tensor("bank%d" % i, [128, 512], F32)) for i in range(8)]
        bT = TL(8, "bank")

        def bankf(i):
            return banks[i][:]

        def bankb(i):
            return banks[i][:].bitcast(BF16)

        def mm(out, lhsT, rhs, start, stop, R, W):
            P.op("pe", lambda e: e.matmul(out, lhsT, rhs, start=start, stop=stop, skip_group_check=True),
                 reads=R, writes=W)

        def tr(out, in_, R, W):
            P.op("pe", lambda e: e.transpose(out, in_, ident), reads=list(R) + [tIdent], writes=W)

        def act(out, in_, func, R, W, bias=None, scale=None, accum=None):
            kw = {}
            if bias is not None:
                kw["bias"] = bias
            if scale is not None:
                kw["scale"] = scale
            if accum is not None:
                kw["accum_out"] = accum
            P.op("act", lambda e: e.activation(out, in_, func, **kw), reads=R, writes=W)

        def tt(eng, out, in0, in1, op, R, W):
            P.op(eng, lambda e: e.tensor_tensor(out, in0, in1, op=op), reads=R, writes=W)

        def ts(eng, out, in0, s1, s2, op0, op1, R, W):
            if s2 is None:
                P.op(eng, lambda e: e.tensor_scalar(out, in0, s1, None, op0=op0), reads=R, writes=W)
            else:
                P.op(eng, lambda e: e.tensor_scalar(out, in0, s1, s2, op0=op0, op1=op1), reads=R, writes=W)

        def stt(eng, out, in0, scalar, in1, op0, op1, R, W):
            P.op(eng, lambda e: e.scalar_tensor_tensor(out, in0, scalar, in1, op0=op0, op1=op1), reads=R, writes=W)

        def cp(eng, out, in_, R, W):
            if eng == "act":
                P.op("act", lambda e: e.copy(out, in_), reads=R, writes=W)
            else:
                P.op(eng, lambda e: e.tensor_copy(out, in_), reads=R, writes=W)

        def memset(eng, out, val, W):
            P.op(eng, lambda e: e.memset(out, val), writes=W)

        def rsqrt_mean(out, ss, n, R, W):
            act(out, ss, AF.Ln, R, W, bias=epsb[0:out.shape[0], :], scale=1.0 / n)
            act(out, out, AF.Exp, W, W, scale=-0.5)

        csb = {}
        tConst = T("consts")
        for (n, s, dt) in CONST_SPECS:
            parts = s[0]
            v = A.alloc(s[1:], dt, parts=parts)
            csb[n] = v
            P.dma("sp", v, cin[n], writes=[tConst])
        ident = csb["ident_bf"]
        tIdent = tConst
        tri = csb["tri_incl"]
        gst = csb["gstrict"]
        ones_f = csb["ones_f"]
        epsb = A.alloc([1], F32)
        memset("pool", epsb, EPS, [tConst])

        F_all = A.alloc([NT, 8], F32)
        tF = T("F_all")
        carry_hist = A.alloc([NST, 8], F32)
        tCH = T("carry_hist")
        modv = A.alloc([3 * D], F32)
        tMod = T("modv")
        ssdg_bc = A.alloc([D], F32)
        dskip_bc = A.alloc([16], F32)
        dtb_bc = A.alloc([16], F32)
        A_bc = A.alloc([16], F32)
        bf_bc = A.alloc([8], F32)
        cw_sb = A.alloc([12, 4], F32)
        cb_sb = A.alloc([12], F32)
        tLay = T("layer_consts")
        tAl = T("a_log")
        base_mark = A.mark()

        m0 = A.mark()
        stg_f = [A.alloc([NIN], F32) for _ in range(2)]
        stg_b = [A.alloc([NIN], BF16) for _ in range(2)]
        tSf = TL(2, "stgf")
        tSb = TL(2, "stgb")
        tSb3 = [TL(3, "stgb3_%d" % i) for i in range(2)]
        it = 0
        conv_engs = ["dve", "act", "pool"]
        for l in range(depth):
            for r in range(8):
                s = it % 2
                P.dma("sp", stg_f[s], w_in[l, r * 128:(r + 1) * 128, :], writes=[tSf[s]])
                thirds = [(0, 2048), (2048, 4624), (4624, NIN)]
                for ei, (c0, c1) in enumerate(thirds):
                    cp(conv_engs[ei], stg_b[s][:, c0:c1], stg_f[s][:, c0:c1], [tSf[s]], [tSb3[s][ei]])
                rds = tSb3[s]
                for g, c0 in enumerate(WG_COLS):
                    P.dma("pool", wb_in[l, g, :, r, :], stg_b[s][:, c0:c0 + 512], reads=rds)
                P.dma("pool", wb_sm[l, :, r, 0:16], stg_b[s][:, C_DT:C_DT + 16], reads=rds)
                P.dma("pool", wb_sm[l, :, r, 16:24], stg_b[s][:, C_FR:C_FR + 8], reads=rds)
                it += 1
            for r in range(16):
                s = it % 2
                P.dma("sp", stg_f[s][:, 0:D], w_out[l, r * 128:(r + 1) * 128, :], writes=[tSf[s]])
                cp(conv_engs[it % 3], stg_b[s][:, 0:D], stg_f[s][:, 0:D], [tSf[s]], [tSb[s]] + tSb3[s])
                if r < 12:
                    P.dma("pool", wb_out[l, :, r, :], stg_b[s][:, 0:D], reads=[tSb[s]])
                else:
                    i2 = r - 12
                    P.dma("pool", wb_outF[l, :, 2 * i2, :], stg_b[s][0:64, 0:D], reads=[tSb[s]])
                    P.dma("pool", wb_outF[l, :, 2 * i2 + 1, :], stg_b[s][64:128, 0:D], reads=[tSb[s]])
                it += 1
        A.reset(m0)
        P.barrier()
        def _phases():
            if stop == "wconv":
                raise _Stop()
            m0 = A.mark()
            pos_i = A.alloc([NT], I32)
            pos_f = A.alloc([NT], F32)
            tPos = T("pos")
            P.dma("sp", pos_i, pos_t, writes=[tPos])
            cp("dve", pos_f, pos_i, [tPos], [tPos])
            GT = 8
            angs = [A.alloc([GT, 64], F32) for _ in range(2)]
            kf = A.alloc([GT, 64], F32)
            ki = A.alloc([GT, 64], I32)
            cstile = [A.alloc([GT, 128], F32) for _ in range(2)]
            tAng = T("ang")
            tCs = TL(2, "cstile")
            C1 = 6.28125
            C2 = 2.0 * math.pi - C1
            for gi, t0 in enumerate(range(0, NT, GT)):
                n = min(GT, NT - t0)
                s = gi % 2
                for which in range(2):
                    ang = angs[which]
                    tt("dve", ang[:, 0:n, :], bc0(csb["freq_bc"], n), bc1(pos_f[:, t0:t0 + n], 64), ALU.mult, [tPos, tConst], [tAng])
                    if which == 0:
                        ts("dve", ang[:, 0:n, :], ang[:, 0:n, :], math.pi / 2, None, ALU.add, None, [tAng], [tAng])
                    ts("dve", kf[:, 0:n, :], ang[:, 0:n, :], 1.0 / (2 * math.pi), None, ALU.mult, None, [tAng], [tAng])
                    cp("dve", ki[:, 0:n, :], kf[:, 0:n, :], [tAng], [tAng])
                    cp("dve", kf[:, 0:n, :], ki[:, 0:n, :], [tAng], [tAng])
                    stt("dve", ang[:, 0:n, :], kf[:, 0:n, :], -C1, ang[:, 0:n, :], ALU.mult, ALU.add, [tAng], [tAng])
                    stt("dve", ang[:, 0:n, :], kf[:, 0:n, :], -C2, ang[:, 0:n, :], ALU.mult, ALU.add, [tAng], [tAng])
                    ts("dve", ang[:, 0:n, :], ang[:, 0:n, :], 3.1415925, -3.1415925, ALU.min, ALU.max, [tAng], [tAng])
                    act(cstile[s][:, 0:n, which * 64:(which + 1) * 64], ang[:, 0:n, :], AF.Sin, [tAng], [tCs[s]])
                P.dma("pool", cs_tab.rearrange("(t p) c -> p t c", p=128)[:, t0:t0 + n, :], cstile[s][:, 0:n, :], reads=[tCs[s]])
            A.reset(m0)

            P.barrier()
            if stop == "cs":
                raise _Stop()
            m0 = A.mark()
            c_sb = A.alloc([8], F32)
            csil = A.alloc([8], F32)
            crep = A.alloc([8, 128], F32)
            tC = T("c")
            P.dma("sp", c_sb, c_t, writes=[tC])
            act(csil, c_sb, AF.Silu, [tC], [tC])
            cp("dve", crep, bc1(csil, 128), [tC], [tC])
            wada_sb = [A.alloc([8, 512], F32) for _ in range(2)]
            tWa = TL(2, "wada")
            bada_bc = A.alloc([3 * D], F32)
            ng_bc = A.alloc([D], F32)
            modst = A.alloc([3 * D], F32)
            tBa = T("bada")
            tNg = T("ng")
            tMs = T("modst")
            it = 0
            for l in range(depth):
                P.dma("sp", bada_bc, b_ada[l:l + 1, :].partition_broadcast(128), writes=[tBa])
                P.dma("sp", ng_bc, norm_g[l:l + 1, :].partition_broadcast(128), writes=[tNg])
                for cg in range(6):
                    s = it % 2
                    it += 1
                    P.dma("sp", wada_sb[s], w_ada[l].rearrange("(k p) c -> p k c", p=128)[:, :, cg * 512:(cg + 1) * 512], writes=[tWa[s]])
                    bk = cg % 2
                    for k in range(8):
                        mm(bankf(bk), crep[:, k, :], wada_sb[s][:, k, :], k == 0, k == 7, [tC, tWa[s]], [bT[bk]])
                    tt("dve", modst[:, cg * 512:(cg + 1) * 512], bankf(bk), bada_bc[:, cg * 512:(cg + 1) * 512], ALU.add, [tBa], [bT[bk], tMs])
                stt("dve", modst[:, D:2 * D], modst[:, D:2 * D], 1.0, ng_bc, ALU.add, ALU.mult, [tNg], [tMs])
                P.dma("pool", modtab[l], modst, reads=[tMs])
            A.reset(m0)
            P.barrier()

            if stop == "ada":
                raise _Stop()
            for l in range(depth):
                last = (l == depth - 1)
                x_src = x_in if l == 0 else xs_buf
                P.dma("sp", modv, modtab[l], writes=[tMod])
                P.dma("sp", ssdg_bc, ssd_g[l:l + 1, :].partition_broadcast(128), writes=[tLay])
                P.dma("sp", dskip_bc, d_skip[l:l + 1, :].partition_broadcast(128), writes=[tLay])
                P.dma("sp", dtb_bc, dt_bias[l:l + 1, :].partition_broadcast(128), writes=[tLay])
                P.dma("sp", A_bc, a_log[l:l + 1, :].partition_broadcast(128), writes=[tAl])
                P.dma("sp", bf_bc, b_forget[l:l + 1, :].partition_broadcast(128), writes=[tLay])
                P.dma("sp", cw_sb, cw_t[l], writes=[tLay])
                P.dma("sp", cb_sb, cb_t[l], writes=[tLay])
                act(A_bc, A_bc, AF.Exp, [tAl], [tAl])
                ts("dve", A_bc, A_bc, -1.0, None, ALU.mult, None, [tAl], [tAl])
                P.barrier()
                shift_bc = modv[:, 0:D]
                gs_bc = modv[:, D:2 * D]
                gate_bc = modv[:, 2 * D:3 * D]

                m1 = A.mark()
                wbuf = [A.alloc([8, 512], BF16) for _ in range(2)]
                tW = TL(2, "wbuf")
                wsm = A.alloc([8, 24], BF16)
                tWs = T("wsm")
                xt = [A.alloc([D], F32) for _ in range(2)]
                tX = TL(2, "xt")
                hn = [A.alloc([D], BF16)] * 2
                tHn = [T("hn")] * 2
                htmp = A.alloc([D], F32)
                tHt = T("htmp")
                junk = A.alloc([D], BF16)
                tJunk = T("junk")
                hT = A.alloc([8, 512], BF16)
                tHT = TL(4, "hT")
                dbufs = []
                for _i in range(2):
                    dbufs.append(dict(
                        rq=A.alloc([4, 512], BF16), rk=A.alloc([4, 512], BF16), rv=A.alloc([4, 512], BF16),
                        srg=A.alloc([4, 512], BF16), sz=A.alloc([4, D], BF16), dtf=A.alloc([4, 24], F32),
                        tRq=TL(4, "rq"), tRk=TL(4, "rk"), tRv=TL(4, "rv"), tRg=TL(4, "rg"), tSz=TL(4, "sz"), tDtf=TL(4, "dtf")))
                fvp = A.alloc([8, 4, 72], BF16)
                tFv = TL(4, "fvp")
                xbcT = A.alloc([12, 512], BF16)
                tXbc = TL(12, "xbcT")
                ubuf = [A.alloc([515], F32) for _ in range(2)]
                tU = TL(2, "ubuf")
                tUh = TL(2, "ubufh")
                cacc = [A.alloc([512], F32)] * 2
                tCa = [T("cacc")] * 2
                hist = A.alloc([12, 3], F32)
                tHist = TL(12, "hist")
                qkst = [A.alloc([512], BF16) for _ in range(2)]
                qkst.append(qkst[0])
                tQk = TL(2, "qkst")
                tQk.append(tQk[0])
                cst = [A.alloc([128], F32) for _ in range(2)]
                tCst = TL(2, "cst")
                ssx = A.alloc([8], F32)
                tSsx = T("ssx")
                rt = [A.alloc([4, 64], F32) for _ in range(4)]
                tRt = TL(4, "rt")
                qrot = A.alloc([4, 128], BF16)
                krot = A.alloc([4, 128], BF16)
                tQrot, tKrot = T("qrot"), T("krot")
                qT = A.alloc([4, 128], BF16)
                kT = A.alloc([4, 128], BF16)
                qdT = A.alloc([4, 128], BF16)
                kd = A.alloc([4, 128], BF16)
                tQT, tKT, tQdT, tKd = T("qT"), T("kT"), T("qdT"), T("kd")
                smT = A.alloc([4, 128], BF16)
                tSmT = T("smT")
                Sr = A.alloc([4, 128], F32)
                Sr_bf = A.alloc([4, 128], BF16)
                tSr, tSrb = T("Sr"), T("Srb")
                rss = A.alloc([8], F32)
                tRss = T("rss")
                otmp = A.alloc([4, 128], F32)
                tOt = T("otmp")
                mixst = [A.alloc([1536], BF16) for _ in range(2)]
                tMixR = TL(2, "mixstR")
                tMixS = TL(2, "mixstS")
                dtv = A.alloc([16], F32)
                av = A.alloc([16], F32)
                tDt = T("dtv")
                acum = A.alloc([16], F32)
                eacum = A.alloc([16], F32)
                elast = A.alloc([16], F32)
                decj = A.alloc([16], F32)
                tAc = T("acum")
                tDecj = T("decj")
                rhs1 = A.alloc([16, 128], F32)
                tRhs1 = TL(2, "rhs1")
                LT = A.alloc([16, 128], BF16)
                tLT = TL(2, "LT")
                MT = LT
                tMT = tLT
                cbm = A.alloc([2, 128], BF16)
                tCbm = T("cbm")
                xs_tm = A.alloc([D], BF16)
                xdt = A.alloc([D], BF16)
                xdtd = A.alloc([D], BF16)
                tXs, tXdt, tXdtd = T("xs_tm"), T("xdt"), T("xdtd")
                B_tm = A.alloc([2, 128], BF16)
                tBtm = T("B_tm")
                Ss = A.alloc([2, 512], F32)
                Ss_bf = A.alloc([2, 512], BF16)
                tSs, tSsb = T("Ss"), T("Ssb")
                y1 = A.alloc([D], F32)
                y2 = A.alloc([D], F32)
                tY1, tY2 = T("y1"), T("y2")
                sss = A.alloc([2], F32)
                tSss = T("sss")
                fr = A.alloc([8], F32)
                tFr = T("fr")
                carry_bc = A.alloc([8], F32)
                tCarry = T("carry")
                relc = A.alloc([1], F32, parts=8)
                tRelc = T("relc")
                fpt = A.alloc([128], F32, parts=8)
                fpl = A.alloc([128], F32, parts=8)
                tFpt = T("fpt")
                fqhi = [A.alloc([512], BF16, parts=8) for _ in range(2)]
                fqlo = [A.alloc([512], BF16, parts=8) for _ in range(2)]
                tFq = TL(2, "fq")
                if l == 0:
                    print("pass1 arena used", A.off, "base", base_mark)

                memset("pool", Sr, 0.0, [tSr])
                memset("pool", Sr_bf, 0.0, [tSrb])
                memset("pool", Ss, 0.0, [tSs])
                memset("pool", Ss_bf, 0.0, [tSsb])
                memset("pool", hist, 0.0, tHist)
                memset("pool", carry_bc, 0.0, [tCarry])
                memset("pool", fvp, 1.0, tFv)

                B_ACC = [0, 1, 2]
                B_T = 4
                B_SM = 3
                B_W = [4, 5, 6, 7]

                wgi = [0]

                def load_wgroup(c0, ncols):
                    s = wgi[0] % 2
                    wgi[0] += 1
                    P.dma("sp", wbuf[s], wb_in[l, WG_COLS.index(c0)], writes=[tW[s]])
                    return s

                acci = [0]

                def next_acc():
                    b = B_ACC[acci[0] % 3]
                    acci[0] += 1
                    return b

                def gen_AB(st):
                    _d = dbufs[st % 2]
                    rq, rk, rv, srg, sz, dtf = _d["rq"], _d["rk"], _d["rv"], _d["srg"], _d["sz"], _d["dtf"]
                    tRq, tRk, tRv, tRg, tSz, tDtf = _d["tRq"], _d["tRk"], _d["tRv"], _d["tRg"], _d["tSz"], _d["tDtf"]
                    for j in range(4):
                        t = st * 4 + j
                        s = t % 2
                        P.dma("sp", xt[s], x_src[t * 128:(t + 1) * 128, :], writes=[tX[s]])
                        act(junk, xt[s], AF.Square, [tX[s]], [tJunk, tSsx], accum=ssx[:, 0:1])
                        rsqrt_mean(ssx[:, 1:2], ssx[:, 0:1], D, [tSsx], [tSsx])
                        stt("dve", htmp, xt[s], ssx[:, 1:2], gs_bc, ALU.mult, ALU.mult, [tX[s], tSsx, tMod], [tHt])
                        tt("dve", hn[s], htmp, shift_bc, ALU.add, [tHt, tMod], [tHn[s]])
                        bA = next_acc()
                        for k in range(8):
                            tr(bankb(bA)[:, k * 128:(k + 1) * 128], hn[s][:, k * 128:(k + 1) * 128], [tHn[s]], [bT[bA]])
                        cp("act" if j % 2 == 0 else "dve", hT[:, :, j * 128:(j + 1) * 128],
                           bankb(bA).rearrange("p (k c) -> p k c", k=8), [], [bT[bA], tHT[j]])
                        yield

                    pend = []

                    def flush():
                        while pend:
                            pend.pop(0)()

                    def tm_group(c0, evac):
                        s = load_wgroup(c0, 512)
                        for j in range(4):
                            b = next_acc()
                            for k in range(8):
                                mm(bankf(b), hT[:, k, j * 128:(j + 1) * 128], wbuf[s][:, k, :], k == 0, k == 7, [tHT[j], tW[s]], [bT[b]])
                            if pend:
                                pend.pop(0)()
                            pend.append(lambda j=j, b=b: evac(j, b))
                            yield

                    def fm_group(c0, evac):
                        s = load_wgroup(c0, 512)
                        for cc in range(4):
                            b = next_acc()
                            for k in range(8):
                                mm(bankf(b), wbuf[s][:, k, cc * 128:(cc + 1) * 128], hT[:, k, :], k == 0, k == 7, tHT + [tW[s]], [bT[b]])
                            if pend:
                                pend.pop(0)()
                            pend.append(lambda cc=cc, b=b: evac(cc, b))
                            yield

                    def qk_evac(dst, scale):
                        def f(cc, b):
                            s = acci[0] % 3
                            if scale is None:
                                cp("dve", qkst[s], bankf(b), [], [bT[b], tQk[s]])
                            else:
                                act(qkst[s], bankf(b), AF.Copy, [], [bT[b], tQk[s]], scale=scale)
                            P.dma("pool", dst.rearrange("h d l -> (h d) l")[cc * 128:(cc + 1) * 128, st * 512:(st + 1) * 512], qkst[s], reads=[tQk[s]])
                        return f
                    yield from fm_group(C_FQ, qk_evac(QTs, 0.125))
                    yield from fm_group(C_FK, qk_evac(KTs, None))
                    yield from tm_group(C_FV, lambda j, b: cp("dve", fvp[:, :, j, 0:64], bankf(b).rearrange("p (h d) -> p h d", h=8), [], [bT[b], tFv[j]]))
                    flush()
                    for h8 in range(8):
                        P.dma("pool", Vs[h8, :, st * 4:(st + 1) * 4, :], fvp[:, h8, :, :], reads=tFv)

                    def g_evac(cc, b):
                        s3 = acci[0] % 3
                        act(qkst[s3], bankf(b), AF.Silu, [], [bT[b], tQk[s3]])
                        P.dma("pool", GTs.rearrange("h d l -> (h d) l")[cc * 128:(cc + 1) * 128, st * 512:(st + 1) * 512], qkst[s3], reads=[tQk[s3]])
                    yield from fm_group(C_FG, g_evac)

                    yield from tm_group(C_RQ, lambda j, b: cp("dve", rq[:, j, :], bankf(b), [], [bT[b], tRq[j]]))
                    yield from tm_group(C_RK, lambda j, b: act(rk[:, j, :], bankf(b), AF.Copy, [], [bT[b], tRk[j]], scale=128.0 ** -0.5))
                    yield from tm_group(C_RV, lambda j, b: cp("dve", rv[:, j, :], bankf(b), [], [bT[b], tRv[j]]))
                    yield from tm_group(C_RG, lambda j, b: act(srg[:, j, :], bankf(b), AF.Silu, [], [bT[b], tRg[j]]))
                    flush()
                    if st == 0:
                        P.dma("sp", wsm, wb_sm[l], writes=[tWs])
                    for j in range(4):
                        b = next_acc()
                        for k in range(8):
                            mm(bankf(b)[:, 0:24], hT[:, k, j * 128:(j + 1) * 128], wsm[:, k, :], k == 0, k == 7, [tHT[j], tWs], [bT[b]])
                        cp("dve", dtf[:, j, :], bankf(b)[:, 0:24], [], [bT[b], tDtf[j]])
                    for half in range(2):
                        yield from tm_group(C_Z + half * 512,
                                 lambda j, b, half=half: act(sz[:, j, half * 512:(half + 1) * 512], bankf(b), AF.Silu, [], [bT[b], tSz[j]]))

                    def conv_evac(g):
                        def f(cc, b):
                            ch = g * 4 + cc
                            s = ch % 2
                            cp("act", ubuf[s][:, 3:515], bankf(b), [], [bT[b], tU[s]])
                            cp("pool", ubuf[s][:, 0:3], hist[:, ch, :], [tHist[ch]], [tUh[s]])
                            ts("dve", cacc[s], ubuf[s][:, 3:515], cw_sb[:, ch, 3:4], cb_sb[:, ch:ch + 1], ALU.mult, ALU.add, [tU[s], tLay], [tCa[s]])
                            for kk in range(3):
                                stt("dve", cacc[s], ubuf[s][:, kk:kk + 512], cw_sb[:, ch, kk:kk + 1], cacc[s], ALU.mult, ALU.add, [tU[s], tUh[s], tLay], [tCa[s]])
                            cp("pool", hist[:, ch, :], ubuf[s][:, 512:515], [tU[s]], [tHist[ch]])
                            act(xbcT[:, ch, :], cacc[s], AF.Silu, [tCa[s]], [tXbc[ch]])
                        return f
                    for g in range(3):
                        yield from fm_group(C_XBC + g * 512, conv_evac(g))

                    flush()

                def gen_C(st):
                    _d = dbufs[st % 2]
                    rq, rk, rv, srg, sz, dtf = _d["rq"], _d["rk"], _d["rv"], _d["srg"], _d["sz"], _d["dtf"]
                    tRq, tRk, tRv, tRg, tSz, tDtf = _d["tRq"], _d["tRk"], _d["tRv"], _d["tRg"], _d["tSz"], _d["tDtf"]
                    for j in range(4):
                        t = st * 4 + j
                        ms = t % 2
                        cs_ = cst[t % 2]
                        P.dma("sp", cs_, cs_tab[t * 128:(t + 1) * 128, :], writes=[tCst[t % 2]])
                        cosb = bc0(cs_[:, 0:64], 4)
                        sinb = bc0(cs_[:, 64:128], 4)
                        for (src, tsrc, dst, tdst, eng) in ((rq, tRq, qrot, tQrot, "pool"), (rk, tRk, krot, tKrot, "dve")):
                            v = src[:, j, :].rearrange("p (h two d) -> p h two d", h=4, two=2)
                            o = dst.rearrange("p h (two d) -> p h two d", two=2)
                            R = [tsrc[j], tCst[t % 2]]
                            tt(eng, rt[0], v[:, :, 0, :], cosb, ALU.mult, R, [tRt[0]])
                            tt(eng, rt[1], v[:, :, 1, :], sinb, ALU.mult, R, [tRt[1]])
                            tt(eng, o[:, :, 0, :], rt[0], rt[1], ALU.subtract, [tRt[0], tRt[1]], [tdst])
                            tt(eng, rt[2], v[:, :, 0, :], sinb, ALU.mult, R, [tRt[2]])
                            tt(eng, rt[3], v[:, :, 1, :], cosb, ALU.mult, R, [tRt[3]])
                            tt(eng, o[:, :, 1, :], rt[2], rt[3], ALU.add, [tRt[2], tRt[3]], [tdst])
                        tt("dve", kd, krot, bc1(csb["dk"], 128), ALU.mult, [tKrot, tConst], [tKd])
                        yield
                        bq = B_W[0]
                        for h in range(4):
                            tr(bankb(bq)[:, h * 128:(h + 1) * 128], qrot[:, h, :], [tQrot], [bT[bq]])
                        for h in range(4):
                            tr(bankb(bq)[:, 512 + h * 128:512 + (h + 1) * 128], krot[:, h, :], [tKrot], [bT[bq]])
                        cp("act", qT, bankb(bq)[:, 0:512].rearrange("p (h c) -> p h c", h=4), [], [bT[bq], tQT])
                        tt("dve", qdT, bankb(bq)[:, 0:512].rearrange("p (h c) -> p h c", h=4), csb["dq"], ALU.mult, [tConst], [bT[bq], tQdT])
                        cp("act", kT, bankb(bq)[:, 512:1024].rearrange("p (h c) -> p h c", h=4), [], [bT[bq], tKT])
                        yield
                        bs_ = B_W[1]
                        for h in range(4):
                            mm(bankf(bs_)[:, h * 128:(h + 1) * 128], kT[:, h, :], qT[:, h, :], h == 0, h == 3, [tKT, tQT], [bT[bs_]])
                        tt("dve", smT, bankf(bs_).rearrange("p (h c) -> p h c", h=4), csb["dintraT"], ALU.mult, [tConst], [bT[bs_], tSmT])
                        yield
                        bo = B_W[2]
                        for h in range(4):
                            mm(bankf(bo)[:, h * 128:(h + 1) * 128], smT[:, h, :], rv[:, j, h * 128:(h + 1) * 128], h == 0, False, [tSmT, tRv[j]], [bT[bo]])
                            mm(bankf(bo)[:, h * 128:(h + 1) * 128], qdT[:, h, :], Sr_bf[:, h, :], False, h == 3, [tQdT, tSrb], [bT[bo]])
                        bn = B_W[3]
                        for h in range(4):
                            mm(bankf(bn)[:, h * 128:(h + 1) * 128], kd[:, h, :], rv[:, j, h * 128:(h + 1) * 128], h == 0, h == 3, [tKd, tRv[j]], [bT[bn]])
                        for h in range(4):
                            stt("dve", Sr[:, h, :], Sr[:, h, :], dchunk[h], bankf(bn)[:, h * 128:(h + 1) * 128], ALU.mult, ALU.add, [], [bT[bn], tSr])
                        cp("act", Sr_bf, Sr, [tSr], [tSrb])
                        yield
                        for h in range(4):
                            act(junk[:, 0:128], bankf(bo)[:, h * 128:(h + 1) * 128], AF.Square, [], [bT[bo], tJunk, tRss], accum=rss[:, h:h + 1])
                        rsqrt_mean(rss[:, 4:8], rss[:, 0:4], 128, [tRss], [tRss])
                        tt("dve", otmp, bankf(bo).rearrange("p (h c) -> p h c", h=4), bc1(rss[:, 4:8], 128), ALU.mult, [tRss], [bT[bo], tOt])
                        tt("pool", mixst[ms][:, 0:512], otmp.rearrange("p h c -> p (h c)"), srg[:, j, :], ALU.mult, [tOt, tRg[j]], [tMixR[ms]])
                        yield

                        tsl = slice(j * 128, (j + 1) * 128)
                        tt("dve", dtv, dtf[:, j, 0:16], dtb_bc, ALU.add, [tDtf[j], tLay], [tDt])
                        act(dtv, dtv, AF.Exp, [tDt], [tDt])
                        act(dtv, dtv, AF.Ln, [tDt], [tDt], bias=1.0)
                        tt("dve", av, dtv, A_bc, ALU.mult, [tDt, tLay], [tDt])
                        bsm = B_SM
                        mm(bankf(bsm)[:, 0:16], tri, av, True, False, [tConst, tDt], [bT[bsm]])
                        mm(bankf(bsm)[:, 16:32], ones_f, av, False, True, [tConst, tDt], [bT[bsm]])
                        act(eacum, bankf(bsm)[:, 0:16], AF.Exp, [], [bT[bsm], tAc])
                        act(elast, bankf(bsm)[:, 16:32], AF.Exp, [], [bT[bsm], tAc])
                        tt("dve", rhs1[:, 0:8, :], bc0(tri, 8), bc1(av[:, 0:8], 128), ALU.mult, [tConst, tDt], [tRhs1[0]])
                        tt("pool", rhs1[:, 8:16, :], bc0(tri, 8), bc1(av[:, 8:16], 128), ALU.mult, [tConst, tDt], [tRhs1[1]])
                        yield
                        for q4 in range(4):
                            b = B_W[q4]
                            mm(bankf(b), gst, rhs1[:, q4 * 4:(q4 + 1) * 4, :].rearrange("p h c -> p (h c)"), True, True, [tConst, tRhs1[q4 // 2]], [bT[b]])
                            act(LT[:, q4 * 4:(q4 + 1) * 4, :].rearrange("p h c -> p (h c)"), bankf(b), AF.Exp, [], [bT[b], tLT[q4 // 2]])
                            cp("dve", decj[:, q4 * 4:(q4 + 1) * 4], bankf(b).rearrange("p (h c) -> p h c", h=4)[:, :, 127], [], [bT[b], tDecj])
                        act(decj, decj, AF.Exp, [tDecj], [tDecj])
                        yield
                        for g in range(2):
                            mm(bankf(bsm)[:, 128 + g * 128:128 + (g + 1) * 128], xbcT[:, 8 + g, tsl], xbcT[:, 10 + g, tsl], False, g == 1,
                               [tXbc[8 + g], tXbc[10 + g]], [bT[bsm]])
                        tt("dve", cbm, bankf(bsm)[:, 128:384].rearrange("p (g c) -> p g c", g=2), bc0(csb["mask01_f"], 2), ALU.mult, [tConst], [bT[bsm], tCbm])
                        for g in range(2):
                            tt("dve" if g == 0 else "pool", MT[:, g * 8:(g + 1) * 8, :], LT[:, g * 8:(g + 1) * 8, :], bc0(cbm[:, g, :], 8), ALU.mult, [tLT[g], tCbm], [tMT[g]])
                        for c8 in range(8):
                            tr(bankb(B_T)[:, c8 * 128:(c8 + 1) * 128], xbcT[:, c8, tsl], [tXbc[c8]], [bT[B_T]])
                        cp("act", xs_tm, bankb(B_T), [], [bT[B_T], tXs])
                        tt("dve", xdt.rearrange("p (h d) -> p h d", h=16), bankb(B_T).rearrange("p (h d) -> p h d", h=16), bc1(dtv, 64), ALU.mult, [tDt], [bT[B_T], tXdt])
                        for g in range(2):
                            tr(bankb(B_T)[:, g * 128:(g + 1) * 128], xbcT[:, 8 + g, tsl], [tXbc[8 + g]], [bT[B_T]])
                        cp("act", B_tm, bankb(B_T)[:, 0:256].rearrange("p (g c) -> p g c", g=2), [], [bT[B_T], tBtm])
                        yield
                        for h in range(16):
                            b = B_W[h // 8]
                            hh = h % 8
                            mm(bankf(b)[:, hh * 64:(hh + 1) * 64], MT[:, h, :], xdt[:, h * 64:(h + 1) * 64], hh == 0, hh == 7, [tMT[h // 8], tXdt], [bT[b]])
                        for g in range(2):
                            b = B_W[2 + g]
                            mm(bankf(b), xbcT[:, 10 + g, tsl], Ss_bf[:, g, :], True, True, [tXbc[10 + g], tSsb], [bT[b]])
                        for g in range(2):
                            b = B_W[2 + g]
                            tt("dve", y1[:, g * 512:(g + 1) * 512].rearrange("p (h d) -> p h d", h=8), bankf(b).rearrange("p (h d) -> p h d", h=8),
                               bc1(eacum[:, g * 8:(g + 1) * 8], 64), ALU.mult, [tAc], [bT[b], tY1])
                        for g in range(2):
                            b = B_W[g]
                            tt("dve", y1[:, g * 512:(g + 1) * 512], y1[:, g * 512:(g + 1) * 512], bankf(b), ALU.add, [], [bT[b], tY1])
                        tt("pool", y2.rearrange("p (h d) -> p h d", h=16), xs_tm.rearrange("p (h d) -> p h d", h=16), bc1(dskip_bc, 64), ALU.mult, [tXs, tLay], [tY2])
                        tt("pool", y2, y2, y1, ALU.add, [tY1], [tY2])
                        tt("pool", y2, y2, sz[:, j, :], ALU.mult, [tSz[j]], [tY2])
                        yield
                        act(junk, y2, AF.Square, [tY2], [tJunk, tSss], accum=sss[:, 0:1])
                        rsqrt_mean(sss[:, 1:2], sss[:, 0:1], D, [tSss], [tSss])
                        stt("dve", mixst[ms][:, 512:1536], y2, sss[:, 1:2], ssdg_bc, ALU.mult, ALU.mult, [tY2, tSss, tLay], [tMixS[ms]])
                        tt("pool", xdtd.rearrange("p (h d) -> p h d", h=16), xdt.rearrange("p (h d) -> p h d", h=16), bc1(decj, 64), ALU.mult, [tXdt, tDecj], [tXdtd])
                        for g in range(2):
                            b = B_W[2 + g]
                            mm(bankf(b), B_tm[:, g, :], xdtd[:, g * 512:(g + 1) * 512], True, True, [tBtm, tXdtd], [bT[b]])
                        for g in range(2):
                            b = B_W[2 + g]
                            tt("pool", Ss[:, g, :].rearrange("p (h d) -> p h d", h=8), Ss[:, g, :].rearrange("p (h d) -> p h d", h=8),
                               bc1(elast[:, g * 8:(g + 1) * 8], 64), ALU.mult, [tAc], [tSs])
                            tt("dve", Ss[:, g, :], Ss[:, g, :], bankf(b), ALU.add, [], [bT[b], tSs])
                        cp("act", Ss_bf, Ss, [tSs], [tSsb])
                        yield
                        P.dma("pool", mixA[t * 128:(t + 1) * 128, :], mixst[ms], reads=[tMixR[ms], tMixS[ms]])

                        fs = st % 2
                        tt("dve", fr, dtf[:, j, 16:24], bf_bc, ALU.add, [tDtf[j], tLay], [tFr])
                        act(fr, fr, AF.Exp, [tFr], [tFr], scale=-1.0)
                        act(fr, fr, AF.Ln, [tFr], [tFr], bias=1.0)
                        ts("dve", fr, fr, -1.0, None, ALU.mult, None, [tFr], [tFr])
                        mm(bankf(bsm)[:, 0:8], tri, fr, True, False, [tConst, tFr], [bT[bsm]])
                        mm(bankf(bsm)[:, 8:16], ones_f, fr, False, False, [tConst, tFr], [bT[bsm]])
                        mm(bankf(bsm)[0:8, 16:144], fr, tri, False, True, [tConst, tFr], [bT[bsm]])
                        if j == 0:
                            cp("dve", carry_hist[:, st, :], carry_bc, [tCarry], [tCH])
                            memset("pool", relc, 0.0, [tRelc])
                        tt("dve", F_all[:, t, :], bankf(bsm)[:, 0:8], carry_bc, ALU.add, [tCarry], [bT[bsm], tF])
                        tt("dve", carry_bc, bankf(bsm)[:, 8:16], carry_bc, ALU.add, [], [bT[bsm], tCarry])
                        ts("dve", fpt, bankf(bsm)[0:8, 16:144], relc[:, 0:1], None, ALU.add, None, [tRelc], [bT[bsm], tFpt])
                        cp("dve", relc, fpt[:, 127:128], [tFpt], [tRelc])
                        cp("dve", fqhi[fs][:, tsl], fpt, [tFpt], [tFq[fs]])
                        tt("dve", fpl, fpt, fqhi[fs][:, tsl], ALU.subtract, [tFpt, tFq[fs]], [tFpt])
                        cp("dve", fqlo[fs][:, tsl], fpl, [tFpt], [tFq[fs]])
                        yield
                    fs = st % 2
                    P.dma("pool", FQs[0, :, st * 512:(st + 1) * 512], fqhi[fs], reads=[tFq[fs]])
                    P.dma("pool", FQs[1, :, st * 512:(st + 1) * 512], fqlo[fs], reads=[tFq[fs]])
                def drive(gens):
                    alive = list(gens)
                    while alive:
                        for g_ in list(alive):
                            try:
                                next(g_)
                            except StopIteration:
                                alive.remove(g_)

                for st in range(NST + 1):
                    gens = []
                    if st < NST:
                        gens.append(gen_AB(st))
                    if st >= 1:
                        gens.append(gen_C(st - 1))
                    drive(gens)
                if debug and l == depth - 1:
                    P.dma("pool", dbgF, F_all, reads=[tF])
                A.reset(m1)
                P.barrier()
                if stop == "pass1":
                    raise _Stop()

                m2 = A.mark()
                wout = A.alloc([12, D], BF16)
                woutF = A.alloc([8, D], BF16, parts=64)
                tWo = T("wout")
                tWoF = T("woutF")
                P.dma("sp", wout, wb_out[l], writes=[tWo])
                P.dma("sp", woutF, wb_outF[l], writes=[tWoF])
                ktb = [A.alloc([L], BF16) for _ in range(2)]
                tKtb = TL(2, "ktb")
                vb = [A.alloc([NT, 72], BF16) for _ in range(2)]
                tVb = TL(2, "vb")
                qtb = [A.alloc([8, 512], BF16) for _ in range(2)]
                tQtb = TL(2, "qtb")
                tQtbH = TL(2, "qtbH")
                tQtbL = TL(2, "qtbL")
                gtb = A.alloc([8, 512], BF16, parts=64)
                tGtb = T("gtb")
                mxa = [A.alloc([1536], BF16) for _ in range(2)]
                tMxa = TL(2, "mxa")
                x2 = [A.alloc([D], F32) for _ in range(2)]
                tX2 = TL(2, "x2")
                pT = [A.alloc([512], BF16) for _ in range(3)]
                tPT = TL(3, "pT")
                mixFT = A.alloc([8, 512], BF16, parts=64)
                tMixFT = T("mixFT")
                mT = A.alloc([12, 128], BF16)
                tMT2 = TL(2, "mT")
                biasq = A.alloc([NT, 8], F32)
                tBq = T("biasq")
                accS = [A.alloc([512], F32) for _ in range(2)]
                tAccS = TL(2, "accS")
                rden = A.alloc([512], F32, parts=64)
                tRden = T("rden")
                o1 = A.alloc([512], F32, parts=64)
                tO1 = T("o1")
                selden = A.alloc([64], F32)
                tSd = T("selden")
                xn = [A.alloc([D], F32) for _ in range(2)]
                tXn = TL(2, "xn")
                fss = A.alloc([2], F32)
                tFss = T("fss")
                fjunk = A.alloc([D], BF16)
                tFj = T("fjunk")
                fgb = A.alloc([D], F32)
                tFgb = T("fgb")
                if l == 0:
                    print("pass2 arena used", A.off)
                if last:
                    P.dma("sp", fgb, final_g.partition_broadcast(128), writes=[tFgb])
                for s in range(2):
                    memset("pool", ktb[s][64:128, :], 1.0, [tKtb[s]])
                memset("pool", selden, 0.0, [tSd])
                memset("pool", selden[64:65, :], 1.0, [tSd])

                BS = [0, 1, 2]
                BA = [3, 4]
                BX = 5
                BO = [6, 7]
                si = 0
                ai = 0
                pi_ = 0
                hi_ = 0
                for qt in range(NST):
                    nkt = 4 * qt + 4
                    qs = qt % 2
                    qcols = slice(qt * 512, (qt + 1) * 512)
                    P.dma("sp", qtb[qs][0:64, :, :], QTs.rearrange("h d l -> d h l")[:, :, qcols], writes=[tQtb[qs]])
                    P.dma("sp", qtb[qs][64:65, :, :], FQs[0:1, :, qcols], writes=[tQtbH[qs]])
                    P.dma("sp", qtb[qs][65:66, :, :], FQs[1:2, :, qcols], writes=[tQtbL[qs]])
                    P.dma("sp", gtb, GTs.rearrange("h d l -> d h l")[:, :, qcols], writes=[tGtb])
                    tt("dve", biasq[:, 0:nkt, :], bc0(carry_hist[:, qt, :], nkt), F_all[:, 0:nkt, :], ALU.subtract, [tCH, tF], [tBq])
                    LOOK = 2
                    its = []
                    for h in range(8):
                        ks = hi_ % 2
                        hi_ += 1
                        ba = BA[ai % 2]
                        as_ = ai % 2
                        ai += 1
                        for kt in range(nkt):
                            its.append((h, kt, ks, ba, as_))
                    state = {}

                    def emit_qk(idx):
                        h, kt, ks, ba, as_ = its[idx]
                        if kt == 0:
                            P.dma("sp", ktb[ks][0:64, 0:nkt * 128], KTs[h, :, 0:nkt * 128], writes=[tKtb[ks]])
                            P.dma("sp", vb[ks][:, 0:nkt, :], Vs[h, :, 0:nkt, :], writes=[tVb[ks]])
                        di = kt - 4 * qt
                        q0 = 128 * max(di, 0)
                        bs_ = BS[idx % 3]
                        ps = (pi_ + idx) % 3
                        mm(bankf(bs_)[:, q0:512], ktb[ks][0:66, kt * 128:(kt + 1) * 128], qtb[qs][0:66, h, q0:512], True, True,
                           [tKtb[ks], tQtb[qs], tQtbH[qs], tQtbL[qs]], [bT[bs_]])
                        act(pT[ps][:, q0:512], bankf(bs_)[:, q0:512], AF.Exp, [tBq], [bT[bs_], tPT[ps]], bias=biasq[:, kt, h:h + 1])
                        if di >= 0:
                            tt("pool", pT[ps][:, q0:q0 + 128], pT[ps][:, q0:q0 + 128], csb["mask01_bf"], ALU.mult, [tConst], [tPT[ps]])

                    def emit_pv(idx):
                        h, kt, ks, ba, as_ = its[idx]
                        di = kt - 4 * qt
                        q0 = 128 * max(di, 0)
                        ps = (pi_ + idx) % 3
                        mm(bankf(ba)[0:65, q0:512], vb[ks][:, kt, 0:65], pT[ps][:, q0:512], kt == 0, kt == nkt - 1, [tPT[ps], tVb[ks]], [bT[ba]])
                        if kt == nkt - 1:
                            cp("act", accS[as_][0:65, :], bankf(ba)[0:65, :], [], [bT[ba], tAccS[as_]])
                            mm(bankf(BX)[0:64, :], selden[0:65, :], accS[as_][0:65, :], True, True, [tSd, tAccS[as_]], [bT[BX]])
                            P.op("dve", lambda e, o=rden, i=bankf(BX)[0:64, :]: e.reciprocal(o, i), reads=[], writes=[bT[BX], tRden])
                            tt("dve", o1, accS[as_][0:64, :], rden, ALU.mult, [tAccS[as_], tRden], [tO1])
                            tt("pool", mixFT[:, h, :], o1, gtb[:, h, :], ALU.mult, [tO1, tGtb], [tMixFT])

                    nit = len(its)
                    for idx in range(min(LOOK, nit)):
                        emit_qk(idx)
                    for idx in range(nit):
                        if idx + LOOK < nit:
                            emit_qk(idx + LOOK)
                        emit_pv(idx)
                    pi_ += nit
                    for j in range(4):
                        t = qt * 4 + j
                        s = t % 2
                        P.dma("sp", mxa[s], mixA[t * 128:(t + 1) * 128, :], writes=[tMxa[s]])
                        P.dma("sp", x2[s], x_src[t * 128:(t + 1) * 128, :], writes=[tX2[s]])
                        for half in range(2):
                            nk = 8 if half == 0 else 4
                            for kk in range(nk):
                                k = half * 8 + kk
                                tr(bankb(BX)[:, kk * 128:(kk + 1) * 128], mxa[s][:, k * 128:(k + 1) * 128], [tMxa[s]], [bT[BX]])
                            cp("act" if half == 0 else "dve", mT[:, half * 8:half * 8 + nk, :],
                               bankb(BX)[:, 0:nk * 128].rearrange("p (k c) -> p k c", k=nk), [], [bT[BX], tMT2[half]])
                        for cg in range(2):
                            bo = BO[cg]
                            csl = slice(cg * 512, (cg + 1) * 512)
                            for k in range(12):
                                mm(bankf(bo), mT[:, k, :], wout[:, k, csl], k == 0, False, [tMT2[k // 8], tWo], [bT[bo]])
                            for h in range(8):
                                mm(bankf(bo), mixFT[:, h, j * 128:(j + 1) * 128], woutF[:, h, csl], False, h == 7, [tMixFT, tWoF], [bT[bo]])
                            tt("dve", xn[s][:, csl], bankf(bo), gate_bc[:, csl], ALU.mult, [tMod], [bT[bo], tXn[s]])
                            tt("pool", xn[s][:, csl], xn[s][:, csl], x2[s][:, csl], ALU.add, [tX2[s]], [tXn[s]])
                        if not last:
                            P.dma("pool", xs_buf[t * 128:(t + 1) * 128, :], xn[s], reads=[tXn[s]], writes=[])
                        else:
                            act(fjunk, xn[s], AF.Square, [tXn[s]], [tFj, tFss], accum=fss[:, 0:1])
                            rsqrt_mean(fss[:, 1:2], fss[:, 0:1], D, [tFss], [tFss])
                            stt("dve", xn[s], xn[s], fss[:, 1:2], fgb, ALU.mult, ALU.mult, [tFss, tFgb], [tXn[s]])
                            out_toks.append(P.dma("pool", y_out[t * 128:(t + 1) * 128, :], xn[s], reads=[tXn[s]]))
                A.reset(m2)
                P.barrier()

        try:
            _phases()
        except _Stop:
            P.barrier()
        waits = P._collect("pool", (), (), extra=out_toks)
        P.streams["pool"].append((waits, None, None))
        print("ops recorded:", P.nops, "sems:", len(P.semkeys))
        P.emit()
    return nc


def make_in_maps(inputs, L, depth, ncores):
    consts = host_consts()
    maps = []
    f32 = np.float32
    x = np.asarray(inputs["x"], f32)
    c = np.asarray(inputs["c"], f32)
    pos = np.asarray(inputs["positions"], np.int32)
    conv_w = np.asarray(inputs["conv_w"], f32)
    conv_b = np.asarray(inputs["conv_b"], f32)
    cw_t = np.ascontiguousarray(conv_w[:depth].reshape(depth, 4, 12, 128).transpose(0, 3, 2, 1))
    cb_t = np.ascontiguousarray(conv_b[:depth].reshape(depth, 12, 128).transpose(0, 2, 1))
    B = x.shape[0]
    shared = {
        "norm_g": np.ascontiguousarray(np.asarray(inputs["norm_g"], f32)[:depth]),
        "w_ada": np.ascontiguousarray(np.asarray(inputs["w_ada"], f32)[:depth]),
        "b_ada": np.ascontiguousarray(np.asarray(inputs["b_ada"], f32)[:depth]),
        "w_in": np.ascontiguousarray(np.asarray(inputs["w_in"], f32)[:depth]),
        "cw_t": cw_t, "cb_t": cb_t,
        "dt_bias": np.ascontiguousarray(np.asarray(inputs["dt_bias"], f32)[:depth]),
        "a_log": np.ascontiguousarray(np.asarray(inputs["a_log"], f32)[:depth]),
        "d_skip": np.ascontiguousarray(np.asarray(inputs["d_skip"], f32)[:depth]),
        "ssd_norm_g": np.ascontiguousarray(np.asarray(inputs["ssd_norm_g"], f32)[:depth]),
        "b_forget": np.ascontiguousarray(np.asarray(inputs["b_forget"], f32)[:depth]),
        "w_out": np.ascontiguousarray(np.asarray(inputs["w_out"], f32)[:depth]),
        "final_g": np.ascontiguousarray(np.asarray(inputs["final_g"], f32).reshape(1, D)),
    }
    for n, s, dt in CONST_SPECS:
        shared[n] = consts[n]
    for core in range(ncores):
        b = core % B
        m = dict(shared)
        m["x"] = np.ascontiguousarray(x[b, :L])
        m["c_t"] = np.ascontiguousarray(c[b].reshape(8, 128).T)
        m["pos_t"] = np.ascontiguousarray(pos[b, :L].reshape(L // 128, 128).T)
        maps.append(m)
    return maps


def kernel(x, c, positions, norm_g, w_ada, b_ada, w_in, conv_w, conv_b, dt_bias, a_log, d_skip,
           ssd_norm_g, b_forget, w_out, final_g):
    inputs = dict(x=x, c=c, positions=positions, norm_g=norm_g, w_ada=w_ada, b_ada=b_ada, w_in=w_in,
                  conv_w=conv_w, conv_b=conv_b, dt_bias=dt_bias, a_log=a_log, d_skip=d_skip,
                  ssd_norm_g=ssd_norm_g, b_forget=b_forget, w_out=w_out, final_g=final_g)
    B, L, _ = np.asarray(x).shape
    nc = build_nc(L, DEPTH)
    maps = make_in_maps(inputs, L, DEPTH, 8)
    res = run_bass_kernel_spmd(nc, maps, core_ids=list(range(8)))
    out = np.stack([np.asarray(res.results[b]["y"], np.float32) for b in range(B)], axis=0)
    return out
```

```python
import contextlib
import math
import numpy as np
import ml_dtypes
import concourse.bass as bass
import concourse.mybir as mybir
from concourse.bass_utils import run_bass_kernel_spmd

F32 = mybir.dt.float32
BF16 = mybir.dt.bfloat16
I32 = mybir.dt.int32
U8 = mybir.dt.uint8
AF = mybir.ActivationFunctionType
ALU = mybir.AluOpType

D = 1024
NIN = 6680
EPS = 1e-6
EPOCH = 30000
SEQ = 8192
DEPTH = 4


class T:
    __slots__ = ("name", "lw", "rd", "rd_dma")

    def __init__(self, name=""):
        self.name = name
        self.lw = None
        self.rd = {}
        self.rd_dma = []


def TL(n, name=""):
    return [T("%s%d" % (name, i)) for i in range(n)]


class Prog:
    QUEUES = ("sp", "pool", "act")
    NDMASEM = 8

    def __init__(self, nc):
        self.nc = nc
        self.streams = {e: [] for e in ("pe", "dve", "act", "pool", "sp")}
        self.cnt = {}
        self.known = {e: {} for e in self.streams}
        self.semkeys = []
        self.semset = set()
        self.dma_rr = {q: 0 for q in self.QUEUES}
        self.dma_last = {}
        self.last_tok = {}
        self.nops = 0

    def _newtok(self, cname, step):
        n = self.cnt.get(cname, 0)
        self.cnt[cname] = n + 1
        key = (cname, n // EPOCH)
        if key not in self.semset:
            self.semset.add(key)
            self.semkeys.append(key)
        tok = (key, (n % EPOCH + 1) * step)
        self.last_tok[cname] = tok
        return tok

    def _collect(self, eng, reads, writes, extra=()):
        waits = {}
        kn = self.known[eng]

        def need(tok):
            if tok is None:
                return
            key, val = tok
            if eng == "pe" and key[0] == "pe":
                return
            if kn.get(key, 0) >= val:
                return
            if waits.get(key, 0) < val:
                waits[key] = val

        for t in reads:
            need(t.lw)
        for t in writes:
            need(t.lw)
            for r in t.rd.values():
                need(r)
            for r in t.rd_dma:
                need(r)
        for tok in extra:
            need(tok)
        for k, v in waits.items():
            kn[k] = v
        return list(waits.items())

    def op(self, eng, fn, reads=(), writes=()):
        waits = self._collect(eng, reads, writes)
        tok = self._newtok(eng, 1)
        self.streams[eng].append((waits, fn, tok))
        for t in reads:
            t.rd[eng] = tok
        for t in writes:
            t.lw = tok
            t.rd = {}
            t.rd_dma = []
        self.nops += 1
        return tok

    def dma(self, q, out, in_, reads=(), writes=()):
        s = self.dma_rr[q]
        self.dma_rr[q] = (s + 1) % self.NDMASEM
        prev = self.dma_last.get((q, s))
        waits = self._collect(q, reads, writes, extra=(prev,) if prev else ())
        tok = self._newtok(("dma", q, s), 16)
        self.dma_last[(q, s)] = tok

        def fn(e, out=out, in_=in_):
            return e.dma_start(out=out, in_=in_)

        self.streams[q].append((waits, fn, tok))
        for t in reads:
            t.rd_dma.append(tok)
        for t in writes:
            t.lw = tok
            t.rd = {}
            t.rd_dma = []
        self.nops += 1
        return tok

    def barrier(self):
        toks = list(self.last_tok.values())
        for eng in self.streams:
            waits = self._collect(eng, (), (), extra=toks)
            if waits:
                self.streams[eng].append((waits, None, None))

    def emit(self):
        nc = self.nc
        with contextlib.ExitStack() as es:
            sems = {}
            for i, key in enumerate(self.semkeys):
                sems[key] = es.enter_context(nc.semaphore("s%d" % i))
            block = es.enter_context(nc.Block())

            def run(eng_name):
                def body(e):
                    for waits, fn, tok in self.streams[eng_name]:
                        for key, val in waits:
                            e.wait_ge(sems[key], val)
                        if fn is not None:
                            ins = fn(e)
                            step = 16 if isinstance(tok[0][0], tuple) else 1
                            ins.then_inc(sems[tok[0]], step)
                return body

            block.tensor(run("pe"))
            block.vector(run("dve"))
            block.scalar(run("act"))
            block.gpsimd(run("pool"))
            block.sync(run("sp"))


class Arena:
    def __init__(self, ap_u8, nbytes):
        self.a = ap_u8
        self.n = nbytes
        self.off = 0
        self.peak = 0

    def alloc(self, dims, dt, parts=128, p0=0):
        esz = {F32: 4, BF16: 2, I32: 4}[dt]
        tot = esz
        for d_ in dims:
            tot *= d_
        tot_al = (tot + 63) // 64 * 64
        if self.off + tot_al > self.n:
            raise RuntimeError("SBUF arena overflow: need %d at %d of %d" % (tot_al, self.off, self.n))
        v = self.a[p0:p0 + parts, self.off:self.off + tot].bitcast(dt)
        self.off += tot_al
        self.peak = max(self.peak, self.off)
        if len(dims) == 2:
            v = v.rearrange("p (a b) -> p a b", a=dims[0])
        elif len(dims) == 3:
            v = v.rearrange("p (a b c) -> p a b c", a=dims[0], b=dims[1])
        return v

    def mark(self):
        return self.off

    def reset(self, m):
        self.off = m


def bc1(ap, n):
    return ap.unsqueeze(2).to_broadcast([ap.shape[0], ap.shape[1], n])


def bc0(ap, n):
    return ap.unsqueeze(1).to_broadcast([ap.shape[0], n, ap.shape[1]])


C_RQ, C_RK, C_RV, C_RG = 0, 512, 1024, 1536
C_XBC, C_DT, C_Z = 2048, 3584, 3600
C_FQ, C_FK, C_FV, C_FG, C_FR = 4624, 5136, 5648, 6160, 6672
WG_COLS = [0, 512, 1024, 1536, 2048, 2560, 3072, 3600, 4112, 4624, 5136, 5648, 6160]


def host_consts():
    c = {}
    idx = np.arange(128)
    c["ident_bf"] = np.eye(128, dtype=np.float32).astype(ml_dtypes.bfloat16)
    c["tri_incl"] = (idx[:, None] <= idx[None, :]).astype(np.float32)
    c["gstrict"] = (idx[:, None] > idx[None, :]).astype(np.float32)
    c["ones_f"] = np.ones((128, 128), np.float32)
    m01 = (idx[None, :] >= idx[:, None]).astype(np.float32)
    c["mask01_f"] = m01
    c["mask01_bf"] = m01.astype(ml_dtypes.bfloat16)
    H = 4
    log_g = np.log(1.0 - 2.0 ** (-5.0 - np.arange(H, dtype=np.float64)))
    diff = idx[None, :] - idx[:, None]
    dintraT = np.where(diff[None] >= 0, np.exp(log_g[:, None, None] * np.maximum(diff[None], 0)), 0.0)
    c["dintraT"] = np.ascontiguousarray(dintraT.transpose(1, 0, 2)).astype(np.float32)
    dq = np.exp(log_g[:, None] * (idx[None, :] + 1.0))
    c["dq"] = np.ascontiguousarray(np.broadcast_to(dq[None], (128, H, 128))).astype(np.float32)
    dk = np.exp(log_g[:, None] * (127.0 - idx[None, :]))
    c["dk"] = np.ascontiguousarray(dk.T).astype(np.float32)
    c["_dchunk"] = [float(np.exp(log_g[h] * 128.0)) for h in range(H)]
    freq = (10000.0 ** (-np.arange(64, dtype=np.float32) / np.float32(64))).astype(np.float32)
    c["freq_bc"] = np.ascontiguousarray(np.broadcast_to(freq[None], (128, 64))).astype(np.float32)
    return c


CONST_SPECS = [("ident_bf", [128, 128], BF16), ("tri_incl", [128, 128], F32), ("gstrict", [128, 128], F32),
               ("ones_f", [128, 128], F32), ("mask01_f", [128, 128], F32), ("mask01_bf", [128, 128], BF16),
               ("dintraT", [128, 4, 128], F32), ("dq", [128, 4, 128], F32), ("dk", [128, 4], F32),
               ("freq_bc", [128, 64], F32)]


class _Stop(Exception):
    pass


def build_nc(L=SEQ, depth=DEPTH, debug=False, stop=None):
    NT = L // 128
    NST = L // 512
    nc = bass.Bass("TRN2", target_bir_lowering=False)
    dchunk = host_consts()["_dchunk"]

    def din(name, shape, dt=F32):
        return nc.dram_tensor(name, shape, dt, kind="ExternalInput").ap()

    dbgkind = "ExternalOutput" if debug else "Internal"

    def dscr(name, shape, dt, dbg=False):
        return nc.dram_tensor(name, shape, dt, kind=(dbgkind if dbg else "Internal")).ap()

    x_in = din("x", [L, D])
    c_t = din("c_t", [128, 8])
    pos_t = din("pos_t", [128, NT], I32)
    norm_g = din("norm_g", [depth, D])
    w_ada = din("w_ada", [depth, D, 3 * D])
    b_ada = din("b_ada", [depth, 3 * D])
    w_in = din("w_in", [depth, D, NIN])
    cw_t = din("cw_t", [depth, 128, 12, 4])
    cb_t = din("cb_t", [depth, 128, 12])
    dt_bias = din("dt_bias", [depth, 16])
    a_log = din("a_log", [depth, 16])
    d_skip = din("d_skip", [depth, 16])
    ssd_g = din("ssd_norm_g", [depth, D])
    b_forget = din("b_forget", [depth, 8])
    w_out = din("w_out", [depth, 2 * D, D])
    final_g = din("final_g", [1, D])
    cin = {n: din(n, s, dt) for (n, s, dt) in CONST_SPECS}
    y_out = nc.dram_tensor("y", [L, D], F32, kind="ExternalOutput").ap()

    wb_in = dscr("wb_in", [depth, 13, 128, 8, 512], BF16)
    wb_sm = dscr("wb_sm", [depth, 128, 8, 24], BF16)
    wb_out = dscr("wb_out", [depth, 128, 12, D], BF16)
    wb_outF = dscr("wb_outF", [depth, 64, 8, D], BF16)
    cs_tab = dscr("cs_tab", [L, 128], F32, dbg=True)
    modtab = dscr("modtab", [depth, 128, 3 * D], F32, dbg=True)
    mixA = dscr("mixA", [L, 1536], BF16, dbg=True)
    QTs = dscr("QTs", [8, 64, L], BF16, dbg=True)
    KTs = dscr("KTs", [8, 64, L], BF16, dbg=True)
    Vs = dscr("Vs", [8, 128, NT, 72], BF16, dbg=True)
    GTs = dscr("GTs", [8, 64, L], BF16, dbg=True)
    FQs = dscr("FQs", [2, 8, L], BF16, dbg=True)
    xs_buf = dscr("xs_buf", [L, D], F32, dbg=True)
    dbgF = dscr("dbgF", [128, NT, 8], F32, dbg=True) if debug else None

    P = Prog(nc)
    out_toks = []

    with contextlib.ExitStack() as es:
        ARENA_BYTES = 207 * 1024
        arena_t = es.enter_context(nc.sbuf_tensor("arena", [128, ARENA_BYTES], U8))
        A = Arena(arena_t, ARENA_BYTES)
        banks = [es.enter_context(nc.psum_tensor("bank%d" % i, [128, 512], F32)) for i in range(8)]
        bT = TL(8, "bank")

        def bankf(i):
            return banks[i][:]

        def bankb(i):
            return banks[i][:].bitcast(BF16)

        def mm(out, lhsT, rhs, start, stop, R, W):
            P.op("pe", lambda e: e.matmul(out, lhsT, rhs, start=start, stop=stop, skip_group_check=True),
                 reads=R, writes=W)

        def tr(out, in_, R, W):
            P.op("pe", lambda e: e.transpose(out, in_, ident), reads=list(R) + [tIdent], writes=W)

        def act(out, in_, func, R, W, bias=None, scale=None, accum=None):
            kw = {}
            if bias is not None:
                kw["bias"] = bias
            if scale is not None:
                kw["scale"] = scale
            if accum is not None:
                kw["accum_out"] = accum
            P.op("act", lambda e: e.activation(out, in_, func, **kw), reads=R, writes=W)

        def tt(eng, out, in0, in1, op, R, W):
            P.op(eng, lambda e: e.tensor_tensor(out, in0, in1, op=op), reads=R, writes=W)

        def ts(eng, out, in0, s1, s2, op0, op1, R, W):
            if s2 is None:
                P.op(eng, lambda e: e.tensor_scalar(out, in0, s1, None, op0=op0), reads=R, writes=W)
            else:
                P.op(eng, lambda e: e.tensor_scalar(out, in0, s1, s2, op0=op0, op1=op1), reads=R, writes=W)

        def stt(eng, out, in0, scalar, in1, op0, op1, R, W):
            P.op(eng, lambda e: e.scalar_tensor_tensor(out, in0, scalar, in1, op0=op0, op1=op1), reads=R, writes=W)

        def cp(eng, out, in_, R, W):
            if eng == "act":
                P.op("act", lambda e: e.copy(out, in_), reads=R, writes=W)
            else:
                P.op(eng, lambda e: e.tensor_copy(out, in_), reads=R, writes=W)

        def memset(eng, out, val, W):
            P.op(eng, lambda e: e.memset(out, val), writes=W)

        def rsqrt_mean(out, ss, n, R, W):
            act(out, ss, AF.Ln, R, W, bias=epsb[0:out.shape[0], :], scale=1.0 / n)
            act(out, out, AF.Exp, W, W, scale=-0.5)

        csb = {}
        tConst = T("consts")
        for (n, s, dt) in CONST_SPECS:
            parts = s[0]
            v = A.alloc(s[1:], dt, parts=parts)
            csb[n] = v
            P.dma("sp", v, cin[n], writes=[tConst])
        ident = csb["ident_bf"]
        tIdent = tConst
        tri = csb["tri_incl"]
        gst = csb["gstrict"]
        ones_f = csb["ones_f"]
        epsb = A.alloc([1], F32)
        memset("pool", epsb, EPS, [tConst])

        F_all = A.alloc([NT, 8], F32)
        tF = T("F_all")
        carry_hist = A.alloc([NST, 8], F32)
        tCH = T("carry_hist")
        modv = A.alloc([3 * D], F32)
        tMod = T("modv")
        ssdg_bc = A.alloc([D], F32)
        dskip_bc = A.alloc([16], F32)
        dtb_bc = A.alloc([16], F32)
        A_bc = A.alloc([16], F32)
        bf_bc = A.alloc([8], F32)
        cw_sb = A.alloc([12, 4], F32)
        cb_sb = A.alloc([12], F32)
        tLay = T("layer_consts")
        tAl = T("a_log")
        base_mark = A.mark()

        m0 = A.mark()
        stg_f = [A.alloc([NIN], F32) for _ in range(2)]
        stg_b = [A.alloc([NIN], BF16) for _ in range(2)]
        tSf = TL(2, "stgf")
        tSb = TL(2, "stgb")
        tSb3 = [TL(3, "stgb3_%d" % i) for i in range(2)]
        it = 0
        conv_engs = ["dve", "act", "pool"]
        for l in range(depth):
            for r in range(8):
                s = it % 2
                P.dma("sp", stg_f[s], w_in[l, r * 128:(r + 1) * 128, :], writes=[tSf[s]])
                thirds = [(0, 2048), (2048, 4624), (4624, NIN)]
                for ei, (c0, c1) in enumerate(thirds):
                    cp(conv_engs[ei], stg_b[s][:, c0:c1], stg_f[s][:, c0:c1], [tSf[s]], [tSb3[s][ei]])
                rds = tSb3[s]
                for g, c0 in enumerate(WG_COLS):
                    P.dma("pool", wb_in[l, g, :, r, :], stg_b[s][:, c0:c0 + 512], reads=rds)
                P.dma("pool", wb_sm[l, :, r, 0:16], stg_b[s][:, C_DT:C_DT + 16], reads=rds)
                P.dma("pool", wb_sm[l, :, r, 16:24], stg_b[s][:, C_FR:C_FR + 8], reads=rds)
                it += 1
            for r in range(16):
                s = it % 2
                P.dma("sp", stg_f[s][:, 0:D], w_out[l, r * 128:(r + 1) * 128, :], writes=[tSf[s]])
                cp(conv_engs[it % 3], stg_b[s][:, 0:D], stg_f[s][:, 0:D], [tSf[s]], [tSb[s]] + tSb3[s])
                if r < 12:
                    P.dma("pool", wb_out[l, :, r, :], stg_b[s][:, 0:D], reads=[tSb[s]])
                else:
                    i2 = r - 12
                    P.dma("pool", wb_outF[l, :, 2 * i2, :], stg_b[s][0:64, 0:D], reads=[tSb[s]])
                    P.dma("pool", wb_outF[l, :, 2 * i2 + 1, :], stg_b[s][64:128, 0:D], reads=[tSb[s]])
                it += 1
        A.reset(m0)
        P.barrier()
        def _phases():
            if stop == "wconv":
                raise _Stop()
            m0 = A.mark()
            pos_i = A.alloc([NT], I32)
            pos_f = A.alloc([NT], F32)
            tPos = T("pos")
            P.dma("sp", pos_i, pos_t, writes=[tPos])
            cp("dve", pos_f, pos_i, [tPos], [tPos])
            GT = 8
            angs = [A.alloc([GT, 64], F32) for _ in range(2)]
            kf = A.alloc([GT, 64], F32)
            ki = A.alloc([GT, 64], I32)
            cstile = [A.alloc([GT, 128], F32) for _ in range(2)]
            tAng = T("ang")
            tCs = TL(2, "cstile")
            C1 = 6.28125
            C2 = 2.0 * math.pi - C1
            for gi, t0 in enumerate(range(0, NT, GT)):
                n = min(GT, NT - t0)
                s = gi % 2
                for which in range(2):
                    ang = angs[which]
                    tt("dve", ang[:, 0:n, :], bc0(csb["freq_bc"], n), bc1(pos_f[:, t0:t0 + n], 64), ALU.mult, [tPos, tConst], [tAng])
                    if which == 0:
                        ts("dve", ang[:, 0:n, :], ang[:, 0:n, :], math.pi / 2, None, ALU.add, None, [tAng], [tAng])
                    ts("dve", kf[:, 0:n, :], ang[:, 0:n, :], 1.0 / (2 * math.pi), None, ALU.mult, None, [tAng], [tAng])
                    cp("dve", ki[:, 0:n, :], kf[:, 0:n, :], [tAng], [tAng])
                    cp("dve", kf[:, 0:n, :], ki[:, 0:n, :], [tAng], [tAng])
                    stt("dve", ang[:, 0:n, :], kf[:, 0:n, :], -C1, ang[:, 0:n, :], ALU.mult, ALU.add, [tAng], [tAng])
                    stt("dve", ang[:, 0:n, :], kf[:, 0:n, :], -C2, ang[:, 0:n, :], ALU.mult, ALU.add, [tAng], [tAng])
                    ts("dve", ang[:, 0:n, :], ang[:, 0:n, :], 3.1415925, -3.1415925, ALU.min, ALU.max, [tAng], [tAng])
                    act(cstile[s][:, 0:n, which * 64:(which + 1) * 64], ang[:, 0:n, :], AF.Sin, [tAng], [tCs[s]])
                P.dma("pool", cs_tab.rearrange("(t p) c -> p t c", p=128)[:, t0:t0 + n, :], cstile[s][:, 0:n, :], reads=[tCs[s]])
            A.reset(m0)

            P.barrier()
            if stop == "cs":
                raise _Stop()
            m0 = A.mark()
            c_sb = A.alloc([8], F32)
            csil = A.alloc([8], F32)
            crep = A.alloc([8, 128], F32)
            tC = T("c")
            P.dma("sp", c_sb, c_t, writes=[tC])
            act(csil, c_sb, AF.Silu, [tC], [tC])
            cp("dve", crep, bc1(csil, 128), [tC], [tC])
            wada_sb = [A.alloc([8, 512], F32) for _ in range(2)]
            tWa = TL(2, "wada")
            bada_bc = A.alloc([3 * D], F32)
            ng_bc = A.alloc([D], F32)
            modst = A.alloc([3 * D], F32)
            tBa = T("bada")
            tNg = T("ng")
            tMs = T("modst")
            it = 0
            for l in range(depth):
                P.dma("sp", bada_bc, b_ada[l:l + 1, :].partition_broadcast(128), writes=[tBa])
                P.dma("sp", ng_bc, norm_g[l:l + 1, :].partition_broadcast(128), writes=[tNg])
                for cg in range(6):
                    s = it % 2
                    it += 1
                    P.dma("sp", wada_sb[s], w_ada[l].rearrange("(k p) c -> p k c", p=128)[:, :, cg * 512:(cg + 1) * 512], writes=[tWa[s]])
                    bk = cg % 2
                    for k in range(8):
                        mm(bankf(bk), crep[:, k, :], wada_sb[s][:, k, :], k == 0, k == 7, [tC, tWa[s]], [bT[bk]])
                    tt("dve", modst[:, cg * 512:(cg + 1) * 512], bankf(bk), bada_bc[:, cg * 512:(cg + 1) * 512], ALU.add, [tBa], [bT[bk], tMs])
                stt("dve", modst[:, D:2 * D], modst[:, D:2 * D], 1.0, ng_bc, ALU.add, ALU.mult, [tNg], [tMs])
                P.dma("pool", modtab[l], modst, reads=[tMs])
            A.reset(m0)
            P.barrier()

            if stop == "ada":
                raise _Stop()
            for l in range(depth):
                last = (l == depth - 1)
                x_src = x_in if l == 0 else xs_buf
                P.dma("sp", modv, modtab[l], writes=[tMod])
                P.dma("sp", ssdg_bc, ssd_g[l:l + 1, :].partition_broadcast(128), writes=[tLay])
                P.dma("sp", dskip_bc, d_skip[l:l + 1, :].partition_broadcast(128), writes=[tLay])
                P.dma("sp", dtb_bc, dt_bias[l:l + 1, :].partition_broadcast(128), writes=[tLay])
                P.dma("sp", A_bc, a_log[l:l + 1, :].partition_broadcast(128), writes=[tAl])
                P.dma("sp", bf_bc, b_forget[l:l + 1, :].partition_broadcast(128), writes=[tLay])
                P.dma("sp", cw_sb, cw_t[l], writes=[tLay])
                P.dma("sp", cb_sb, cb_t[l], writes=[tLay])
                act(A_bc, A_bc, AF.Exp, [tAl], [tAl])
                ts("dve", A_bc, A_bc, -1.0, None, ALU.mult, None, [tAl], [tAl])
                P.barrier()
                shift_bc = modv[:, 0:D]
                gs_bc = modv[:, D:2 * D]
                gate_bc = modv[:, 2 * D:3 * D]

                m1 = A.mark()
                wbuf = [A.alloc([8, 512], BF16) for _ in range(2)]
                tW = TL(2, "wbuf")
                wsm = A.alloc([8, 24], BF16)
                tWs = T("wsm")
                xt = [A.alloc([D], F32) for _ in range(2)]
                tX = TL(2, "xt")
                hn = [A.alloc([D], BF16)] * 2
                tHn = [T("hn")] * 2
                htmp = A.alloc([D], F32)
                tHt = T("htmp")
                junk = A.alloc([D], BF16)
                tJunk = T("junk")
                hT = A.alloc([8, 512], BF16)
                tHT = TL(4, "hT")
                dbufs = []
                for _i in range(2):
                    dbufs.append(dict(
                        rq=A.alloc([4, 512], BF16), rk=A.alloc([4, 512], BF16), rv=A.alloc([4, 512], BF16),
                        srg=A.alloc([4, 512], BF16), sz=A.alloc([4, D], BF16), dtf=A.alloc([4, 24], F32),
                        dtv4=A.alloc([4, 16], F32), av4=A.alloc([4, 16], F32), fr4=A.alloc([4, 8], F32), tDt=T("dtv4"), tFr=T("fr4"),
                        tRq=TL(4, "rq"), tRk=TL(4, "rk"), tRv=TL(4, "rv"), tRg=TL(4, "rg"), tSz=TL(4, "sz"), tDtf=TL(4, "dtf")))
                fvp = A.alloc([8, 4, 72], BF16)
                tFv = TL(4, "fvp")
                xbcT = A.alloc([12, 512], BF16)
                tXbc = TL(12, "xbcT")
                ubuf = [A.alloc([515], F32) for _ in range(2)]
                tU = TL(2, "ubuf")
                tUh = TL(2, "ubufh")
                cacc = [A.alloc([512], F32)] * 2
                tCa = [T("cacc")] * 2
                hist = A.alloc([12, 3], F32)
                tHist = TL(12, "hist")
                qkst = [A.alloc([512], BF16) for _ in range(2)]
                qkst.append(qkst[0])
                tQk = TL(2, "qkst")
                tQk.append(tQk[0])
                cst = [A.alloc([128], F32) for _ in range(2)]
                tCst = TL(2, "cst")
                ssx = A.alloc([8], F32)
                tSsx = T("ssx")
                rt = [A.alloc([4, 64], F32) for _ in range(4)]
                tRt = TL(4, "rt")
                qrot = A.alloc([4, 128], BF16)
                krot = A.alloc([4, 128], BF16)
                tQrot, tKrot = T("qrot"), T("krot")
                qT = A.alloc([4, 128], BF16)
                kT = A.alloc([4, 128], BF16)
                qdT = A.alloc([4, 128], BF16)
                kd = A.alloc([4, 128], BF16)
                tQT, tKT, tQdT, tKd = T("qT"), T("kT"), T("qdT"), T("kd")
                smT = A.alloc([4, 128], BF16)
                tSmT = T("smT")
                Sr = A.alloc([4, 128], F32)
                Sr_bf = A.alloc([4, 128], BF16)
                tSr, tSrb = T("Sr"), T("Srb")
                rss = A.alloc([8], F32)
                tRss = T("rss")
                otmp = A.alloc([4, 128], F32)
                tOt = T("otmp")
                mixst = [A.alloc([1536], BF16) for _ in range(2)]
                tMixR = TL(2, "mixstR")
                tMixS = TL(2, "mixstS")
                dtv = A.alloc([16], F32)
                av = A.alloc([16], F32)
                tDt = T("dtv")
                acum = A.alloc([16], F32)
                eacum = A.alloc([16], F32)
                elast = A.alloc([16], F32)
                decj = A.alloc([16], F32)
                tAc = T("acum")
                tDecj = T("decj")
                rhs1 = A.alloc([16, 128], F32)
                tRhs1 = TL(2, "rhs1")
                LT = A.alloc([16, 128], BF16)
                tLT = TL(2, "LT")
                MT = LT
                tMT = tLT
                cbm = A.alloc([2, 128], BF16)
                tCbm = T("cbm")
                xs_tm = A.alloc([D], BF16)
                xdt = A.alloc([D], BF16)
                xdtd = A.alloc([D], BF16)
                tXs, tXdt, tXdtd = T("xs_tm"), T("xdt"), T("xdtd")
                B_tm = A.alloc([2, 128], BF16)
                tBtm = T("B_tm")
                Ss = A.alloc([2, 512], F32)
                Ss_bf = A.alloc([2, 512], BF16)
                tSs, tSsb = T("Ss"), T("Ssb")
                y1 = A.alloc([D], F32)
                y2 = A.alloc([D], F32)
                tY1, tY2 = T("y1"), T("y2")
                sss = A.alloc([2], F32)
                tSss = T("sss")
                fr = A.alloc([8], F32)
                tFr = T("fr")
                carry_bc = A.alloc([8], F32)
                tCarry = T("carry")
                relc = A.alloc([1], F32, parts=8)
                tRelc = T("relc")
                fpt = A.alloc([128], F32, parts=8)
                fpl = A.alloc([128], F32, parts=8)
                tFpt = T("fpt")
                fqhi = [A.alloc([512], BF16, parts=8) for _ in range(2)]
                fqlo = [A.alloc([512], BF16, parts=8) for _ in range(2)]
                tFq = TL(2, "fq")
                if l == 0:
                    print("pass1 arena used", A.off, "base", base_mark)

                memset("pool", Sr, 0.0, [tSr])
                memset("pool", Sr_bf, 0.0, [tSrb])
                memset("pool", Ss, 0.0, [tSs])
                memset("pool", Ss_bf, 0.0, [tSsb])
                memset("pool", hist, 0.0, tHist)
                memset("pool", carry_bc, 0.0, [tCarry])
                memset("pool", fvp, 1.0, tFv)

                B_ACC = [0, 1, 2]
                B_T = 4
                B_SM = 3
                B_W = [4, 5, 6, 7]

                wgi = [0]

                def load_wgroup(c0, ncols):
                    s = wgi[0] % 2
                    wgi[0] += 1
                    P.dma("sp", wbuf[s], wb_in[l, WG_COLS.index(c0)], writes=[tW[s]])
                    return s

                acci = [0]

                def next_acc():
                    b = B_ACC[acci[0] % 3]
                    acci[0] += 1
                    return b

                def gen_AB(st):
                    _d = dbufs[st % 2]
                    rq, rk, rv, srg, sz, dtf = _d["rq"], _d["rk"], _d["rv"], _d["srg"], _d["sz"], _d["dtf"]
                    tRq, tRk, tRv, tRg, tSz, tDtf = _d["tRq"], _d["tRk"], _d["tRv"], _d["tRg"], _d["tSz"], _d["tDtf"]
                    for j in range(4):
                        t = st * 4 + j
                        s = t % 2
                        P.dma("sp", xt[s], x_src[t * 128:(t + 1) * 128, :], writes=[tX[s]])
                        act(junk, xt[s], AF.Square, [tX[s]], [tJunk, tSsx], accum=ssx[:, 0:1])
                        rsqrt_mean(ssx[:, 1:2], ssx[:, 0:1], D, [tSsx], [tSsx])
                        stt("dve", htmp, xt[s], ssx[:, 1:2], gs_bc, ALU.mult, ALU.mult, [tX[s], tSsx, tMod], [tHt])
                        tt("dve", hn[s], htmp, shift_bc, ALU.add, [tHt, tMod], [tHn[s]])
                        bA = next_acc()
                        for k in range(8):
                            tr(bankb(bA)[:, k * 128:(k + 1) * 128], hn[s][:, k * 128:(k + 1) * 128], [tHn[s]], [bT[bA]])
                        cp("act" if j % 2 == 0 else "dve", hT[:, :, j * 128:(j + 1) * 128],
                           bankb(bA).rearrange("p (k c) -> p k c", k=8), [], [bT[bA], tHT[j]])
                        yield

                    pend = []

                    def flush():
                        while pend:
                            pend.pop(0)()

                    def tm_group(c0, evac):
                        s = load_wgroup(c0, 512)
                        for j in range(4):
                            b = next_acc()
                            for k in range(8):
                                mm(bankf(b), hT[:, k, j * 128:(j + 1) * 128], wbuf[s][:, k, :], k == 0, k == 7, [tHT[j], tW[s]], [bT[b]])
                            if pend:
                                pend.pop(0)()
                            pend.append(lambda j=j, b=b: evac(j, b))
                            yield

                    def fm_group(c0, evac):
                        s = load_wgroup(c0, 512)
                        for cc in range(4):
                            b = next_acc()
                            for k in range(8):
                                mm(bankf(b), wbuf[s][:, k, cc * 128:(cc + 1) * 128], hT[:, k, :], k == 0, k == 7, tHT + [tW[s]], [bT[b]])
                            if pend:
                                pend.pop(0)()
                            pend.append(lambda cc=cc, b=b: evac(cc, b))
                            yield

                    def qk_evac(dst, scale):
                        def f(cc, b):
                            s = acci[0] % 3
                            if scale is None:
                                cp("dve", qkst[s], bankf(b), [], [bT[b], tQk[s]])
                            else:
                                act(qkst[s], bankf(b), AF.Copy, [], [bT[b], tQk[s]], scale=scale)
                            P.dma("pool", dst.rearrange("h d l -> (h d) l")[cc * 128:(cc + 1) * 128, st * 512:(st + 1) * 512], qkst[s], reads=[tQk[s]])
                        return f
                    yield from fm_group(C_FQ, qk_evac(QTs, 0.125))
                    yield from fm_group(C_FK, qk_evac(KTs, None))
                    yield from tm_group(C_FV, lambda j, b: cp("dve", fvp[:, :, j, 0:64], bankf(b).rearrange("p (h d) -> p h d", h=8), [], [bT[b], tFv[j]]))
                    flush()
                    for h8 in range(8):
                        P.dma("pool", Vs[h8, :, st * 4:(st + 1) * 4, :], fvp[:, h8, :, :], reads=tFv)

                    def g_evac(cc, b):
                        s3 = acci[0] % 3
                        act(qkst[s3], bankf(b), AF.Silu, [], [bT[b], tQk[s3]])
                        P.dma("pool", GTs.rearrange("h d l -> (h d) l")[cc * 128:(cc + 1) * 128, st * 512:(st + 1) * 512], qkst[s3], reads=[tQk[s3]])
                    yield from fm_group(C_FG, g_evac)

                    yield from tm_group(C_RQ, lambda j, b: cp("dve", rq[:, j, :], bankf(b), [], [bT[b], tRq[j]]))
                    yield from tm_group(C_RK, lambda j, b: act(rk[:, j, :], bankf(b), AF.Copy, [], [bT[b], tRk[j]], scale=128.0 ** -0.5))
                    yield from tm_group(C_RV, lambda j, b: cp("dve", rv[:, j, :], bankf(b), [], [bT[b], tRv[j]]))
                    yield from tm_group(C_RG, lambda j, b: act(srg[:, j, :], bankf(b), AF.Silu, [], [bT[b], tRg[j]]))
                    flush()
                    if st == 0:
                        P.dma("sp", wsm, wb_sm[l], writes=[tWs])
                    for j in range(4):
                        b = next_acc()
                        for k in range(8):
                            mm(bankf(b)[:, 0:24], hT[:, k, j * 128:(j + 1) * 128], wsm[:, k, :], k == 0, k == 7, [tHT[j], tWs], [bT[b]])
                        cp("dve", dtf[:, j, :], bankf(b)[:, 0:24], [], [bT[b], tDtf[j]])
                    dtv4, av4, fr4, tDt4, tFr4 = _d["dtv4"], _d["av4"], _d["fr4"], _d["tDt"], _d["tFr"]
                    tt("dve", dtv4, dtf[:, :, 0:16], bc0(dtb_bc, 4), ALU.add, tDtf + [tLay], [tDt4])
                    act(dtv4, dtv4, AF.Exp, [tDt4], [tDt4])
                    act(dtv4, dtv4, AF.Ln, [tDt4], [tDt4], bias=1.0)
                    tt("dve", av4, dtv4, bc0(A_bc, 4), ALU.mult, [tDt4, tLay], [tDt4])
                    tt("dve", fr4, dtf[:, :, 16:24], bc0(bf_bc, 4), ALU.add, tDtf + [tLay], [tFr4])
                    act(fr4, fr4, AF.Exp, [tFr4], [tFr4], scale=-1.0)
                    act(fr4, fr4, AF.Ln, [tFr4], [tFr4], bias=1.0)
                    ts("dve", fr4, fr4, -1.0, None, ALU.mult, None, [tFr4], [tFr4])
                    for half in range(2):
                        yield from tm_group(C_Z + half * 512,
                                 lambda j, b, half=half: act(sz[:, j, half * 512:(half + 1) * 512], bankf(b), AF.Silu, [], [bT[b], tSz[j]]))

                    def conv_evac(g):
                        def f(cc, b):
                            ch = g * 4 + cc
                            s = ch % 2
                            cp("act", ubuf[s][:, 3:515], bankf(b), [], [bT[b], tU[s]])
                            cp("pool", ubuf[s][:, 0:3], hist[:, ch, :], [tHist[ch]], [tUh[s]])
                            ts("dve", cacc[s], ubuf[s][:, 3:515], cw_sb[:, ch, 3:4], cb_sb[:, ch:ch + 1], ALU.mult, ALU.add, [tU[s], tLay], [tCa[s]])
                            for kk in range(3):
                                stt("dve", cacc[s], ubuf[s][:, kk:kk + 512], cw_sb[:, ch, kk:kk + 1], cacc[s], ALU.mult, ALU.add, [tU[s], tUh[s], tLay], [tCa[s]])
                            cp("pool", hist[:, ch, :], ubuf[s][:, 512:515], [tU[s]], [tHist[ch]])
                            act(xbcT[:, ch, :], cacc[s], AF.Silu, [tCa[s]], [tXbc[ch]])
                        return f
                    for g in range(3):
                        yield from fm_group(C_XBC + g * 512, conv_evac(g))

                    flush()

                def gen_C(st):
                    _d = dbufs[st % 2]
                    rq, rk, rv, srg, sz, dtf = _d["rq"], _d["rk"], _d["rv"], _d["srg"], _d["sz"], _d["dtf"]
                    tRq, tRk, tRv, tRg, tSz, tDtf = _d["tRq"], _d["tRk"], _d["tRv"], _d["tRg"], _d["tSz"], _d["tDtf"]
                    for j in range(4):
                        t = st * 4 + j
                        ms = t % 2
                        cs_ = cst[t % 2]
                        P.dma("sp", cs_, cs_tab[t * 128:(t + 1) * 128, :], writes=[tCst[t % 2]])
                        cosb = bc0(cs_[:, 0:64], 4)
                        sinb = bc0(cs_[:, 64:128], 4)
                        for (src, tsrc, dst, tdst, eng) in ((rq, tRq, qrot, tQrot, "pool"), (rk, tRk, krot, tKrot, "dve")):
                            v = src[:, j, :].rearrange("p (h two d) -> p h two d", h=4, two=2)
                            o = dst.rearrange("p h (two d) -> p h two d", two=2)
                            R = [tsrc[j], tCst[t % 2]]
                            tt(eng, rt[0], v[:, :, 0, :], cosb, ALU.mult, R, [tRt[0]])
                            tt(eng, rt[1], v[:, :, 1, :], sinb, ALU.mult, R, [tRt[1]])
                            tt(eng, o[:, :, 0, :], rt[0], rt[1], ALU.subtract, [tRt[0], tRt[1]], [tdst])
                            tt(eng, rt[2], v[:, :, 0, :], sinb, ALU.mult, R, [tRt[2]])
                            tt(eng, rt[3], v[:, :, 1, :], cosb, ALU.mult, R, [tRt[3]])
                            tt(eng, o[:, :, 1, :], rt[2], rt[3], ALU.add, [tRt[2], tRt[3]], [tdst])
                        tt("dve", kd, krot, bc1(csb["dk"], 128), ALU.mult, [tKrot, tConst], [tKd])
                        yield
                        bq = B_W[0]
                        for h in range(4):
                            tr(bankb(bq)[:, h * 128:(h + 1) * 128], qrot[:, h, :], [tQrot], [bT[bq]])
                        for h in range(4):
                            tr(bankb(bq)[:, 512 + h * 128:512 + (h + 1) * 128], krot[:, h, :], [tKrot], [bT[bq]])
                        cp("act", qT, bankb(bq)[:, 0:512].rearrange("p (h c) -> p h c", h=4), [], [bT[bq], tQT])
                        tt("dve", qdT, bankb(bq)[:, 0:512].rearrange("p (h c) -> p h c", h=4), csb["dq"], ALU.mult, [tConst], [bT[bq], tQdT])
                        cp("act", kT, bankb(bq)[:, 512:1024].rearrange("p (h c) -> p h c", h=4), [], [bT[bq], tKT])
                        yield
                        bs_ = B_W[1]
                        for h in range(4):
                            mm(bankf(bs_)[:, h * 128:(h + 1) * 128], kT[:, h, :], qT[:, h, :], h == 0, h == 3, [tKT, tQT], [bT[bs_]])
                        tt("dve", smT, bankf(bs_).rearrange("p (h c) -> p h c", h=4), csb["dintraT"], ALU.mult, [tConst], [bT[bs_], tSmT])
                        yield
                        bo = B_W[2]
                        for h in range(4):
                            mm(bankf(bo)[:, h * 128:(h + 1) * 128], smT[:, h, :], rv[:, j, h * 128:(h + 1) * 128], h == 0, False, [tSmT, tRv[j]], [bT[bo]])
                            mm(bankf(bo)[:, h * 128:(h + 1) * 128], qdT[:, h, :], Sr_bf[:, h, :], False, h == 3, [tQdT, tSrb], [bT[bo]])
                        bn = B_W[3]
                        for h in range(4):
                            mm(bankf(bn)[:, h * 128:(h + 1) * 128], kd[:, h, :], rv[:, j, h * 128:(h + 1) * 128], h == 0, h == 3, [tKd, tRv[j]], [bT[bn]])
                        for h in range(4):
                            stt("dve", Sr[:, h, :], Sr[:, h, :], dchunk[h], bankf(bn)[:, h * 128:(h + 1) * 128], ALU.mult, ALU.add, [], [bT[bn], tSr])
                        cp("act", Sr_bf, Sr, [tSr], [tSrb])
                        yield
                        for h in range(4):
                            act(junk[:, 0:128], bankf(bo)[:, h * 128:(h + 1) * 128], AF.Square, [], [bT[bo], tJunk, tRss], accum=rss[:, h:h + 1])
                        rsqrt_mean(rss[:, 4:8], rss[:, 0:4], 128, [tRss], [tRss])
                        tt("dve", otmp, bankf(bo).rearrange("p (h c) -> p h c", h=4), bc1(rss[:, 4:8], 128), ALU.mult, [tRss], [bT[bo], tOt])
                        tt("pool", mixst[ms][:, 0:512], otmp.rearrange("p h c -> p (h c)"), srg[:, j, :], ALU.mult, [tOt, tRg[j]], [tMixR[ms]])
                        yield

                        tsl = slice(j * 128, (j + 1) * 128)
                        dtv = _d["dtv4"][:, j, :]
                        av = _d["av4"][:, j, :]
                        tDt = _d["tDt"]
                        bsm = B_SM
                        mm(bankf(bsm)[:, 0:16], tri, av, True, False, [tConst, tDt], [bT[bsm]])
                        mm(bankf(bsm)[:, 16:32], ones_f, av, False, True, [tConst, tDt], [bT[bsm]])
                        act(eacum, bankf(bsm)[:, 0:16], AF.Exp, [], [bT[bsm], tAc])
                        act(elast, bankf(bsm)[:, 16:32], AF.Exp, [], [bT[bsm], tAc])
                        tt("dve", rhs1[:, 0:8, :], bc0(tri, 8), bc1(av[:, 0:8], 128), ALU.mult, [tConst, tDt], [tRhs1[0]])
                        tt("pool", rhs1[:, 8:16, :], bc0(tri, 8), bc1(av[:, 8:16], 128), ALU.mult, [tConst, tDt], [tRhs1[1]])
                        yield
                        for q4 in range(4):
                            b = B_W[q4]
                            mm(bankf(b), gst, rhs1[:, q4 * 4:(q4 + 1) * 4, :].rearrange("p h c -> p (h c)"), True, True, [tConst, tRhs1[q4 // 2]], [bT[b]])
                            act(LT[:, q4 * 4:(q4 + 1) * 4, :].rearrange("p h c -> p (h c)"), bankf(b), AF.Exp, [], [bT[b], tLT[q4 // 2]])
                            cp("dve", decj[:, q4 * 4:(q4 + 1) * 4], bankf(b).rearrange("p (h c) -> p h c", h=4)[:, :, 127], [], [bT[b], tDecj])
                        act(decj, decj, AF.Exp, [tDecj], [tDecj])
                        yield
                        for g in range(2):
                            mm(bankf(bsm)[:, 128 + g * 128:128 + (g + 1) * 128], xbcT[:, 8 + g, tsl], xbcT[:, 10 + g, tsl], False, g == 1,
                               [tXbc[8 + g], tXbc[10 + g]], [bT[bsm]])
                        tt("dve", cbm, bankf(bsm)[:, 128:384].rearrange("p (g c) -> p g c", g=2), bc0(csb["mask01_f"], 2), ALU.mult, [tConst], [bT[bsm], tCbm])
                        for g in range(2):
                            tt("dve" if g == 0 else "pool", MT[:, g * 8:(g + 1) * 8, :], LT[:, g * 8:(g + 1) * 8, :], bc0(cbm[:, g, :], 8), ALU.mult, [tLT[g], tCbm], [tMT[g]])
                        for c8 in range(8):
                            tr(bankb(B_T)[:, c8 * 128:(c8 + 1) * 128], xbcT[:, c8, tsl], [tXbc[c8]], [bT[B_T]])
                        cp("act", xs_tm, bankb(B_T), [], [bT[B_T], tXs])
                        tt("dve", xdt.rearrange("p (h d) -> p h d", h=16), bankb(B_T).rearrange("p (h d) -> p h d", h=16), bc1(dtv, 64), ALU.mult, [tDt], [bT[B_T], tXdt])
                        for g in range(2):
                            tr(bankb(B_T)[:, g * 128:(g + 1) * 128], xbcT[:, 8 + g, tsl], [tXbc[8 + g]], [bT[B_T]])
                        cp("act", B_tm, bankb(B_T)[:, 0:256].rearrange("p (g c) -> p g c", g=2), [], [bT[B_T], tBtm])
                        tt("pool", xdtd.rearrange("p (h d) -> p h d", h=16), xdt.rearrange("p (h d) -> p h d", h=16), bc1(decj, 64), ALU.mult, [tXdt, tDecj], [tXdtd])
                        yield
                        for h in range(16):
                            b = B_W[h // 8]
                            hh = h % 8
                            mm(bankf(b)[:, hh * 64:(hh + 1) * 64], MT[:, h, :], xdt[:, h * 64:(h + 1) * 64], hh == 0, hh == 7, [tMT[h // 8], tXdt], [bT[b]])
                        for g in range(2):
                            b = B_W[2 + g]
                            mm(bankf(b), xbcT[:, 10 + g, tsl], Ss_bf[:, g, :], True, True, [tXbc[10 + g], tSsb], [bT[b]])
                        for g in range(2):
                            tt("pool", Ss[:, g, :].rearrange("p (h d) -> p h d", h=8), Ss[:, g, :].rearrange("p (h d) -> p h d", h=8),
                               bc1(elast[:, g * 8:(g + 1) * 8], 64), ALU.mult, [tAc], [tSs])
                            mm(bankf(bsm), B_tm[:, g, :], xdtd[:, g * 512:(g + 1) * 512], True, True, [tBtm, tXdtd], [bT[bsm]])
                            tt("dve", Ss[:, g, :], Ss[:, g, :], bankf(bsm), ALU.add, [], [bT[bsm], tSs])
                        cp("act", Ss_bf, Ss, [tSs], [tSsb])
                        for g in range(2):
                            b = B_W[2 + g]
                            tt("dve", y1[:, g * 512:(g + 1) * 512].rearrange("p (h d) -> p h d", h=8), bankf(b).rearrange("p (h d) -> p h d", h=8),
                               bc1(eacum[:, g * 8:(g + 1) * 8], 64), ALU.mult, [tAc], [bT[b], tY1])
                        for g in range(2):
                            b = B_W[g]
                            tt("dve", y1[:, g * 512:(g + 1) * 512], y1[:, g * 512:(g + 1) * 512], bankf(b), ALU.add, [], [bT[b], tY1])
                        tt("pool", y2.rearrange("p (h d) -> p h d", h=16), xs_tm.rearrange("p (h d) -> p h d", h=16), bc1(dskip_bc, 64), ALU.mult, [tXs, tLay], [tY2])
                        tt("pool", y2, y2, y1, ALU.add, [tY1], [tY2])
                        tt("pool", y2, y2, sz[:, j, :], ALU.mult, [tSz[j]], [tY2])
                        yield
                        act(junk, y2, AF.Square, [tY2], [tJunk, tSss], accum=sss[:, 0:1])
                        rsqrt_mean(sss[:, 1:2], sss[:, 0:1], D, [tSss], [tSss])
                        stt("dve", mixst[ms][:, 512:1536], y2, sss[:, 1:2], ssdg_bc, ALU.mult, ALU.mult, [tY2, tSss, tLay], [tMixS[ms]])
                        yield
                        P.dma("pool", mixA[t * 128:(t + 1) * 128, :], mixst[ms], reads=[tMixR[ms], tMixS[ms]])

                        fs = st % 2
                        fr = _d["fr4"][:, j, :]
                        tFr = _d["tFr"]
                        mm(bankf(bsm)[:, 0:8], tri, fr, True, False, [tConst, tFr], [bT[bsm]])
                        mm(bankf(bsm)[:, 8:16], ones_f, fr, False, False, [tConst, tFr], [bT[bsm]])
                        mm(bankf(bsm)[0:8, 16:144], fr, tri, False, True, [tConst, tFr], [bT[bsm]])
                        if j == 0:
                            cp("dve", carry_hist[:, st, :], carry_bc, [tCarry], [tCH])
                            memset("pool", relc, 0.0, [tRelc])
                        tt("dve", F_all[:, t, :], bankf(bsm)[:, 0:8], carry_bc, ALU.add, [tCarry], [bT[bsm], tF])
                        tt("dve", carry_bc, bankf(bsm)[:, 8:16], carry_bc, ALU.add, [], [bT[bsm], tCarry])
                        ts("dve", fpt, bankf(bsm)[0:8, 16:144], relc[:, 0:1], None, ALU.add, None, [tRelc], [bT[bsm], tFpt])
                        cp("dve", relc, fpt[:, 127:128], [tFpt], [tRelc])
                        cp("dve", fqhi[fs][:, tsl], fpt, [tFpt], [tFq[fs]])
                        tt("dve", fpl, fpt, fqhi[fs][:, tsl], ALU.subtract, [tFpt, tFq[fs]], [tFpt])
                        cp("dve", fqlo[fs][:, tsl], fpl, [tFpt], [tFq[fs]])
                        yield
                    fs = st % 2
                    P.dma("pool", FQs[0, :, st * 512:(st + 1) * 512], fqhi[fs], reads=[tFq[fs]])
                    P.dma("pool", FQs[1, :, st * 512:(st + 1) * 512], fqlo[fs], reads=[tFq[fs]])
                def drive(gens):
                    alive = list(gens)
                    while alive:
                        for g_ in list(alive):
                            try:
                                next(g_)
                            except StopIteration:
                                alive.remove(g_)

                for st in range(NST + 1):
                    gens = []
                    if st < NST:
                        gens.append(gen_AB(st))
                    if st >= 1:
                        gens.append(gen_C(st - 1))
                    drive(gens)
                if debug and l == depth - 1:
                    P.dma("pool", dbgF, F_all, reads=[tF])
                A.reset(m1)
                P.barrier()
                if stop == "pass1":
                    raise _Stop()

                m2 = A.mark()
                wout = A.alloc([12, D], BF16)
                woutF = A.alloc([8, D], BF16, parts=64)
                tWo = T("wout")
                tWoF = T("woutF")
                P.dma("sp", wout, wb_out[l], writes=[tWo])
                P.dma("sp", woutF, wb_outF[l], writes=[tWoF])
                ktb = [A.alloc([L], BF16) for _ in range(2)]
                tKtb = TL(2, "ktb")
                vb = [A.alloc([NT, 72], BF16) for _ in range(2)]
                tVb = TL(2, "vb")
                qtb = [A.alloc([8, 512], BF16) for _ in range(2)]
                tQtb = TL(2, "qtb")
                tQtbH = TL(2, "qtbH")
                tQtbL = TL(2, "qtbL")
                gtb = A.alloc([8, 512], BF16, parts=64)
                tGtb = T("gtb")
                mxa = [A.alloc([1536], BF16) for _ in range(2)]
                tMxa = TL(2, "mxa")
                x2 = [A.alloc([D], F32) for _ in range(2)]
                tX2 = TL(2, "x2")
                pT = [A.alloc([512], BF16) for _ in range(3)]
                tPT = TL(3, "pT")
                mixFT = A.alloc([8, 512], BF16, parts=64)
                tMixFT = T("mixFT")
                mT = A.alloc([12, 128], BF16)
                tMT2 = TL(2, "mT")
                biasq = A.alloc([NT, 8], F32)
                tBq = T("biasq")
                accS = [A.alloc([512], F32) for _ in range(2)]
                tAccS = TL(2, "accS")
                rden = A.alloc([512], F32, parts=64)
                tRden = T("rden")
                o1 = A.alloc([512], F32, parts=64)
                tO1 = T("o1")
                selden = A.alloc([64], F32)
                tSd = T("selden")
                xn = [A.alloc([D], F32) for _ in range(2)]
                tXn = TL(2, "xn")
                fss = A.alloc([2], F32)
                tFss = T("fss")
                fjunk = A.alloc([D], BF16)
                tFj = T("fjunk")
                fgb = A.alloc([D], F32)
                tFgb = T("fgb")
                if l == 0:
                    print("pass2 arena used", A.off)
                if last:
                    P.dma("sp", fgb, final_g.partition_broadcast(128), writes=[tFgb])
                for s in range(2):
                    memset("pool", ktb[s][64:128, :], 1.0, [tKtb[s]])
                memset("pool", selden, 0.0, [tSd])
                memset("pool", selden[64:65, :], 1.0, [tSd])

                BS = [0, 1, 2]
                BA = [3, 4]
                BX = 5
                BO = [6, 7]
                si = 0
                ai = 0
                pi_ = 0
                hi_ = 0
                for qt in range(NST):
                    nkt = 4 * qt + 4
                    qs = qt % 2
                    qcols = slice(qt * 512, (qt + 1) * 512)
                    P.dma("sp", qtb[qs][0:64, :, :], QTs.rearrange("h d l -> d h l")[:, :, qcols], writes=[tQtb[qs]])
                    P.dma("sp", qtb[qs][64:65, :, :], FQs[0:1, :, qcols], writes=[tQtbH[qs]])
                    P.dma("sp", qtb[qs][65:66, :, :], FQs[1:2, :, qcols], writes=[tQtbL[qs]])
                    P.dma("sp", gtb, GTs.rearrange("h d l -> d h l")[:, :, qcols], writes=[tGtb])
                    tt("dve", biasq[:, 0:nkt, :], bc0(carry_hist[:, qt, :], nkt), F_all[:, 0:nkt, :], ALU.subtract, [tCH, tF], [tBq])
                    LOOK = 2
                    its = []
                    for h in range(8):
                        ks = hi_ % 2
                        hi_ += 1
                        ba = BA[ai % 2]
                        as_ = ai % 2
                        ai += 1
                        for kt in range(nkt):
                            its.append((h, kt, ks, ba, as_))
                    state = {}

                    def emit_qk(idx):
                        h, kt, ks, ba, as_ = its[idx]
                        if kt == 0:
                            P.dma("sp", ktb[ks][0:64, 0:nkt * 128], KTs[h, :, 0:nkt * 128], writes=[tKtb[ks]])
                            P.dma("sp", vb[ks][:, 0:nkt, :], Vs[h, :, 0:nkt, :], writes=[tVb[ks]])
                        di = kt - 4 * qt
                        q0 = 128 * max(di, 0)
                        bs_ = BS[idx % 3]
                        ps = (pi_ + idx) % 3
                        mm(bankf(bs_)[:, q0:512], ktb[ks][0:66, kt * 128:(kt + 1) * 128], qtb[qs][0:66, h, q0:512], True, True,
                           [tKtb[ks], tQtb[qs], tQtbH[qs], tQtbL[qs]], [bT[bs_]])
                        act(pT[ps][:, q0:512], bankf(bs_)[:, q0:512], AF.Exp, [tBq], [bT[bs_], tPT[ps]], bias=biasq[:, kt, h:h + 1])
                        if di >= 0:
                            tt("pool", pT[ps][:, q0:q0 + 128], pT[ps][:, q0:q0 + 128], csb["mask01_bf"], ALU.mult, [tConst], [tPT[ps]])

                    def emit_pv(idx):
                        h, kt, ks, ba, as_ = its[idx]
                        di = kt - 4 * qt
                        q0 = 128 * max(di, 0)
                        ps = (pi_ + idx) % 3
                        mm(bankf(ba)[0:65, q0:512], vb[ks][:, kt, 0:65], pT[ps][:, q0:512], kt == 0, kt == nkt - 1, [tPT[ps], tVb[ks]], [bT[ba]])
                        if kt == nkt - 1:
                            cp("act", accS[as_][0:65, :], bankf(ba)[0:65, :], [], [bT[ba], tAccS[as_]])
                            mm(bankf(BX)[0:64, :], selden[0:65, :], accS[as_][0:65, :], True, True, [tSd, tAccS[as_]], [bT[BX]])
                            P.op("dve", lambda e, o=rden, i=bankf(BX)[0:64, :]: e.reciprocal(o, i), reads=[], writes=[bT[BX], tRden])
                            tt("dve", o1, accS[as_][0:64, :], rden, ALU.mult, [tAccS[as_], tRden], [tO1])
                            tt("pool", mixFT[:, h, :], o1, gtb[:, h, :], ALU.mult, [tO1, tGtb], [tMixFT])

                    nit = len(its)
                    for idx in range(min(LOOK, nit)):
                        emit_qk(idx)
                    for idx in range(nit):
                        if idx + LOOK < nit:
                            emit_qk(idx + LOOK)
                        emit_pv(idx)
                    pi_ += nit
                    for j in range(4):
                        t = qt * 4 + j
                        s = t % 2
                        P.dma("sp", mxa[s], mixA[t * 128:(t + 1) * 128, :], writes=[tMxa[s]])
                        P.dma("sp", x2[s], x_src[t * 128:(t + 1) * 128, :], writes=[tX2[s]])
                        for half in range(2):
                            nk = 8 if half == 0 else 4
                            for kk in range(nk):
                                k = half * 8 + kk
                                tr(bankb(BX)[:, kk * 128:(kk + 1) * 128], mxa[s][:, k * 128:(k + 1) * 128], [tMxa[s]], [bT[BX]])
                            cp("act" if half == 0 else "dve", mT[:, half * 8:half * 8 + nk, :],
                               bankb(BX)[:, 0:nk * 128].rearrange("p (k c) -> p k c", k=nk), [], [bT[BX], tMT2[half]])
                        for cg in range(2):
                            bo = BO[cg]
                            csl = slice(cg * 512, (cg + 1) * 512)
                            for k in range(12):
                                mm(bankf(bo), mT[:, k, :], wout[:, k, csl], k == 0, False, [tMT2[k // 8], tWo], [bT[bo]])
                            for h in range(8):
                                mm(bankf(bo), mixFT[:, h, j * 128:(j + 1) * 128], woutF[:, h, csl], False, h == 7, [tMixFT, tWoF], [bT[bo]])
                            tt("dve", xn[s][:, csl], bankf(bo), gate_bc[:, csl], ALU.mult, [tMod], [bT[bo], tXn[s]])
                            tt("pool", xn[s][:, csl], xn[s][:, csl], x2[s][:, csl], ALU.add, [tX2[s]], [tXn[s]])
                        if not last:
                            P.dma("pool", xs_buf[t * 128:(t + 1) * 128, :], xn[s], reads=[tXn[s]], writes=[])
                        else:
                            act(fjunk, xn[s], AF.Square, [tXn[s]], [tFj, tFss], accum=fss[:, 0:1])
                            rsqrt_mean(fss[:, 1:2], fss[:, 0:1], D, [tFss], [tFss])
                            stt("dve", xn[s], xn[s], fss[:, 1:2], fgb, ALU.mult, ALU.mult, [tFss, tFgb], [tXn[s]])
                            out_toks.append(P.dma("pool", y_out[t * 128:(t + 1) * 128, :], xn[s], reads=[tXn[s]]))
                A.reset(m2)
                P.barrier()

        try:
            _phases()
        except _Stop:
            P.barrier()
        waits = P._collect("pool", (), (), extra=out_toks)
        P.streams["pool"].append((waits, None, None))
        print("ops recorded:", P.nops, "sems:", len(P.semkeys))
        P.emit()
    return nc


def make_in_maps(inputs, L, depth, ncores):
    consts = host_consts()
    maps = []
    f32 = np.float32
    x = np.asarray(inputs["x"], f32)
    c = np.asarray(inputs["c"], f32)
    pos = np.asarray(inputs["positions"], np.int32)
    conv_w = np.asarray(inputs["conv_w"], f32)
    conv_b = np.asarray(inputs["conv_b"], f32)
    cw_t = np.ascontiguousarray(conv_w[:depth].reshape(depth, 4, 12, 128).transpose(0, 3, 2, 1))
    cb_t = np.ascontiguousarray(conv_b[:depth].reshape(depth, 12, 128).transpose(0, 2, 1))
    B = x.shape[0]
    shared = {
        "norm_g": np.ascontiguousarray(np.asarray(inputs["norm_g"], f32)[:depth]),
        "w_ada": np.ascontiguousarray(np.asarray(inputs["w_ada"], f32)[:depth]),
        "b_ada": np.ascontiguousarray(np.asarray(inputs["b_ada"], f32)[:depth]),
        "w_in": np.ascontiguousarray(np.asarray(inputs["w_in"], f32)[:depth]),
        "cw_t": cw_t, "cb_t": cb_t,
        "dt_bias": np.ascontiguousarray(np.asarray(inputs["dt_bias"], f32)[:depth]),
        "a_log": np.ascontiguousarray(np.asarray(inputs["a_log"], f32)[:depth]),
        "d_skip": np.ascontiguousarray(np.asarray(inputs["d_skip"], f32)[:depth]),
        "ssd_norm_g": np.ascontiguousarray(np.asarray(inputs["ssd_norm_g"], f32)[:depth]),
        "b_forget": np.ascontiguousarray(np.asarray(inputs["b_forget"], f32)[:depth]),
        "w_out": np.ascontiguousarray(np.asarray(inputs["w_out"], f32)[:depth]),
        "final_g": np.ascontiguousarray(np.asarray(inputs["final_g"], f32).reshape(1, D)),
    }
    for n, s, dt in CONST_SPECS:
        shared[n] = consts[n]
    for core in range(ncores):
        b = core % B
        m = dict(shared)
        m["x"] = np.ascontiguousarray(x[b, :L])
        m["c_t"] = np.ascontiguousarray(c[b].reshape(8, 128).T)
        m["pos_t"] = np.ascontiguousarray(pos[b, :L].reshape(L // 128, 128).T)
        maps.append(m)
    return maps


def kernel(x, c, positions, norm_g, w_ada, b_ada, w_in, conv_w, conv_b, dt_bias, a_log, d_skip,
           ssd_norm_g, b_forget, w_out, final_g):
    inputs = dict(x=x, c=c, positions=positions, norm_g=norm_g, w_ada=w_ada, b_ada=b_ada, w_in=w_in,
                  conv_w=conv_w, conv_b=conv_b, dt_bias=dt_bias, a_log=a_log, d_skip=d_skip,
                  ssd_norm_g=ssd_norm_g, b_forget=b_forget, w_out=w_out, final_g=final_g)
    B, L, _ = np.asarray(x).shape
    nc = build_nc(L, DEPTH)
    maps = make_in_maps(inputs, L, DEPTH, 8)
    res = run_bass_kernel_spmd(nc, maps, core_ids=list(range(8)))
    out = np.stack([np.asarray(res.results[b]["y"], np.float32) for b in range(B)], axis=0)
    return out
```

```python
import contextlib
import math
import numpy as np
import ml_dtypes
import concourse.bass as bass
import concourse.mybir as mybir
from concourse.bass_utils import run_bass_kernel_spmd

F32 = mybir.dt.float32
BF16 = mybir.dt.bfloat16
I32 = mybir.dt.int32
U8 = mybir.dt.uint8
AF = mybir.ActivationFunctionType
ALU = mybir.AluOpType

D = 1024
NIN = 6680
EPS = 1e-6
EPOCH = 30000
SEQ = 8192
DEPTH = 4


class T:
    __slots__ = ("name", "lw", "rd", "rd_dma")

    def __init__(self, name=""):
        self.name = name
        self.lw = None
        self.rd = {}
        self.rd_dma = []


def TL(n, name=""):
    return [T("%s%d" % (name, i)) for i in range(n)]


class Prog:
    QUEUES = ("sp", "pool", "act")
    NDMASEM = 8

    def __init__(self, nc):
        self.nc = nc
        self.streams = {e: [] for e in ("pe", "dve", "act", "pool", "sp")}
        self.cnt = {}
        self.known = {e: {} for e in self.streams}
        self.semkeys = []
        self.semset = set()
        self.dma_rr = {q: 0 for q in self.QUEUES}
        self.dma_last = {}
        self.last_tok = {}
        self.nops = 0

    def _newtok(self, cname, step):
        n = self.cnt.get(cname, 0)
        self.cnt[cname] = n + 1
        key = (cname, n // EPOCH)
        if key not in self.semset:
            self.semset.add(key)
            self.semkeys.append(key)
        tok = (key, (n % EPOCH + 1) * step)
        self.last_tok[cname] = tok
        return tok

    def _collect(self, eng, reads, writes, extra=()):
        waits = {}
        kn = self.known[eng]

        def need(tok):
            if tok is None:
                return
            key, val = tok
            if eng == "pe" and key[0] == "pe":
                return
            if kn.get(key, 0) >= val:
                return
            if waits.get(key, 0) < val:
                waits[key] = val

        for t in reads:
            need(t.lw)
        for t in writes:
            need(t.lw)
            for r in t.rd.values():
                need(r)
            for r in t.rd_dma:
                need(r)
        for tok in extra:
            need(tok)
        for k, v in waits.items():
            kn[k] = v
        return list(waits.items())

    def op(self, eng, fn, reads=(), writes=()):
        waits = self._collect(eng, reads, writes)
        tok = self._newtok(eng, 1)
        self.streams[eng].append((waits, fn, tok))
        for t in reads:
            t.rd[eng] = tok
        for t in writes:
            t.lw = tok
            t.rd = {}
            t.rd_dma = []
        self.nops += 1
        return tok

    def dma(self, q, out, in_, reads=(), writes=()):
        s = self.dma_rr[q]
        self.dma_rr[q] = (s + 1) % self.NDMASEM
        prev = self.dma_last.get((q, s))
        waits = self._collect(q, reads, writes, extra=(prev,) if prev else ())
        tok = self._newtok(("dma", q, s), 16)
        self.dma_last[(q, s)] = tok

        def fn(e, out=out, in_=in_):
            return e.dma_start(out=out, in_=in_)

        self.streams[q].append((waits, fn, tok))
        for t in reads:
            t.rd_dma.append(tok)
        for t in writes:
            t.lw = tok
            t.rd = {}
            t.rd_dma = []
        self.nops += 1
        return tok

    def barrier(self):
        toks = list(self.last_tok.values())
        for eng in self.streams:
            waits = self._collect(eng, (), (), extra=toks)
            if waits:
                self.streams[eng].append((waits, None, None))

    def emit(self):
        nc = self.nc
        with contextlib.ExitStack() as es:
            sems = {}
            for i, key in enumerate(self.semkeys):
                sems[key] = es.enter_context(nc.semaphore("s%d" % i))
            block = es.enter_context(nc.Block())

            def run(eng_name):
                def body(e):
                    for waits, fn, tok in self.streams[eng_name]:
                        for key, val in waits:
                            e.wait_ge(sems[key], val)
                        if fn is not None:
                            ins = fn(e)
                            step = 16 if isinstance(tok[0][0], tuple) else 1
                            ins.then_inc(sems[tok[0]], step)
                return body

            block.tensor(run("pe"))
            block.vector(run("dve"))
            block.scalar(run("act"))
            block.gpsimd(run("pool"))
            block.sync(run("sp"))


class Arena:
    def __init__(self, ap_u8, nbytes):
        self.a = ap_u8
        self.n = nbytes
        self.off = 0
        self.peak = 0

    def alloc(self, dims, dt, parts=128, p0=0):
        esz = {F32: 4, BF16: 2, I32: 4}[dt]
        tot = esz
        for d_ in dims:
            tot *= d_
        tot_al = (tot + 63) // 64 * 64
        if self.off + tot_al > self.n:
            raise RuntimeError("SBUF arena overflow: need %d at %d of %d" % (tot_al, self.off, self.n))
        v = self.a[p0:p0 + parts, self.off:self.off + tot].bitcast(dt)
        self.off += tot_al
        self.peak = max(self.peak, self.off)
        if len(dims) == 2:
            v = v.rearrange("p (a b) -> p a b", a=dims[0])
        elif len(dims) == 3:
            v = v.rearrange("p (a b c) -> p a b c", a=dims[0], b=dims[1])
        return v

    def mark(self):
        return self.off

    def reset(self, m):
        self.off = m


def bc1(ap, n):
    return ap.unsqueeze(2).to_broadcast([ap.shape[0], ap.shape[1], n])


def bc0(ap, n):
    return ap.unsqueeze(1).to_broadcast([ap.shape[0], n, ap.shape[1]])


C_RQ, C_RK, C_RV, C_RG = 0, 512, 1024, 1536
C_XBC, C_DT, C_Z = 2048, 3584, 3600
C_FQ, C_FK, C_FV, C_FG, C_FR = 4624, 5136, 5648, 6160, 6672
WG_COLS = [0, 512, 1024, 1536, 2048, 2560, 3072, 3600, 4112, 4624, 5136, 5648, 6160]


def host_consts():
    c = {}
    idx = np.arange(128)
    c["ident_bf"] = np.eye(128, dtype=np.float32).astype(ml_dtypes.bfloat16)
    c["tri_incl"] = (idx[:, None] <= idx[None, :]).astype(np.float32)
    c["gstrict"] = (idx[:, None] > idx[None, :]).astype(np.float32)
    c["ones_f"] = np.ones((128, 128), np.float32)
    m01 = (idx[None, :] >= idx[:, None]).astype(np.float32)
    c["mask01_f"] = m01
    c["mask01_bf"] = m01.astype(ml_dtypes.bfloat16)
    H = 4
    log_g = np.log(1.0 - 2.0 ** (-5.0 - np.arange(H, dtype=np.float64)))
    diff = idx[None, :] - idx[:, None]
    dintraT = np.where(diff[None] >= 0, np.exp(log_g[:, None, None] * np.maximum(diff[None], 0)), 0.0)
    c["dintraT"] = np.ascontiguousarray(dintraT.transpose(1, 0, 2)).astype(np.float32)
    dq = np.exp(log_g[:, None] * (idx[None, :] + 1.0))
    c["dq"] = np.ascontiguousarray(np.broadcast_to(dq[None], (128, H, 128))).astype(np.float32)
    dk = np.exp(log_g[:, None] * (127.0 - idx[None, :]))
    c["dk"] = np.ascontiguousarray(dk.T).astype(np.float32)
    c["_dchunk"] = [float(np.exp(log_g[h] * 128.0)) for h in range(H)]
    freq = (10000.0 ** (-np.arange(64, dtype=np.float32) / np.float32(64))).astype(np.float32)
    c["freq_bc"] = np.ascontiguousarray(np.broadcast_to(freq[None], (128, 64))).astype(np.float32)
    return c


CONST_SPECS = [("ident_bf", [128, 128], BF16), ("tri_incl", [128, 128], F32), ("gstrict", [128, 128], F32),
               ("ones_f", [128, 128], F32), ("mask01_f", [128, 128], F32), ("mask01_bf", [128, 128], BF16),
               ("dintraT", [128, 4, 128], F32), ("dq", [128, 4, 128], F32), ("dk", [128, 4], F32),
               ("freq_bc", [128, 64], F32)]


class _Stop(Exception):
    pass


def build_nc(L=SEQ, depth=DEPTH, debug=False, stop=None):
    NT = L // 128
    NST = L // 512
    nc = bass.Bass("TRN2", target_bir_lowering=False)
    dchunk = host_consts()["_dchunk"]

    def din(name, shape, dt=F32):
        return nc.dram_tensor(name, shape, dt, kind="ExternalInput").ap()

    dbgkind = "ExternalOutput" if debug else "Internal"

    def dscr(name, shape, dt, dbg=False):
        return nc.dram_tensor(name, shape, dt, kind=(dbgkind if dbg else "Internal")).ap()

    x_in = din("x", [L, D])
    c_t = din("c_t", [128, 8])
    pos_t = din("pos_t", [128, NT], I32)
    norm_g = din("norm_g", [depth, D])
    w_ada = din("w_ada", [depth, D, 3 * D])
    b_ada = din("b_ada", [depth, 3 * D])
    w_in = din("w_in", [depth, D, NIN])
    cw_t = din("cw_t", [depth, 128, 12, 4])
    cb_t = din("cb_t", [depth, 128, 12])
    dt_bias = din("dt_bias", [depth, 16])
    a_log = din("a_log", [depth, 16])
    d_skip = din("d_skip", [depth, 16])
    ssd_g = din("ssd_norm_g", [depth, D])
    b_forget = din("b_forget", [depth, 8])
    w_out = din("w_out", [depth, 2 * D, D])
    final_g = din("final_g", [1, D])
    cin = {n: din(n, s, dt) for (n, s, dt) in CONST_SPECS}
    y_out = nc.dram_tensor("y", [L, D], F32, kind="ExternalOutput").ap()

    wb_in = dscr("wb_in", [depth, 13, 128, 8, 512], BF16)
    wb_sm = dscr("wb_sm", [depth, 128, 8, 24], BF16)
    wb_out = dscr("wb_out", [depth, 128, 12, D], BF16)
    wb_outF = dscr("wb_outF", [depth, 64, 8, D], BF16)
    cs_tab = dscr("cs_tab", [L, 128], F32, dbg=True)
    modtab = dscr("modtab", [depth, 128, 3 * D], F32, dbg=True)
    mixA = dscr("mixA", [L, 1536], BF16, dbg=True)
    QTs = dscr("QTs", [8, 64, L], BF16, dbg=True)
    KTs = dscr("KTs", [8, 64, L], BF16, dbg=True)
    Vs = dscr("Vs", [8, 128, NT, 72], BF16, dbg=True)
    GTs = dscr("GTs", [8, 64, L], BF16, dbg=True)
    FQs = dscr("FQs", [2, 8, L], BF16, dbg=True)
    xs_buf = dscr("xs_buf", [L, D], F32, dbg=True)
    dbgF = dscr("dbgF", [128, NT, 8], F32, dbg=True) if debug else None

    P = Prog(nc)
    out_toks = []

    with contextlib.ExitStack() as es:
        ARENA_BYTES = 207 * 1024
        arena_t = es.enter_context(nc.sbuf_tensor("arena", [128, ARENA_BYTES], U8))
        A = Arena(arena_t, ARENA_BYTES)
        banks = [es.enter_context(nc.psum_tensor("bank%d" % i, [128, 512], F32)) for i in range(8)]
        bT = TL(8, "bank")

        def bankf(i):
            return banks[i][:]

        def bankb(i):
            return banks[i][:].bitcast(BF16)

        def mm(out, lhsT, rhs, start, stop, R, W):
            P.op("pe", lambda e: e.matmul(out, lhsT, rhs, start=start, stop=stop, skip_group_check=True),
                 reads=R, writes=W)

        def tr(out, in_, R, W):
            P.op("pe", lambda e: e.transpose(out, in_, ident), reads=list(R) + [tIdent], writes=W)

        def act(out, in_, func, R, W, bias=None, scale=None, accum=None):
            kw = {}
            if bias is not None:
                kw["bias"] = bias
            if scale is not None:
                kw["scale"] = scale
            if accum is not None:
                kw["accum_out"] = accum
            P.op("act", lambda e: e.activation(out, in_, func, **kw), reads=R, writes=W)

        def tt(eng, out, in0, in1, op, R, W):
            P.op(eng, lambda e: e.tensor_tensor(out, in0, in1, op=op), reads=R, writes=W)

        def ts(eng, out, in0, s1, s2, op0, op1, R, W):
            if s2 is None:
                P.op(eng, lambda e: e.tensor_scalar(out, in0, s1, None, op0=op0), reads=R, writes=W)
            else:
                P.op(eng, lambda e: e.tensor_scalar(out, in0, s1, s2, op0=op0, op1=op1), reads=R, writes=W)

        def stt(eng, out, in0, scalar, in1, op0, op1, R, W):
            P.op(eng, lambda e: e.scalar_tensor_tensor(out, in0, scalar, in1, op0=op0, op1=op1), reads=R, writes=W)

        def cp(eng, out, in_, R, W):
            if eng == "act":
                P.op("act", lambda e: e.copy(out, in_), reads=R, writes=W)
            else:
                P.op(eng, lambda e: e.tensor_copy(out, in_), reads=R, writes=W)

        def memset(eng, out, val, W):
            P.op(eng, lambda e: e.memset(out, val), writes=W)

        def rsqrt_mean(out, ss, n, R, W):
            act(out, ss, AF.Ln, R, W, bias=epsb[0:out.shape[0], :], scale=1.0 / n)
            act(out, out, AF.Exp, W, W, scale=-0.5)

        csb = {}
        tConst = T("consts")
        for (n, s, dt) in CONST_SPECS:
            parts = s[0]
            v = A.alloc(s[1:], dt, parts=parts)
            csb[n] = v
            P.dma("sp", v, cin[n], writes=[tConst])
        ident = csb["ident_bf"]
        tIdent = tConst
        tri = csb["tri_incl"]
        gst = csb["gstrict"]
        ones_f = csb["ones_f"]
        epsb = A.alloc([1], F32)
        memset("pool", epsb, EPS, [tConst])

        F_all = A.alloc([NT, 8], F32)
        tF = T("F_all")
        carry_hist = A.alloc([NST, 8], F32)
        tCH = T("carry_hist")
        modv = A.alloc([3 * D], F32)
        tMod = T("modv")
        ssdg_bc = A.alloc([D], F32)
        dskip_bc = A.alloc([16], F32)
        dtb_bc = A.alloc([16], F32)
        A_bc = A.alloc([16], F32)
        bf_bc = A.alloc([8], F32)
        cw_sb = A.alloc([12, 4], F32)
        cb_sb = A.alloc([12], F32)
        tLay = T("layer_consts")
        tAl = T("a_log")
        base_mark = A.mark()

        m0 = A.mark()
        stg_f = [A.alloc([NIN], F32) for _ in range(2)]
        stg_b = [A.alloc([NIN], BF16) for _ in range(2)]
        tSf = TL(2, "stgf")
        tSb = TL(2, "stgb")
        tSb3 = [TL(3, "stgb3_%d" % i) for i in range(2)]
        it = 0
        conv_engs = ["dve", "act", "pool"]
        for l in range(depth):
            for r in range(8):
                s = it % 2
                P.dma("sp", stg_f[s], w_in[l, r * 128:(r + 1) * 128, :], writes=[tSf[s]])
                thirds = [(0, 2048), (2048, 4624), (4624, NIN)]
                for ei, (c0, c1) in enumerate(thirds):
                    cp(conv_engs[ei], stg_b[s][:, c0:c1], stg_f[s][:, c0:c1], [tSf[s]], [tSb3[s][ei]])
                rds = tSb3[s]
                for g, c0 in enumerate(WG_COLS):
                    P.dma("pool", wb_in[l, g, :, r, :], stg_b[s][:, c0:c0 + 512], reads=rds)
                P.dma("pool", wb_sm[l, :, r, 0:16], stg_b[s][:, C_DT:C_DT + 16], reads=rds)
                P.dma("pool", wb_sm[l, :, r, 16:24], stg_b[s][:, C_FR:C_FR + 8], reads=rds)
                it += 1
            for r in range(16):
                s = it % 2
                P.dma("sp", stg_f[s][:, 0:D], w_out[l, r * 128:(r + 1) * 128, :], writes=[tSf[s]])
                cp(conv_engs[it % 3], stg_b[s][:, 0:D], stg_f[s][:, 0:D], [tSf[s]], [tSb[s]] + tSb3[s])
                if r < 12:
                    P.dma("pool", wb_out[l, :, r, :], stg_b[s][:, 0:D], reads=[tSb[s]])
                else:
                    i2 = r - 12
                    P.dma("pool", wb_outF[l, :, 2 * i2, :], stg_b[s][0:64, 0:D], reads=[tSb[s]])
                    P.dma("pool", wb_outF[l, :, 2 * i2 + 1, :], stg_b[s][64:128, 0:D], reads=[tSb[s]])
                it += 1
        A.reset(m0)
        P.barrier()
        def _phases():
            if stop == "wconv":
                raise _Stop()
            m0 = A.mark()
            pos_i = A.alloc([NT], I32)
            pos_f = A.alloc([NT], F32)
            tPos = T("pos")
            P.dma("sp", pos_i, pos_t, writes=[tPos])
            cp("dve", pos_f, pos_i, [tPos], [tPos])
            GT = 8
            angs = [A.alloc([GT, 64], F32) for _ in range(2)]
            kf = A.alloc([GT, 64], F32)
            ki = A.alloc([GT, 64], I32)
            cstile = [A.alloc([GT, 128], F32) for _ in range(2)]
            tAng = T("ang")
            tCs = TL(2, "cstile")
            C1 = 6.28125
            C2 = 2.0 * math.pi - C1
            for gi, t0 in enumerate(range(0, NT, GT)):
                n = min(GT, NT - t0)
                s = gi % 2
                for which in range(2):
                    ang = angs[which]
                    tt("dve", ang[:, 0:n, :], bc0(csb["freq_bc"], n), bc1(pos_f[:, t0:t0 + n], 64), ALU.mult, [tPos, tConst], [tAng])
                    if which == 0:
                        ts("dve", ang[:, 0:n, :], ang[:, 0:n, :], math.pi / 2, None, ALU.add, None, [tAng], [tAng])
                    ts("dve", kf[:, 0:n, :], ang[:, 0:n, :], 1.0 / (2 * math.pi), None, ALU.mult, None, [tAng], [tAng])
                    cp("dve", ki[:, 0:n, :], kf[:, 0:n, :], [tAng], [tAng])
                    cp("dve", kf[:, 0:n, :], ki[:, 0:n, :], [tAng], [tAng])
                    stt("dve", ang[:, 0:n, :], kf[:, 0:n, :], -C1, ang[:, 0:n, :], ALU.mult, ALU.add, [tAng], [tAng])
                    stt("dve", ang[:, 0:n, :], kf[:, 0:n, :], -C2, ang[:, 0:n, :], ALU.mult, ALU.add, [tAng], [tAng])
                    ts("dve", ang[:, 0:n, :], ang[:, 0:n, :], 3.1415925, -3.1415925, ALU.min, ALU.max, [tAng], [tAng])
                    act(cstile[s][:, 0:n, which * 64:(which + 1) * 64], ang[:, 0:n, :], AF.Sin, [tAng], [tCs[s]])
                P.dma("pool", cs_tab.rearrange("(t p) c -> p t c", p=128)[:, t0:t0 + n, :], cstile[s][:, 0:n, :], reads=[tCs[s]])
            A.reset(m0)

            P.barrier()
            if stop == "cs":
                raise _Stop()
            m0 = A.mark()
            c_sb = A.alloc([8], F32)
            csil = A.alloc([8], F32)
            crep = A.alloc([8, 128], F32)
            tC = T("c")
            P.dma("sp", c_sb, c_t, writes=[tC])
            act(csil, c_sb, AF.Silu, [tC], [tC])
            cp("dve", crep, bc1(csil, 128), [tC], [tC])
            wada_sb = [A.alloc([8, 512], F32) for _ in range(2)]
            tWa = TL(2, "wada")
            bada_bc = A.alloc([3 * D], F32)
            ng_bc = A.alloc([D], F32)
            modst = A.alloc([3 * D], F32)
            tBa = T("bada")
            tNg = T("ng")
            tMs = T("modst")
            it = 0
            for l in range(depth):
                P.dma("sp", bada_bc, b_ada[l:l + 1, :].partition_broadcast(128), writes=[tBa])
                P.dma("sp", ng_bc, norm_g[l:l + 1, :].partition_broadcast(128), writes=[tNg])
                for cg in range(6):
                    s = it % 2
                    it += 1
                    P.dma("sp", wada_sb[s], w_ada[l].rearrange("(k p) c -> p k c", p=128)[:, :, cg * 512:(cg + 1) * 512], writes=[tWa[s]])
                    bk = cg % 2
                    for k in range(8):
                        mm(bankf(bk), crep[:, k, :], wada_sb[s][:, k, :], k == 0, k == 7, [tC, tWa[s]], [bT[bk]])
                    tt("dve", modst[:, cg * 512:(cg + 1) * 512], bankf(bk), bada_bc[:, cg * 512:(cg + 1) * 512], ALU.add, [tBa], [bT[bk], tMs])
                stt("dve", modst[:, D:2 * D], modst[:, D:2 * D], 1.0, ng_bc, ALU.add, ALU.mult, [tNg], [tMs])
                P.dma("pool", modtab[l], modst, reads=[tMs])
            A.reset(m0)
            P.barrier()

            if stop == "ada":
                raise _Stop()
            for l in range(depth):
                last = (l == depth - 1)
                x_src = x_in if l == 0 else xs_buf
                P.dma("sp", modv, modtab[l], writes=[tMod])
                P.dma("sp", ssdg_bc, ssd_g[l:l + 1, :].partition_broadcast(128), writes=[tLay])
                P.dma("sp", dskip_bc, d_skip[l:l + 1, :].partition_broadcast(128), writes=[tLay])
                P.dma("sp", dtb_bc, dt_bias[l:l + 1, :].partition_broadcast(128), writes=[tLay])
                P.dma("sp", A_bc, a_log[l:l + 1, :].partition_broadcast(128), writes=[tAl])
                P.dma("sp", bf_bc, b_forget[l:l + 1, :].partition_broadcast(128), writes=[tLay])
                P.dma("sp", cw_sb, cw_t[l], writes=[tLay])
                P.dma("sp", cb_sb, cb_t[l], writes=[tLay])
                act(A_bc, A_bc, AF.Exp, [tAl], [tAl])
                ts("dve", A_bc, A_bc, -1.0, None, ALU.mult, None, [tAl], [tAl])
                P.barrier()
                shift_bc = modv[:, 0:D]
                gs_bc = modv[:, D:2 * D]
                gate_bc = modv[:, 2 * D:3 * D]

                m1 = A.mark()
                wbuf = [A.alloc([8, 512], BF16) for _ in range(2)]
                tW = TL(2, "wbuf")
                wsm = A.alloc([8, 24], BF16)
                tWs = T("wsm")
                xt = [A.alloc([D], F32) for _ in range(2)]
                tX = TL(2, "xt")
                hn = [A.alloc([D], BF16)] * 2
                tHn = [T("hn")] * 2
                htmp = A.alloc([D], F32)
                tHt = T("htmp")
                junk = A.alloc([D], BF16)
                tJunk = T("junk")
                hT = A.alloc([8, 512], BF16)
                tHT = TL(4, "hT")
                dbufs = []
                for _i in range(2):
                    dbufs.append(dict(
                        rq=A.alloc([4, 512], BF16), rk=A.alloc([4, 512], BF16), rv=A.alloc([4, 512], BF16),
                        srg=A.alloc([4, 512], BF16), sz=A.alloc([4, D], BF16), dtf=A.alloc([4, 24], F32),
                        dtv4=A.alloc([4, 16], F32), av4=A.alloc([4, 16], F32), fr4=A.alloc([4, 8], F32), tDt=T("dtv4"), tFr=T("fr4"),
                        tRq=TL(4, "rq"), tRk=TL(4, "rk"), tRv=TL(4, "rv"), tRg=TL(4, "rg"), tSz=TL(4, "sz"), tDtf=TL(4, "dtf")))
                fvp = A.alloc([8, 4, 72], BF16)
                tFv = TL(4, "fvp")
                xbcT = A.alloc([12, 512], BF16)
                tXbc = TL(12, "xbcT")
                ubuf = [A.alloc([515], F32) for _ in range(2)]
                tU = TL(2, "ubuf")
                tUh = TL(2, "ubufh")
                cacc = [A.alloc([512], F32)] * 2
                tCa = [T("cacc")] * 2
                hist = A.alloc([12, 3], F32)
                tHist = TL(12, "hist")
                qkst = [A.alloc([512], BF16) for _ in range(2)]
                qkst.append(qkst[0])
                tQk = TL(2, "qkst")
                tQk.append(tQk[0])
                cst = [A.alloc([128], F32) for _ in range(2)]
                tCst = TL(2, "cst")
                ssx = A.alloc([8], F32)
                tSsx = T("ssx")
                rt = [A.alloc([4, 64], F32) for _ in range(4)]
                tRt = TL(4, "rt")
                qrot = A.alloc([4, 128], BF16)
                krot = A.alloc([4, 128], BF16)
                tQrot, tKrot = T("qrot"), T("krot")
                qT = A.alloc([4, 128], BF16)
                kT = A.alloc([4, 128], BF16)
                qdT = A.alloc([4, 128], BF16)
                kd = A.alloc([4, 128], BF16)
                tQT, tKT, tQdT, tKd = T("qT"), T("kT"), T("qdT"), T("kd")
                smT = A.alloc([4, 128], BF16)
                tSmT = T("smT")
                Sr = A.alloc([4, 128], F32)
                Sr_bf = A.alloc([4, 128], BF16)
                tSr, tSrb = T("Sr"), T("Srb")
                rss = A.alloc([8], F32)
                tRss = T("rss")
                otmp = A.alloc([4, 128], F32)
                tOt = T("otmp")
                mixst = [A.alloc([1536], BF16) for _ in range(2)]
                tMixR = TL(2, "mixstR")
                tMixS = TL(2, "mixstS")
                dtv = A.alloc([16], F32)
                av = A.alloc([16], F32)
                tDt = T("dtv")
                acum = A.alloc([16], F32)
                eacum = A.alloc([16], F32)
                elast = A.alloc([16], F32)
                decj = A.alloc([16], F32)
                tAc = T("acum")
                tDecj = T("decj")
                rhs1 = A.alloc([16, 128], F32)
                tRhs1 = TL(2, "rhs1")
                LT = A.alloc([16, 128], BF16)
                tLT = TL(2, "LT")
                MT = LT
                tMT = tLT
                cbm = A.alloc([2, 128], BF16)
                tCbm = T("cbm")
                xs_tm = A.alloc([D], BF16)
                xdt = A.alloc([D], BF16)
                xdtd = A.alloc([D], BF16)
                tXs, tXdt, tXdtd = T("xs_tm"), T("xdt"), T("xdtd")
                B_tm = A.alloc([2, 128], BF16)
                tBtm = T("B_tm")
                Ss = A.alloc([2, 512], F32)
                Ss_bf = A.alloc([2, 512], BF16)
                tSs, tSsb = T("Ss"), T("Ssb")
                y1 = A.alloc([D], F32)
                y2 = A.alloc([D], F32)
                tY1, tY2 = T("y1"), T("y2")
                sss = A.alloc([2], F32)
                tSss = T("sss")
                fr = A.alloc([8], F32)
                tFr = T("fr")
                carry_bc = A.alloc([8], F32)
                tCarry = T("carry")
                relc = A.alloc([1], F32, parts=8)
                tRelc = T("relc")
                fpt = A.alloc([128], F32, parts=8)
                fpl = A.alloc([128], F32, parts=8)
                tFpt = T("fpt")
                fqhi = [A.alloc([512], BF16, parts=8) for _ in range(2)]
                fqlo = [A.alloc([512], BF16, parts=8) for _ in range(2)]
                tFq = TL(2, "fq")
                if l == 0:
                    print("pass1 arena used", A.off, "base", base_mark)

                memset("pool", Sr, 0.0, [tSr])
                memset("pool", Sr_bf, 0.0, [tSrb])
                memset("pool", Ss, 0.0, [tSs])
                memset("pool", Ss_bf, 0.0, [tSsb])
                memset("pool", hist, 0.0, tHist)
                memset("pool", carry_bc, 0.0, [tCarry])
                memset("pool", fvp, 1.0, tFv)

                B_ACC = [0, 1, 2]
                B_T = 4
                B_SM = 3
                B_W = [4, 5, 6, 7]

                wgi = [0]

                def load_wgroup(c0, ncols):
                    s = wgi[0] % 2
                    wgi[0] += 1
                    P.dma("sp", wbuf[s], wb_in[l, WG_COLS.index(c0)], writes=[tW[s]])
                    return s

                acci = [0]

                def next_acc():
                    b = B_ACC[acci[0] % 3]
                    acci[0] += 1
                    return b

                def gen_AB(st):
                    _d = dbufs[st % 2]
                    rq, rk, rv, srg, sz, dtf = _d["rq"], _d["rk"], _d["rv"], _d["srg"], _d["sz"], _d["dtf"]
                    tRq, tRk, tRv, tRg, tSz, tDtf = _d["tRq"], _d["tRk"], _d["tRv"], _d["tRg"], _d["tSz"], _d["tDtf"]
                    for j in range(4):
                        t = st * 4 + j
                        s = t % 2
                        P.dma("sp", xt[s], x_src[t * 128:(t + 1) * 128, :], writes=[tX[s]])
                        act(junk, xt[s], AF.Square, [tX[s]], [tJunk, tSsx], accum=ssx[:, 0:1])
                        rsqrt_mean(ssx[:, 1:2], ssx[:, 0:1], D, [tSsx], [tSsx])
                        stt("dve", htmp, xt[s], ssx[:, 1:2], gs_bc, ALU.mult, ALU.mult, [tX[s], tSsx, tMod], [tHt])
                        tt("dve", hn[s], htmp, shift_bc, ALU.add, [tHt, tMod], [tHn[s]])
                        bA = next_acc()
                        for k in range(8):
                            tr(bankb(bA)[:, k * 128:(k + 1) * 128], hn[s][:, k * 128:(k + 1) * 128], [tHn[s]], [bT[bA]])
                        cp("act" if j % 2 == 0 else "dve", hT[:, :, j * 128:(j + 1) * 128],
                           bankb(bA).rearrange("p (k c) -> p k c", k=8), [], [bT[bA], tHT[j]])
                        yield

                    pend = []

                    def flush():
                        while pend:
                            pend.pop(0)()

                    def tm_group(c0, evac):
                        s = load_wgroup(c0, 512)
                        for j in range(4):
                            b = next_acc()
                            for k in range(8):
                                mm(bankf(b), hT[:, k, j * 128:(j + 1) * 128], wbuf[s][:, k, :], k == 0, k == 7, [tHT[j], tW[s]], [bT[b]])
                            if pend:
                                pend.pop(0)()
                            pend.append(lambda j=j, b=b: evac(j, b))
                            yield

                    def fm_group(c0, evac):
                        s = load_wgroup(c0, 512)
                        for cc in range(4):
                            b = next_acc()
                            for k in range(8):
                                mm(bankf(b), wbuf[s][:, k, cc * 128:(cc + 1) * 128], hT[:, k, :], k == 0, k == 7, tHT + [tW[s]], [bT[b]])
                            if pend:
                                pend.pop(0)()
                            pend.append(lambda cc=cc, b=b: evac(cc, b))
                            yield

                    def qk_evac(dst, scale):
                        def f(cc, b):
                            s = acci[0] % 3
                            if scale is None:
                                cp("dve", qkst[s], bankf(b), [], [bT[b], tQk[s]])
                            else:
                                act(qkst[s], bankf(b), AF.Copy, [], [bT[b], tQk[s]], scale=scale)
                            P.dma("act", dst.rearrange("h d l -> (h d) l")[cc * 128:(cc + 1) * 128, st * 512:(st + 1) * 512], qkst[s], reads=[tQk[s]])
                        return f
                    yield from fm_group(C_FQ, qk_evac(QTs, 0.125))
                    yield from fm_group(C_FK, qk_evac(KTs, None))
                    yield from tm_group(C_FV, lambda j, b: cp("dve", fvp[:, :, j, 0:64], bankf(b).rearrange("p (h d) -> p h d", h=8), [], [bT[b], tFv[j]]))
                    flush()
                    for h8 in range(8):
                        P.dma("act", Vs[h8, :, st * 4:(st + 1) * 4, :], fvp[:, h8, :, :], reads=tFv)

                    def g_evac(cc, b):
                        s3 = acci[0] % 3
                        act(qkst[s3], bankf(b), AF.Silu, [], [bT[b], tQk[s3]])
                        P.dma("act", GTs.rearrange("h d l -> (h d) l")[cc * 128:(cc + 1) * 128, st * 512:(st + 1) * 512], qkst[s3], reads=[tQk[s3]])
                    yield from fm_group(C_FG, g_evac)

                    yield from tm_group(C_RQ, lambda j, b: cp("dve", rq[:, j, :], bankf(b), [], [bT[b], tRq[j]]))
                    yield from tm_group(C_RK, lambda j, b: act(rk[:, j, :], bankf(b), AF.Copy, [], [bT[b], tRk[j]], scale=128.0 ** -0.5))
                    yield from tm_group(C_RV, lambda j, b: cp("dve", rv[:, j, :], bankf(b), [], [bT[b], tRv[j]]))
                    yield from tm_group(C_RG, lambda j, b: act(srg[:, j, :], bankf(b), AF.Silu, [], [bT[b], tRg[j]]))
                    flush()
                    if st == 0:
                        P.dma("sp", wsm, wb_sm[l], writes=[tWs])
                    for j in range(4):
                        b = next_acc()
                        for k in range(8):
                            mm(bankf(b)[:, 0:24], hT[:, k, j * 128:(j + 1) * 128], wsm[:, k, :], k == 0, k == 7, [tHT[j], tWs], [bT[b]])
                        cp("dve", dtf[:, j, :], bankf(b)[:, 0:24], [], [bT[b], tDtf[j]])
                    dtv4, av4, fr4, tDt4, tFr4 = _d["dtv4"], _d["av4"], _d["fr4"], _d["tDt"], _d["tFr"]
                    tt("dve", dtv4, dtf[:, :, 0:16], bc0(dtb_bc, 4), ALU.add, tDtf + [tLay], [tDt4])
                    act(dtv4, dtv4, AF.Exp, [tDt4], [tDt4])
                    act(dtv4, dtv4, AF.Ln, [tDt4], [tDt4], bias=1.0)
                    tt("dve", av4, dtv4, bc0(A_bc, 4), ALU.mult, [tDt4, tLay], [tDt4])
                    tt("dve", fr4, dtf[:, :, 16:24], bc0(bf_bc, 4), ALU.add, tDtf + [tLay], [tFr4])
                    act(fr4, fr4, AF.Exp, [tFr4], [tFr4], scale=-1.0)
                    act(fr4, fr4, AF.Ln, [tFr4], [tFr4], bias=1.0)
                    ts("dve", fr4, fr4, -1.0, None, ALU.mult, None, [tFr4], [tFr4])
                    for half in range(2):
                        yield from tm_group(C_Z + half * 512,
                                 lambda j, b, half=half: act(sz[:, j, half * 512:(half + 1) * 512], bankf(b), AF.Silu, [], [bT[b], tSz[j]]))

                    def conv_evac(g):
                        def f(cc, b):
                            ch = g * 4 + cc
                            s = ch % 2
                            cp("act", ubuf[s][:, 3:515], bankf(b), [], [bT[b], tU[s]])
                            cp("pool", ubuf[s][:, 0:3], hist[:, ch, :], [tHist[ch]], [tUh[s]])
                            ts("dve", cacc[s], ubuf[s][:, 3:515], cw_sb[:, ch, 3:4], cb_sb[:, ch:ch + 1], ALU.mult, ALU.add, [tU[s], tLay], [tCa[s]])
                            for kk in range(3):
                                stt("dve", cacc[s], ubuf[s][:, kk:kk + 512], cw_sb[:, ch, kk:kk + 1], cacc[s], ALU.mult, ALU.add, [tU[s], tUh[s], tLay], [tCa[s]])
                            cp("pool", hist[:, ch, :], ubuf[s][:, 512:515], [tU[s]], [tHist[ch]])
                            act(xbcT[:, ch, :], cacc[s], AF.Silu, [tCa[s]], [tXbc[ch]])
                        return f
                    for g in range(3):
                        yield from fm_group(C_XBC + g * 512, conv_evac(g))

                    flush()

                def gen_C(st):
                    _d = dbufs[st % 2]
                    rq, rk, rv, srg, sz, dtf = _d["rq"], _d["rk"], _d["rv"], _d["srg"], _d["sz"], _d["dtf"]
                    tRq, tRk, tRv, tRg, tSz, tDtf = _d["tRq"], _d["tRk"], _d["tRv"], _d["tRg"], _d["tSz"], _d["tDtf"]
                    for j in range(4):
                        t = st * 4 + j
                        ms = t % 2
                        cs_ = cst[t % 2]
                        P.dma("sp", cs_, cs_tab[t * 128:(t + 1) * 128, :], writes=[tCst[t % 2]])
                        cosb = bc0(cs_[:, 0:64], 4)
                        sinb = bc0(cs_[:, 64:128], 4)
                        for (src, tsrc, dst, tdst, eng) in ((rq, tRq, qrot, tQrot, "pool"), (rk, tRk, krot, tKrot, "dve")):
                            v = src[:, j, :].rearrange("p (h two d) -> p h two d", h=4, two=2)
                            o = dst.rearrange("p h (two d) -> p h two d", two=2)
                            R = [tsrc[j], tCst[t % 2]]
                            tt(eng, rt[0], v[:, :, 0, :], cosb, ALU.mult, R, [tRt[0]])
                            tt(eng, rt[1], v[:, :, 1, :], sinb, ALU.mult, R, [tRt[1]])
                            tt(eng, o[:, :, 0, :], rt[0], rt[1], ALU.subtract, [tRt[0], tRt[1]], [tdst])
                            tt(eng, rt[2], v[:, :, 0, :], sinb, ALU.mult, R, [tRt[2]])
                            tt(eng, rt[3], v[:, :, 1, :], cosb, ALU.mult, R, [tRt[3]])
                            tt(eng, o[:, :, 1, :], rt[2], rt[3], ALU.add, [tRt[2], tRt[3]], [tdst])
                        tt("dve", kd, krot, bc1(csb["dk"], 128), ALU.mult, [tKrot, tConst], [tKd])
                        yield
                        bq = B_W[0]
                        for h in range(4):
                            tr(bankb(bq)[:, h * 128:(h + 1) * 128], qrot[:, h, :], [tQrot], [bT[bq]])
                        for h in range(4):
                            tr(bankb(bq)[:, 512 + h * 128:512 + (h + 1) * 128], krot[:, h, :], [tKrot], [bT[bq]])
                        cp("act", qT, bankb(bq)[:, 0:512].rearrange("p (h c) -> p h c", h=4), [], [bT[bq], tQT])
                        tt("dve", qdT, bankb(bq)[:, 0:512].rearrange("p (h c) -> p h c", h=4), csb["dq"], ALU.mult, [tConst], [bT[bq], tQdT])
                        cp("act", kT, bankb(bq)[:, 512:1024].rearrange("p (h c) -> p h c", h=4), [], [bT[bq], tKT])
                        yield
                        bs_ = B_W[1]
                        for h in range(4):
                            mm(bankf(bs_)[:, h * 128:(h + 1) * 128], kT[:, h, :], qT[:, h, :], h == 0, h == 3, [tKT, tQT], [bT[bs_]])
                        tt("dve", smT, bankf(bs_).rearrange("p (h c) -> p h c", h=4), csb["dintraT"], ALU.mult, [tConst], [bT[bs_], tSmT])
                        yield
                        bo = B_W[2]
                        for h in range(4):
                            mm(bankf(bo)[:, h * 128:(h + 1) * 128], smT[:, h, :], rv[:, j, h * 128:(h + 1) * 128], h == 0, False, [tSmT, tRv[j]], [bT[bo]])
                            mm(bankf(bo)[:, h * 128:(h + 1) * 128], qdT[:, h, :], Sr_bf[:, h, :], False, h == 3, [tQdT, tSrb], [bT[bo]])
                        bn = B_W[3]
                        for h in range(4):
                            mm(bankf(bn)[:, h * 128:(h + 1) * 128], kd[:, h, :], rv[:, j, h * 128:(h + 1) * 128], h == 0, h == 3, [tKd, tRv[j]], [bT[bn]])
                        for h in range(4):
                            stt("dve", Sr[:, h, :], Sr[:, h, :], dchunk[h], bankf(bn)[:, h * 128:(h + 1) * 128], ALU.mult, ALU.add, [], [bT[bn], tSr])
                        cp("act", Sr_bf, Sr, [tSr], [tSrb])
                        yield
                        for h in range(4):
                            act(junk[:, 0:128], bankf(bo)[:, h * 128:(h + 1) * 128], AF.Square, [], [bT[bo], tJunk, tRss], accum=rss[:, h:h + 1])
                        rsqrt_mean(rss[:, 4:8], rss[:, 0:4], 128, [tRss], [tRss])
                        tt("dve", otmp, bankf(bo).rearrange("p (h c) -> p h c", h=4), bc1(rss[:, 4:8], 128), ALU.mult, [tRss], [bT[bo], tOt])
                        tt("pool", mixst[ms][:, 0:512], otmp.rearrange("p h c -> p (h c)"), srg[:, j, :], ALU.mult, [tOt, tRg[j]], [tMixR[ms]])
                        yield

                        tsl = slice(j * 128, (j + 1) * 128)
                        dtv = _d["dtv4"][:, j, :]
                        av = _d["av4"][:, j, :]
                        tDt = _d["tDt"]
                        bsm = B_SM
                        mm(bankf(bsm)[:, 0:16], tri, av, True, False, [tConst, tDt], [bT[bsm]])
                        mm(bankf(bsm)[:, 16:32], ones_f, av, False, True, [tConst, tDt], [bT[bsm]])
                        act(eacum, bankf(bsm)[:, 0:16], AF.Exp, [], [bT[bsm], tAc])
                        act(elast, bankf(bsm)[:, 16:32], AF.Exp, [], [bT[bsm], tAc])
                        tt("dve", rhs1[:, 0:8, :], bc0(tri, 8), bc1(av[:, 0:8], 128), ALU.mult, [tConst, tDt], [tRhs1[0]])
                        tt("pool", rhs1[:, 8:16, :], bc0(tri, 8), bc1(av[:, 8:16], 128), ALU.mult, [tConst, tDt], [tRhs1[1]])
                        yield
                        for q4 in range(4):
                            b = B_W[q4]
                            mm(bankf(b), gst, rhs1[:, q4 * 4:(q4 + 1) * 4, :].rearrange("p h c -> p (h c)"), True, True, [tConst, tRhs1[q4 // 2]], [bT[b]])
                            act(LT[:, q4 * 4:(q4 + 1) * 4, :].rearrange("p h c -> p (h c)"), bankf(b), AF.Exp, [], [bT[b], tLT[q4 // 2]])
                            cp("dve", decj[:, q4 * 4:(q4 + 1) * 4], bankf(b).rearrange("p (h c) -> p h c", h=4)[:, :, 127], [], [bT[b], tDecj])
                        act(decj, decj, AF.Exp, [tDecj], [tDecj])
                        yield
                        for g in range(2):
                            mm(bankf(bsm)[:, 128 + g * 128:128 + (g + 1) * 128], xbcT[:, 8 + g, tsl], xbcT[:, 10 + g, tsl], False, g == 1,
                               [tXbc[8 + g], tXbc[10 + g]], [bT[bsm]])
                        tt("dve", cbm, bankf(bsm)[:, 128:384].rearrange("p (g c) -> p g c", g=2), bc0(csb["mask01_f"], 2), ALU.mult, [tConst], [bT[bsm], tCbm])
                        for g in range(2):
                            tt("dve" if g == 0 else "pool", MT[:, g * 8:(g + 1) * 8, :], LT[:, g * 8:(g + 1) * 8, :], bc0(cbm[:, g, :], 8), ALU.mult, [tLT[g], tCbm], [tMT[g]])
                        for c8 in range(8):
                            tr(bankb(B_T)[:, c8 * 128:(c8 + 1) * 128], xbcT[:, c8, tsl], [tXbc[c8]], [bT[B_T]])
                        cp("act", xs_tm, bankb(B_T), [], [bT[B_T], tXs])
                        tt("dve", xdt.rearrange("p (h d) -> p h d", h=16), bankb(B_T).rearrange("p (h d) -> p h d", h=16), bc1(dtv, 64), ALU.mult, [tDt], [bT[B_T], tXdt])
                        for g in range(2):
                            tr(bankb(B_T)[:, g * 128:(g + 1) * 128], xbcT[:, 8 + g, tsl], [tXbc[8 + g]], [bT[B_T]])
                        cp("act", B_tm, bankb(B_T)[:, 0:256].rearrange("p (g c) -> p g c", g=2), [], [bT[B_T], tBtm])
                        tt("pool", xdtd.rearrange("p (h d) -> p h d", h=16), xdt.rearrange("p (h d) -> p h d", h=16), bc1(decj, 64), ALU.mult, [tXdt, tDecj], [tXdtd])
                        yield
                        for h in range(16):
                            b = B_W[h // 8]
                            hh = h % 8
                            mm(bankf(b)[:, hh * 64:(hh + 1) * 64], MT[:, h, :], xdt[:, h * 64:(h + 1) * 64], hh == 0, hh == 7, [tMT[h // 8], tXdt], [bT[b]])
                        for g in range(2):
                            b = B_W[2 + g]
                            mm(bankf(b), xbcT[:, 10 + g, tsl], Ss_bf[:, g, :], True, True, [tXbc[10 + g], tSsb], [bT[b]])
                        for g in range(2):
                            tt("pool", Ss[:, g, :].rearrange("p (h d) -> p h d", h=8), Ss[:, g, :].rearrange("p (h d) -> p h d", h=8),
                               bc1(elast[:, g * 8:(g + 1) * 8], 64), ALU.mult, [tAc], [tSs])
                            mm(bankf(bsm), B_tm[:, g, :], xdtd[:, g * 512:(g + 1) * 512], True, True, [tBtm, tXdtd], [bT[bsm]])
                            tt("dve", Ss[:, g, :], Ss[:, g, :], bankf(bsm), ALU.add, [], [bT[bsm], tSs])
                        cp("act", Ss_bf, Ss, [tSs], [tSsb])
                        for g in range(2):
                            b = B_W[2 + g]
                            tt("dve", y1[:, g * 512:(g + 1) * 512].rearrange("p (h d) -> p h d", h=8), bankf(b).rearrange("p (h d) -> p h d", h=8),
                               bc1(eacum[:, g * 8:(g + 1) * 8], 64), ALU.mult, [tAc], [bT[b], tY1])
                        for g in range(2):
                            b = B_W[g]
                            tt("dve", y1[:, g * 512:(g + 1) * 512], y1[:, g * 512:(g + 1) * 512], bankf(b), ALU.add, [], [bT[b], tY1])
                        tt("pool", y2.rearrange("p (h d) -> p h d", h=16), xs_tm.rearrange("p (h d) -> p h d", h=16), bc1(dskip_bc, 64), ALU.mult, [tXs, tLay], [tY2])
                        tt("pool", y2, y2, y1, ALU.add, [tY1], [tY2])
                        tt("pool", y2, y2, sz[:, j, :], ALU.mult, [tSz[j]], [tY2])
                        yield
                        act(junk, y2, AF.Square, [tY2], [tJunk, tSss], accum=sss[:, 0:1])
                        rsqrt_mean(sss[:, 1:2], sss[:, 0:1], D, [tSss], [tSss])
                        stt("dve", mixst[ms][:, 512:1536], y2, sss[:, 1:2], ssdg_bc, ALU.mult, ALU.mult, [tY2, tSss, tLay], [tMixS[ms]])
                        yield
                        P.dma("act", mixA[t * 128:(t + 1) * 128, :], mixst[ms], reads=[tMixR[ms], tMixS[ms]])

                        fs = st % 2
                        fr = _d["fr4"][:, j, :]
                        tFr = _d["tFr"]
                        mm(bankf(bsm)[:, 0:8], tri, fr, True, False, [tConst, tFr], [bT[bsm]])
                        mm(bankf(bsm)[:, 8:16], ones_f, fr, False, False, [tConst, tFr], [bT[bsm]])
                        mm(bankf(bsm)[0:8, 16:144], fr, tri, False, True, [tConst, tFr], [bT[bsm]])
                        if j == 0:
                            cp("dve", carry_hist[:, st, :], carry_bc, [tCarry], [tCH])
                            memset("pool", relc, 0.0, [tRelc])
                        tt("dve", F_all[:, t, :], bankf(bsm)[:, 0:8], carry_bc, ALU.add, [tCarry], [bT[bsm], tF])
                        tt("dve", carry_bc, bankf(bsm)[:, 8:16], carry_bc, ALU.add, [], [bT[bsm], tCarry])
                        ts("dve", fpt, bankf(bsm)[0:8, 16:144], relc[:, 0:1], None, ALU.add, None, [tRelc], [bT[bsm], tFpt])
                        cp("dve", relc, fpt[:, 127:128], [tFpt], [tRelc])
                        cp("dve", fqhi[fs][:, tsl], fpt, [tFpt], [tFq[fs]])
                        tt("dve", fpl, fpt, fqhi[fs][:, tsl], ALU.subtract, [tFpt, tFq[fs]], [tFpt])
                        cp("dve", fqlo[fs][:, tsl], fpl, [tFpt], [tFq[fs]])
                        yield
                    fs = st % 2
                    P.dma("act", FQs[0, :, st * 512:(st + 1) * 512], fqhi[fs], reads=[tFq[fs]])
                    P.dma("act", FQs[1, :, st * 512:(st + 1) * 512], fqlo[fs], reads=[tFq[fs]])
                def drive(gens):
                    alive = list(gens)
                    while alive:
                        for g_ in list(alive):
                            try:
                                next(g_)
                            except StopIteration:
                                alive.remove(g_)

                for st in range(NST + 1):
                    gens = []
                    if st < NST:
                        gens.append(gen_AB(st))
                    if st >= 1:
                        gens.append(gen_C(st - 1))
                    drive(gens)
                if debug and l == depth - 1:
                    P.dma("pool", dbgF, F_all, reads=[tF])
                A.reset(m1)
                P.barrier()
                if stop == "pass1":
                    raise _Stop()

                m2 = A.mark()
                wout = A.alloc([12, D], BF16)
                woutF = A.alloc([8, D], BF16, parts=64)
                tWo = T("wout")
                tWoF = T("woutF")
                P.dma("sp", wout, wb_out[l], writes=[tWo])
                P.dma("sp", woutF, wb_outF[l], writes=[tWoF])
                ktb = [A.alloc([L], BF16) for _ in range(2)]
                tKtb = TL(2, "ktb")
                vb = [A.alloc([NT, 72], BF16) for _ in range(2)]
                tVb = TL(2, "vb")
                qtb = [A.alloc([8, 512], BF16) for _ in range(2)]
                tQtb = TL(2, "qtb")
                tQtbH = TL(2, "qtbH")
                tQtbL = TL(2, "qtbL")
                gtb = A.alloc([8, 512], BF16, parts=64)
                tGtb = T("gtb")
                mxa = [A.alloc([1536], BF16) for _ in range(2)]
                tMxa = TL(2, "mxa")
                x2 = [A.alloc([D], F32) for _ in range(2)]
                tX2 = TL(2, "x2")
                pT = [A.alloc([512], BF16) for _ in range(3)]
                tPT = TL(3, "pT")
                mixFT = A.alloc([8, 512], BF16, parts=64)
                tMixFT = T("mixFT")
                mT = A.alloc([12, 128], BF16)
                tMT2 = TL(2, "mT")
                biasq = A.alloc([NT, 8], F32)
                tBq = T("biasq")
                accS = [A.alloc([512], F32) for _ in range(2)]
                tAccS = TL(2, "accS")
                rden = A.alloc([512], F32, parts=64)
                tRden = T("rden")
                o1 = A.alloc([512], F32, parts=64)
                tO1 = T("o1")
                selden = A.alloc([64], F32)
                tSd = T("selden")
                xn = [A.alloc([D], F32) for _ in range(2)]
                tXn = TL(2, "xn")
                fss = A.alloc([2], F32)
                tFss = T("fss")
                fjunk = A.alloc([D], BF16)
                tFj = T("fjunk")
                fgb = A.alloc([D], F32)
                tFgb = T("fgb")
                if l == 0:
                    print("pass2 arena used", A.off)
                if last:
                    P.dma("sp", fgb, final_g.partition_broadcast(128), writes=[tFgb])
                for s in range(2):
                    memset("pool", ktb[s][64:128, :], 1.0, [tKtb[s]])
                memset("pool", selden, 0.0, [tSd])
                memset("pool", selden[64:65, :], 1.0, [tSd])

                BS = [0, 1, 2]
                BA = [3, 4]
                BX = 5
                BO = [6, 7]
                si = 0
                ai = 0
                pi_ = 0
                hi_ = 0
                for qt in range(NST):
                    nkt = 4 * qt + 4
                    qs = qt % 2
                    qcols = slice(qt * 512, (qt + 1) * 512)
                    P.dma("sp", qtb[qs][0:64, :, :], QTs.rearrange("h d l -> d h l")[:, :, qcols], writes=[tQtb[qs]])
                    P.dma("sp", qtb[qs][64:65, :, :], FQs[0:1, :, qcols], writes=[tQtbH[qs]])
                    P.dma("sp", qtb[qs][65:66, :, :], FQs[1:2, :, qcols], writes=[tQtbL[qs]])
                    P.dma("sp", gtb, GTs.rearrange("h d l -> d h l")[:, :, qcols], writes=[tGtb])
                    tt("dve", biasq[:, 0:nkt, :], bc0(carry_hist[:, qt, :], nkt), F_all[:, 0:nkt, :], ALU.subtract, [tCH, tF], [tBq])
                    LOOK = 2
                    its = []
                    for h in range(8):
                        ks = hi_ % 2
                        hi_ += 1
                        ba = BA[ai % 2]
                        as_ = ai % 2
                        ai += 1
                        for kt in range(nkt):
                            its.append((h, kt, ks, ba, as_))
                    state = {}

                    def emit_qk(idx):
                        h, kt, ks, ba, as_ = its[idx]
                        if kt == 0:
                            P.dma("sp", ktb[ks][0:64, 0:nkt * 128], KTs[h, :, 0:nkt * 128], writes=[tKtb[ks]])
                            P.dma("sp", vb[ks][:, 0:nkt, :], Vs[h, :, 0:nkt, :], writes=[tVb[ks]])
                        di = kt - 4 * qt
                        q0 = 128 * max(di, 0)
                        bs_ = BS[idx % 3]
                        ps = (pi_ + idx) % 3
                        mm(bankf(bs_)[:, q0:512], ktb[ks][0:66, kt * 128:(kt + 1) * 128], qtb[qs][0:66, h, q0:512], True, True,
                           [tKtb[ks], tQtb[qs], tQtbH[qs], tQtbL[qs]], [bT[bs_]])
                        act(pT[ps][:, q0:512], bankf(bs_)[:, q0:512], AF.Exp, [tBq], [bT[bs_], tPT[ps]], bias=biasq[:, kt, h:h + 1])
                        if di >= 0:
                            tt("pool", pT[ps][:, q0:q0 + 128], pT[ps][:, q0:q0 + 128], csb["mask01_bf"], ALU.mult, [tConst], [tPT[ps]])

                    def emit_pv(idx):
                        h, kt, ks, ba, as_ = its[idx]
                        di = kt - 4 * qt
                        q0 = 128 * max(di, 0)
                        ps = (pi_ + idx) % 3
                        mm(bankf(ba)[0:65, q0:512], vb[ks][:, kt, 0:65], pT[ps][:, q0:512], kt == 0, kt == nkt - 1, [tPT[ps], tVb[ks]], [bT[ba]])
                        if kt == nkt - 1:
                            cp("act", accS[as_][0:65, :], bankf(ba)[0:65, :], [], [bT[ba], tAccS[as_]])
                            mm(bankf(BX)[0:64, :], selden[0:65, :], accS[as_][0:65, :], True, True, [tSd, tAccS[as_]], [bT[BX]])
                            P.op("dve", lambda e, o=rden, i=bankf(BX)[0:64, :]: e.reciprocal(o, i), reads=[], writes=[bT[BX], tRden])
                            tt("dve", o1, accS[as_][0:64, :], rden, ALU.mult, [tAccS[as_], tRden], [tO1])
                            tt("pool", mixFT[:, h, :], o1, gtb[:, h, :], ALU.mult, [tO1, tGtb], [tMixFT])

                    nit = len(its)
                    for idx in range(min(LOOK, nit)):
                        emit_qk(idx)
                    for idx in range(nit):
                        if idx + LOOK < nit:
                            emit_qk(idx + LOOK)
                        emit_pv(idx)
                    pi_ += nit
                    for j in range(4):
                        t = qt * 4 + j
                        s = t % 2
                        P.dma("sp", mxa[s], mixA[t * 128:(t + 1) * 128, :], writes=[tMxa[s]])
                        P.dma("sp", x2[s], x_src[t * 128:(t + 1) * 128, :], writes=[tX2[s]])
                        for half in range(2):
                            nk = 8 if half == 0 else 4
                            for kk in range(nk):
                                k = half * 8 + kk
                                tr(bankb(BX)[:, kk * 128:(kk + 1) * 128], mxa[s][:, k * 128:(k + 1) * 128], [tMxa[s]], [bT[BX]])
                            cp("act" if half == 0 else "dve", mT[:, half * 8:half * 8 + nk, :],
                               bankb(BX)[:, 0:nk * 128].rearrange("p (k c) -> p k c", k=nk), [], [bT[BX], tMT2[half]])
                        for cg in range(2):
                            bo = BO[cg]
                            csl = slice(cg * 512, (cg + 1) * 512)
                            for k in range(12):
                                mm(bankf(bo), mT[:, k, :], wout[:, k, csl], k == 0, False, [tMT2[k // 8], tWo], [bT[bo]])
                            for h in range(8):
                                mm(bankf(bo), mixFT[:, h, j * 128:(j + 1) * 128], woutF[:, h, csl], False, h == 7, [tMixFT, tWoF], [bT[bo]])
                            tt("dve", xn[s][:, csl], bankf(bo), gate_bc[:, csl], ALU.mult, [tMod], [bT[bo], tXn[s]])
                            tt("pool", xn[s][:, csl], xn[s][:, csl], x2[s][:, csl], ALU.add, [tX2[s]], [tXn[s]])
                        if not last:
                            P.dma("pool", xs_buf[t * 128:(t + 1) * 128, :], xn[s], reads=[tXn[s]], writes=[])
                        else:
                            act(fjunk, xn[s], AF.Square, [tXn[s]], [tFj, tFss], accum=fss[:, 0:1])
                            rsqrt_mean(fss[:, 1:2], fss[:, 0:1], D, [tFss], [tFss])
                            stt("dve", xn[s], xn[s], fss[:, 1:2], fgb, ALU.mult, ALU.mult, [tFss, tFgb], [tXn[s]])
                            out_toks.append(P.dma("pool", y_out[t * 128:(t + 1) * 128, :], xn[s], reads=[tXn[s]]))
                A.reset(m2)
                P.barrier()

        try:
            _phases()
        except _Stop:
            P.barrier()
        waits = P._collect("pool", (), (), extra=out_toks)
        P.streams["pool"].append((waits, None, None))
        print("ops recorded:", P.nops, "sems:", len(P.semkeys))
        P.emit()
    return nc


def make_in_maps(inputs, L, depth, ncores):
    consts = host_consts()
    maps = []
    f32 = np.float32
    x = np.asarray(inputs["x"], f32)
    c = np.asarray(inputs["c"], f32)
    pos = np.asarray(inputs["positions"], np.int32)
    conv_w = np.asarray(inputs["conv_w"], f32)
    conv_b = np.asarray(inputs["conv_b"], f32)
    cw_t = np.ascontiguousarray(conv_w[:depth].reshape(depth, 4, 12, 128).transpose(0, 3, 2, 1))
    cb_t = np.ascontiguousarray(conv_b[:depth].reshape(depth, 12, 128).transpose(0, 2, 1))
    B = x.shape[0]
    shared = {
        "norm_g": np.ascontiguousarray(np.asarray(inputs["norm_g"], f32)[:depth]),
        "w_ada": np.ascontiguousarray(np.asarray(inputs["w_ada"], f32)[:depth]),
        "b_ada": np.ascontiguousarray(np.asarray(inputs["b_ada"], f32)[:depth]),
        "w_in": np.ascontiguousarray(np.asarray(inputs["w_in"], f32)[:depth]),
        "cw_t": cw_t, "cb_t": cb_t,
        "dt_bias": np.ascontiguousarray(np.asarray(inputs["dt_bias"], f32)[:depth]),
        "a_log": np.ascontiguousarray(np.asarray(inputs["a_log"], f32)[:depth]),
        "d_skip": np.ascontiguousarray(np.asarray(inputs["d_skip"], f32)[:depth]),
        "ssd_norm_g": np.ascontiguousarray(np.asarray(inputs["ssd_norm_g"], f32)[:depth]),
        "b_forget": np.ascontiguousarray(np.asarray(inputs["b_forget"], f32)[:depth]),
        "w_out": np.ascontiguousarray(np.asarray(inputs["w_out"], f32)[:depth]),
        "final_g": np.ascontiguousarray(np.asarray(inputs["final_g"], f32).reshape(1, D)),
    }
    for n, s, dt in CONST_SPECS:
        shared[n] = consts[n]
    for core in range(ncores):
        b = core % B
        m = dict(shared)
        m["x"] = np.ascontiguousarray(x[b, :L])
        m["c_t"] = np.ascontiguousarray(c[b].reshape(8, 128).T)
        m["pos_t"] = np.ascontiguousarray(pos[b, :L].reshape(L // 128, 128).T)
        maps.append(m)
    return maps


def kernel(x, c, positions, norm_g, w_ada, b_ada, w_in, conv_w, conv_b, dt_bias, a_log, d_skip,
           ssd_norm_g, b_forget, w_out, final_g):
    inputs = dict(x=x, c=c, positions=positions, norm_g=norm_g, w_ada=w_ada, b_ada=b_ada, w_in=w_in,
                  conv_w=conv_w, conv_b=conv_b, dt_bias=dt_bias, a_log=a_log, d_skip=d_skip,
                  ssd_norm_g=ssd_norm_g, b_forget=b_forget, w_out=w_out, final_g=final_g)
    B, L, _ = np.asarray(x).shape
    nc = build_nc(L, DEPTH)
    maps = make_in_maps(inputs, L, DEPTH, 8)
    res = run_bass_kernel_spmd(nc, maps, core_ids=list(range(8)))
    out = np.stack([np.asarray(res.results[b]["y"], np.float32) for b in range(B)], axis=0)
    return out
```
